# Optimizing a Trainium2 kernel written in Bass

```python
import jax, jax.numpy as jnp
from jax import lax
import numpy as np


D_MODEL = 1024
BATCH = 8
SEQ = 8192
DEPTH = 4

CTX_LEN = 256
GRID_W = 64
N_MIXERS = 3
N_SGU_LAYERS = (DEPTH + 2) // 3
N_RWKV_LAYERS = (DEPTH + 1) // 3
N_MLA_LAYERS = DEPTH // 3
RMS_EPS = 1e-6
SGU_WIDTH = 2 * D_MODEL
SGU_CHUNK = 128
SGU_GROUPS = 8
SGU_GROUP_W = SGU_WIDTH // SGU_GROUPS
RWKV_HEAD = 64
RWKV_WIDTH = D_MODEL
RWKV_HEADS = RWKV_WIDTH // RWKV_HEAD
RWKV_DECAY_LORA = 64
RWKV_ICLR_LORA = 64
RWKV_LN_EPS = 64e-5
RWKV_N_LERP = 6
MLA_HEADS = 16
MLA_NOPE = 128
MLA_ROPE = 64
MLA_QK_DIM = MLA_NOPE + MLA_ROPE
MLA_VDIM = 128
MLA_Q_RANK = 768
MLA_KV_RANK = 256
MLA_WIDTH = MLA_HEADS * MLA_VDIM
MLA_QBLOCK = 128
MLA_SCALE = MLA_QK_DIM ** -0.5
ROPE_BASE = 10000.0
ROPE_FREQS_PER_AXIS = MLA_ROPE // 4

kernel_name = 'hybrid_sgu_rwkv7_mla_dit_trunk'


def rms_norm(t, gain=None):
    tf = t.astype(jnp.float32)
    y = tf * lax.rsqrt(jnp.mean(tf * tf, axis=-1, keepdims=True) + RMS_EPS)
    if gain is not None:
        y = y * gain.astype(jnp.float32)
    return y.astype(t.dtype)


def ada_modulation(cond, w, b):
    m = jax.nn.silu(cond) @ w + b
    shift, scale, gate = jnp.split(m, 3, axis=-1)
    return shift[..., None, :], scale[..., None, :], gate[..., None, :]


def sgu_mixer(h, w_in, gain, w_s, b_s, w_out):
    bsz, length, _ = h.shape
    u, v, z = jnp.split(h @ w_in, 3, axis=-1)
    u = jax.nn.gelu(u)
    v = rms_norm(jax.nn.gelu(v), gain)
    vc = v.reshape(bsz, length // SGU_CHUNK, SGU_CHUNK, SGU_GROUPS, SGU_GROUP_W)
    mixed = jnp.einsum('gpq,bcqgd->bcpgd', w_s, vc) + b_s.T[:, :, None]
    s = u * mixed.reshape(bsz, length, SGU_WIDTH)
    return (s * jax.nn.silu(z)) @ w_out


def split_heads(t):
    return t.reshape(t.shape[:-1] + (RWKV_HEADS, RWKV_HEAD))


def center_shift(h):
    prev = jnp.pad(h[:, :-1], ((0, 0), (1, 0), (0, 0)))
    nxt = jnp.pad(h[:, 1:], ((0, 0), (0, 1), (0, 0)))
    return 0.5 * (prev + nxt) - h


def rwkv7_features(h, mu, w_in, w_lora1, w_lora2, w0, a_lora1, a_lora2, a0, k_k, k_a):
    xx = center_shift(h)
    xs = h[:, :, None, :] + xx[:, :, None, :] * mu
    rkvz = jnp.einsum('blcd,cde->blce', xs[:, :, :4], w_in)
    r, k, v, z = rkvz[:, :, 0], rkvz[:, :, 1], rkvz[:, :, 2], rkvz[:, :, 3]
    w_lora = jnp.einsum('nblr,nre->nble', jnp.tanh(jnp.einsum('bld,ndr->nblr', xs[:, :, 4], w_lora1)), w_lora2)
    log_w = -jax.nn.softplus(-(w0[:, None, None, :] + w_lora)) - 0.5
    decay = jnp.exp(-jnp.exp(log_w.astype(jnp.float32)))
    a = jax.nn.sigmoid(a0[:, None, None, :] + jnp.einsum('nblr,nre->nble', jnp.einsum('bld,ndr->nblr', xs[:, :, 5], a_lora1), a_lora2))
    kk = split_heads(k * k_k).astype(jnp.float32)
    kk = (kk * lax.rsqrt(jnp.sum(kk * kk, axis=-1, keepdims=True) + 1e-12)).reshape(k.shape)
    k_dir = k * (1 + (a - 1) * k_a)
    return r, decay, k_dir, v, kk, a, z


def to_scan_order(t_dir):
    return jnp.stack([t_dir[0], jnp.flip(t_dir[1], axis=1)])


def rwkv7_scan(state0, r, decay, k_dir, v, kk, a, emit):
    bsz, length = v.shape[:2]
    both = lambda t: jnp.broadcast_to(t, (2,) + t.shape)
    inputs = (decay, k_dir, both(v), both(kk), a) + ((both(r),) if emit else ())
    seqs = tuple(jnp.moveaxis(split_heads(to_scan_order(t).astype(jnp.float32)), 2, 0) for t in inputs)

    def step(S, inp):
        w_t, k_t, v_t, kk_t, a_t = inp[:5]
        s_kk = jnp.einsum('nbhvk,nbhk->nbhv', S, kk_t)
        S = S * w_t[..., None, :] - s_kk[..., :, None] * (kk_t * a_t)[..., None, :] + v_t[..., :, None] * k_t[..., None, :]
        y_t = jnp.einsum('nbhvk,nbhk->nbhv', S, inp[5]) if emit else None
        return S, y_t

    s_final, ys = lax.scan(step, state0, seqs)
    if not emit:
        return s_final, None
    ys = jnp.moveaxis(ys, 0, 2)
    y = ys[0] + jnp.flip(ys[1], axis=1)
    return s_final, y.reshape(bsz, length, RWKV_WIDTH)


def rwkv7_output(y, r, k_dir, v, z, r_k, ln_gain, ln_bias, w_out):
    yh = split_heads(y)
    mean = jnp.mean(yh, axis=-1, keepdims=True)
    var = jnp.mean(jnp.square(yh - mean), axis=-1, keepdims=True)
    yn = ((yh - mean) * lax.rsqrt(var + RWKV_LN_EPS)).reshape(y.shape) * ln_gain + ln_bias
    bonus = jnp.sum(split_heads(r)[None] * split_heads(k_dir) * r_k, axis=-1, keepdims=True) * split_heads(v)[None]
    out = (yn + jnp.sum(bonus, axis=0).reshape(y.shape)).astype(z.dtype)
    return (out * jax.nn.silu(z)) @ w_out


def axial_rope_tables(length):
    rows = length // GRID_W
    row = jnp.repeat(jnp.arange(rows, dtype=jnp.float32), GRID_W)
    col = jnp.tile(jnp.arange(GRID_W, dtype=jnp.float32), rows)
    inv_freq = ROPE_BASE ** (-jnp.arange(ROPE_FREQS_PER_AXIS, dtype=jnp.float32) / ROPE_FREQS_PER_AXIS)
    ang = jnp.concatenate([row[:, None] * inv_freq, col[:, None] * inv_freq], axis=-1)
    return jnp.cos(ang), jnp.sin(ang)


def apply_rope(t, cos, sin):
    t_nope, t_rope = jnp.split(t, [MLA_NOPE], axis=-1)
    x1, x2 = jnp.split(t_rope, 2, axis=-1)
    cs, sn = cos[:, None, :], sin[:, None, :]
    out = jnp.concatenate([t_nope, x1 * cs - x2 * sn, x1 * sn + x2 * cs], axis=-1)
    return out.astype(t.dtype)


def mla_project(h, w_in, q_norm, kv_norm, w_uq, w_ukv, qk_gain_q, qk_gain_k, need_q):
    bsz, length, _ = h.shape
    if need_q:
        c_q, c_kv, k_rope, z = jnp.split(h @ w_in, [MLA_Q_RANK, MLA_Q_RANK + MLA_KV_RANK, MLA_Q_RANK + MLA_KV_RANK + MLA_ROPE], axis=-1)
        q = (rms_norm(c_q, q_norm) @ w_uq).reshape(bsz, length, MLA_HEADS, MLA_QK_DIM)
        q = rms_norm(q, qk_gain_q)
    else:
        c_kv, k_rope = jnp.split(h @ w_in[:, MLA_Q_RANK:MLA_Q_RANK + MLA_KV_RANK + MLA_ROPE], [MLA_KV_RANK], axis=-1)
        q, z = None, None
    kv = (rms_norm(c_kv, kv_norm) @ w_ukv).reshape(bsz, length, MLA_HEADS, MLA_NOPE + MLA_VDIM)
    k_nope, v = jnp.split(kv, [MLA_NOPE], axis=-1)
    k = jnp.concatenate([k_nope, jnp.broadcast_to(k_rope[:, :, None, :], (bsz, length, MLA_HEADS, MLA_ROPE))], axis=-1)
    k = rms_norm(k, qk_gain_k)
    return q, k, v, z


def mla_latent_attention(q, k, v, k_ctx, v_ctx, cos, sin):
    bsz, length = q.shape[:2]
    n_ctx = k_ctx.shape[1]
    q_rot, k_rot = apply_rope(q, cos, sin), apply_rope(k, cos, sin)
    nblk = length // MLA_QBLOCK
    blocks = lambda t: jnp.moveaxis(t.reshape(bsz, nblk, MLA_QBLOCK, MLA_HEADS, MLA_QK_DIM), 1, 0)

    def one_block(qs):
        qr, qp = qs
        s = jnp.concatenate([jnp.einsum('bqhd,bkhd->bhqk', qp, k_ctx), jnp.einsum('bqhd,bkhd->bhqk', qr, k_rot)], axis=-1)
        p = jax.nn.softmax(s.astype(jnp.float32) * MLA_SCALE, axis=-1).astype(v.dtype)
        return jnp.einsum('bhqk,bkhd->bqhd', p[..., :n_ctx], v_ctx) + jnp.einsum('bhqk,bkhd->bqhd', p[..., n_ctx:], v)

    o = lax.map(one_block, (blocks(q_rot), blocks(q)))
    return jnp.moveaxis(o, 0, 1).reshape(bsz, length, MLA_WIDTH)


def mla_context_attention(q, k, v):
    bsz, n_ctx = q.shape[:2]
    p = jax.nn.softmax(jnp.einsum('bqhd,bkhd->bhqk', q, k).astype(jnp.float32) * MLA_SCALE, axis=-1).astype(v.dtype)
    return jnp.einsum('bhqk,bkhd->bqhd', p, v).reshape(bsz, n_ctx, MLA_WIDTH)


def setup_inputs(seed: int = 0) -> dict:
    key = jax.random.key(seed)
    it = iter(jax.random.split(key, 48))
    nrm = lambda shape, scale: scale * jax.random.normal(next(it), shape, jnp.float32)
    D = D_MODEL
    nA, nB, nC = N_SGU_LAYERS, N_RWKV_LAYERS, N_MLA_LAYERS
    return {
        'x': nrm((BATCH, SEQ, D), 1.0),
        'c': nrm((BATCH, D), 1.0),
        'ctx': nrm((BATCH, CTX_LEN, D), 1.0),
        'c_ctx': nrm((D,), 1.0),
        'ada_w': nrm((DEPTH, D, 3 * D), 0.5 * D ** -0.5),
        'ada_b': nrm((DEPTH, 3 * D), 0.02),
        'sgu_w_in': nrm((nA, D, 3 * SGU_WIDTH), D ** -0.5),
        'sgu_gain': 1.0 + nrm((nA, SGU_WIDTH), 0.02),
        'sgu_w_s': nrm((nA, SGU_GROUPS, SGU_CHUNK, SGU_CHUNK), SGU_CHUNK ** -0.5),
        'sgu_b_s': 1.0 + nrm((nA, SGU_GROUPS, SGU_CHUNK), 0.02),
        'sgu_w_out': nrm((nA, SGU_WIDTH, D), SGU_WIDTH ** -0.5),
        'rwkv_mu': jax.random.uniform(next(it), (nB, RWKV_N_LERP, D), jnp.float32),
        'rwkv_w_in': nrm((nB, 4, D, RWKV_WIDTH), D ** -0.5),
        'rwkv_w_lora1': nrm((nB, 2, D, RWKV_DECAY_LORA), D ** -0.5),
        'rwkv_w_lora2': nrm((nB, 2, RWKV_DECAY_LORA, RWKV_WIDTH), 0.5 * RWKV_DECAY_LORA ** -0.5),
        'rwkv_w0': -1.0 + nrm((nB, 2, RWKV_WIDTH), 0.5),
        'rwkv_a_lora1': nrm((nB, 2, D, RWKV_ICLR_LORA), D ** -0.5),
        'rwkv_a_lora2': nrm((nB, 2, RWKV_ICLR_LORA, RWKV_WIDTH), 0.5 * RWKV_ICLR_LORA ** -0.5),
        'rwkv_a0': nrm((nB, 2, RWKV_WIDTH), 0.1),
        'rwkv_k_k': 0.85 + nrm((nB, RWKV_WIDTH), 0.02),
        'rwkv_k_a': 1.0 + nrm((nB, RWKV_WIDTH), 0.02),
        'rwkv_r_k': nrm((nB, RWKV_HEADS, RWKV_HEAD), 0.1),
        'rwkv_ln_gain': 1.0 + nrm((nB, RWKV_WIDTH), 0.02),
        'rwkv_ln_bias': nrm((nB, RWKV_WIDTH), 0.02),
        'rwkv_w_out': nrm((nB, RWKV_WIDTH, D), RWKV_WIDTH ** -0.5),
        'mla_w_in': nrm((nC, D, MLA_Q_RANK + MLA_KV_RANK + MLA_ROPE + MLA_WIDTH), D ** -0.5),
        'mla_q_norm': 1.0 + nrm((nC, MLA_Q_RANK), 0.02),
        'mla_kv_norm': 1.0 + nrm((nC, MLA_KV_RANK), 0.02),
        'mla_w_uq': nrm((nC, MLA_Q_RANK, MLA_HEADS * MLA_QK_DIM), MLA_Q_RANK ** -0.5),
        'mla_w_ukv': nrm((nC, MLA_KV_RANK, MLA_HEADS * (MLA_NOPE + MLA_VDIM)), MLA_KV_RANK ** -0.5),
        'mla_qk_gain_q': 1.0 + nrm((nC, MLA_QK_DIM), 0.02),
        'mla_qk_gain_k': 1.0 + nrm((nC, MLA_QK_DIM), 0.02),
        'mla_w_out': nrm((nC, MLA_WIDTH, D), MLA_WIDTH ** -0.5),
    }


def reference(x, c, ctx, c_ctx, ada_w, ada_b, sgu_w_in, sgu_gain, sgu_w_s, sgu_b_s, sgu_w_out,
              rwkv_mu, rwkv_w_in, rwkv_w_lora1, rwkv_w_lora2, rwkv_w0, rwkv_a_lora1, rwkv_a_lora2, rwkv_a0,
              rwkv_k_k, rwkv_k_a, rwkv_r_k, rwkv_ln_gain, rwkv_ln_bias, rwkv_w_out,
              mla_w_in, mla_q_norm, mla_kv_norm, mla_w_uq, mla_w_ukv, mla_qk_gain_q, mla_qk_gain_k, mla_w_out):
    cos, sin = axial_rope_tables(x.shape[1])
    ctx_readers = [i for i in range(DEPTH) if i % N_MIXERS != 0]
    last_ctx_reader = ctx_readers[-1] if ctx_readers else -1
    for i in range(DEPTH):
        kind, j = i % N_MIXERS, i // N_MIXERS
        update_ctx = i < last_ctx_reader
        shift, scale, gate = ada_modulation(c, ada_w[i], ada_b[i])
        h = rms_norm(x) * (1 + scale) + shift
        if kind != 0 or update_ctx:
            c_shift, c_scale, c_gate = ada_modulation(c_ctx, ada_w[i], ada_b[i])
            hc = rms_norm(ctx) * (1 + c_scale) + c_shift
        if kind == 0:
            sgu_args = (sgu_w_in[j], sgu_gain[j], sgu_w_s[j], sgu_b_s[j], sgu_w_out[j])
            x = x + gate * sgu_mixer(h, *sgu_args)
            if update_ctx:
                ctx = ctx + c_gate * sgu_mixer(hc, *sgu_args)
        elif kind == 1:
            feat_args = (rwkv_mu[j], rwkv_w_in[j], rwkv_w_lora1[j], rwkv_w_lora2[j], rwkv_w0[j],
                         rwkv_a_lora1[j], rwkv_a_lora2[j], rwkv_a0[j], rwkv_k_k[j], rwkv_k_a[j])
            out_args = (rwkv_r_k[j], rwkv_ln_gain[j], rwkv_ln_bias[j], rwkv_w_out[j])
            r_c, w_c, kd_c, v_c, kk_c, a_c, z_c = rwkv7_features(hc, *feat_args)
            state0 = jnp.zeros((2, hc.shape[0], RWKV_HEADS, RWKV_HEAD, RWKV_HEAD), jnp.float32)
            s_ctx, y_c = rwkv7_scan(state0, r_c, w_c, kd_c, v_c, kk_c, a_c, emit=update_ctx)
            r_l, w_l, kd_l, v_l, kk_l, a_l, z_l = rwkv7_features(h, *feat_args)
            _, y_l = rwkv7_scan(s_ctx, r_l, w_l, kd_l, v_l, kk_l, a_l, emit=True)
            x = x + gate * rwkv7_output(y_l, r_l, kd_l, v_l, z_l, *out_args)
            if update_ctx:
                ctx = ctx + c_gate * rwkv7_output(y_c, r_c, kd_c, v_c, z_c, *out_args)
        else:
            mla_args = (mla_w_in[j], mla_q_norm[j], mla_kv_norm[j], mla_w_uq[j], mla_w_ukv[j],
                        mla_qk_gain_q[j], mla_qk_gain_k[j])
            q_c, k_c, v_c, z_c = mla_project(hc, *mla_args, need_q=update_ctx)
            q_l, k_l, v_l, z_l = mla_project(h, *mla_args, need_q=True)
            o_l = mla_latent_attention(q_l, k_l, v_l, k_c, v_c, cos, sin)
            x = x + gate * ((o_l * jax.nn.silu(z_l)) @ mla_w_out[j])
            if update_ctx:
                o_c = mla_context_attention(q_c, k_c, v_c)
                ctx = ctx + c_gate * ((o_c * jax.nn.silu(z_c)) @ mla_w_out[j])
    return x
```

```python
import bisect
from contextlib import ExitStack
import numpy as np
import concourse.bass as bass
import concourse.mybir as mybir
from concourse.bass_utils import run_bass_kernel_spmd

F32 = mybir.dt.float32
BF16 = mybir.dt.bfloat16
ALU = mybir.AluOpType
AF = mybir.ActivationFunctionType
AX = mybir.AxisListType

D = 1024
L_FULL = 8192
LC = 256
EPS = 1e-6


class Sem:
    def __init__(self, h):
        self.h = h
        self.count = 0


class Buf:
    def __init__(self, ap, name=""):
        self.ap = ap
        self.name = name
        self.w_evs = []
        self.r_evs = []
        self.dsem = None
        self.dsem_ph = None
        self.is_psum = False

    def __getitem__(self, k):
        return self.ap[k]


class Op:
    __slots__ = ("fn", "waits", "inc", "dma")

    def __init__(self, fn):
        self.fn = fn
        self.waits = []
        self.inc = False
        self.dma = None


class EngRec:
    def __init__(self, name, sem):
        self.name = name
        self.sem = sem
        self.ops = []
        self.mark_idx = []
        self.mark_tick = []
        self.seen = {}

    def ensure_tick(self, idx):
        i = bisect.bisect_left(self.mark_idx, idx)
        if i < len(self.mark_idx):
            return self.mark_tick[i]
        self.ops[idx].inc = True
        self.sem.count += 1
        self.mark_idx.append(idx)
        self.mark_tick.append(self.sem.count)
        return self.sem.count


class Prog:
    def __init__(self, nc, es, n_dma_sems=56):
        self.nc = nc
        self.es = es
        self.esem = {e: Sem(es.enter_context(nc.semaphore("s_" + e))) for e in ("pe", "act", "dve", "pool")}
        self.dsems = [Sem(es.enter_context(nc.semaphore("d%d" % i))) for i in range(n_dma_sems)]
        self.n_instr = 0
        self.nalloc = 0

    def sb(self, es, shape, dtype, name="t"):
        self.nalloc += 1
        nm = "%s%d" % (name, self.nalloc)
        t = es.enter_context(self.nc.sbuf_tensor(nm, list(shape), dtype))
        return Buf(t[tuple(slice(None) for _ in shape)], nm)

    def ps(self, es, shape, dtype, name="p"):
        self.nalloc += 1
        nm = "%s%d" % (name, self.nalloc)
        t = es.enter_context(self.nc.psum_tensor(nm, list(shape), dtype))
        bf = Buf(t[tuple(slice(None) for _ in shape)], nm)
        bf.is_psum = True
        return bf


class Phase:
    def __init__(self, prog, name):
        self.p = prog
        self.nc = prog.nc
        self.name = name
        self.es = ExitStack()
        self.eng = {e: EngRec(e, prog.esem.get(e)) for e in ("pe", "act", "dve", "pool", "sp")}
        self.free_dsems = list(prog.dsems)
        self.rr = 0

    def sb(self, shape, dtype, name="t"):
        return self.p.sb(self.es, shape, dtype, self.name + "_" + name)

    def ps(self, shape, dtype, name="p"):
        return self.p.ps(self.es, shape, dtype, self.name + "_" + name)

    def view(self, buf, key, name=""):
        return Buf(buf.ap[key], name or buf.name + "_v")

    def _wait_for(self, er, op, ev):
        if ev[-1] is not self:
            return
        if ev[0] == "c":
            fe, idx = ev[1], ev[2]
            if fe is er and er.name == "pe":
                return
            val = fe.ensure_tick(idx)
            sem = fe.sem
        else:
            sem, val = ev[1], ev[2]
        key = id(sem)
        if er.seen.get(key, 0) >= val:
            return
        er.seen[key] = val
        op.waits.append((sem.h, val))

    def _deps(self, er, op, ev, R, W):
        for b in R:
            for e in b.w_evs:
                self._wait_for(er, op, e)
            if b.is_psum:
                for e in b.r_evs:
                    if e[0] == "c" and e[1] is not er:
                        self._wait_for(er, op, e)
        for b in W:
            for e in b.w_evs:
                self._wait_for(er, op, e)
            for e in b.r_evs:
                self._wait_for(er, op, e)
        for b in R:
            if any(b is w for w in W):
                continue
            if ev[0] == "c":
                b.r_evs = [e for e in b.r_evs if not (e[0] == "c" and e[1] is ev[1]) and e[-1] is self]
            b.r_evs.append(ev)
        for b in W:
            b.w_evs = [ev]
            b.r_evs = []

    def op(self, eng, fn, R=(), W=()):
        er = self.eng[eng]
        o = Op(fn)
        er.ops.append(o)
        ev = ("c", er, len(er.ops) - 1, self)
        self._deps(er, o, ev, R, W)
        return o

    def dma(self, q, out, in_, R=(), W=(), sbuf=None, **kw):
        q = "sp"
        er = self.eng[q]
        b = sbuf or (W[0] if W else R[0])
        if b.dsem is None or b.dsem_ph is not self:
            b.dsem = self.free_dsems.pop()
            b.dsem_ph = self
        sem = b.dsem
        sem.count += 16
        o = Op(lambda e: e.dma_start(out=out, in_=in_, **kw))
        o.dma = sem.h
        er.ops.append(o)
        ev = ("d", sem, sem.count, self)
        self._deps(er, o, ev, R, W)
        return o

    def dmaq(self):
        self.rr += 1
        return ("sp", "pool")[self.rr % 2]

    def act(self, out, in_, func, R, W, **kw):
        return self.op("act", lambda e: e.activation(out=out, in_=in_, func=func, **kw), R, W)

    def mm(self, out, lhsT, rhs, start, stop, R, W):
        return self.op("pe", lambda e: e.matmul(out, lhsT=lhsT, rhs=rhs, start=start, stop=stop), R, W)

    def tr(self, out, in_, ident, R, W):
        return self.op("pe", lambda e: e.transpose(out=out, in_=in_, identity=ident), R, W)

    def tt(self, eng, out, in0, in1, op, R, W):
        return self.op(eng, lambda e: e.tensor_tensor(out=out, in0=in0, in1=in1, op=op), R, W)

    def ts(self, eng, out, in0, s1, s2, op0, op1, R, W, **kw):
        if op1 is None:
            return self.op(eng, lambda e: e.tensor_scalar(out=out, in0=in0, scalar1=s1, scalar2=None, op0=op0, **kw), R, W)
        return self.op(eng, lambda e: e.tensor_scalar(out=out, in0=in0, scalar1=s1, scalar2=s2, op0=op0, op1=op1, **kw), R, W)

    def stt(self, eng, out, in0, scalar, in1, op0, op1, R, W):
        return self.op(eng, lambda e: e.scalar_tensor_tensor(out=out, in0=in0, scalar=scalar, in1=in1, op0=op0, op1=op1), R, W)

    def copy(self, eng, out, in_, R, W):
        if eng == "act":
            return self.op("act", lambda e: e.activation(out=out, in_=in_, func=AF.Copy), R, W)
        return self.op(eng, lambda e: e.tensor_copy(out=out, in_=in_), R, W)

    def red(self, eng, out, in_, op, R, W, axis=AX.X):
        return self.op(eng, lambda e: e.tensor_reduce(out=out, in_=in_, axis=axis, op=op), R, W)

    def memset(self, eng, ap, val, W):
        return self.op(eng, lambda e: e.memset(ap, val), (), W)

    def recip(self, out, in_, R, W):
        return self.op("dve", lambda e: e.reciprocal(out=out, in_=in_), R, W)

    def rstd(self, out, mean, tmp, R, W):
        self.ts("dve", tmp, mean, EPS, None, ALU.add, None, R, W)
        self.act(tmp, tmp, AF.Sqrt, W, W)
        self.recip(out, tmp, W, W)

    def run(self):
        used = [s for s in self.p.dsems if s not in self.free_dsems]
        er = self.eng["sp"]
        o = Op(None)
        for s in used:
            self._wait_for(er, o, ("d", s, s.count, self))
        er.ops.append(o)
        with self.nc.Block() as block:
            def mk(e):
                er = self.eng[e]
                sem_h = er.sem.h if er.sem is not None else None

                def body(engine):
                    for o in er.ops:
                        for (sh, val) in o.waits:
                            engine.wait_ge(sh, val)
                        if o.fn is None:
                            continue
                        ins = o.fn(engine)
                        if o.dma is not None:
                            ins.then_inc(o.dma, 16)
                        elif o.inc:
                            ins.then_inc(sem_h, 1)
                return body
            block.tensor(mk("pe"))
            block.scalar(mk("act"))
            block.vector(mk("dve"))
            block.gpsimd(mk("pool"))
            block.sync(mk("sp"))
        for e in self.eng.values():
            self.p.n_instr += len(e.ops)
        self.es.close()


class T:
    pass


def declare(nc, L):
    t = T()

    def din(name, shape):
        setattr(t, name, nc.dram_tensor(name, list(shape), F32, kind="ExternalInput").ap())

    def scr(name, shape, dt=F32):
        setattr(t, name, nc.dram_tensor(name, list(shape), dt, kind="Internal").ap())

    din("x", [L, D]); din("ctx", [LC, D]); din("condT", [128, 8, 2])
    din("ada_w", [4, D, 3 * D]); din("ada_b2", [4, 2, 3 * D])
    din("ident", [128, 128]); din("sel", [2, 2, 128])
    din("sgu_w_in", [2, D, 6144]); din("sgu_w_out", [2, 2048, D]); din("sgu_w_sT", [2, 128, 8, 128])
    din("sgu_b_sT", [2, 128, 8]); din("sgu_gain", [2, 2048])
    t.out = nc.dram_tensor("out", [L, D], F32, kind="ExternalOutput").ap()
    scr("ctxs", [LC, D])
    scr("gate_d", [4, 2, 128, D])
    return t


def phase_ada(prog, t, G):
    ph = Phase(prog, "ada")
    scond = ph.sb([128, 8, 2], F32)
    ph.dma("sp", scond[:], t.condT[:, :, :], W=[scond])
    ph.act(scond[:], scond[:], AF.Silu, [scond], [scond])
    identf = ph.sb([128, 128], F32)
    ph.dma("pool", identf[:], t.ident[:, :], W=[identf])
    sel = ph.sb([2, 2, 128], F32)
    ph.dma("pool", sel[:], t.sel[:, :, :], W=[sel])
    wk = [ph.sb([128, 3 * D], F32, "wk") for _ in range(8)]
    b2 = ph.sb([2, 3 * D], F32)
    mrow = ph.sb([2, 3 * D], F32)
    gsb = [ph.sb([128, D], F32, "gsb") for _ in range(2)]
    pm = [ph.ps([2, 512], F32, "pm") for _ in range(6)]
    pT = ph.ps([128, 16, 2], F32, "pT")
    pg = ph.ps([128, 512], F32, "pg")
    mod = G.mod
    for li in range(4):
        for k in range(8):
            ph.dma(ph.dmaq(), wk[k][:], t.ada_w[li, k * 128:(k + 1) * 128, :], W=[wk[k]])
        ph.dma("sp", b2[:], t.ada_b2[li, :, :], W=[b2])
        for k in range(8):
            for n in range(6):
                ph.mm(pm[n][:], scond[:, k, :], wk[k][:, n * 512:(n + 1) * 512], k == 0, k == 7, [scond, wk[k]], [pm[n]])
        for n in range(6):
            ph.tt("dve", mrow[:, n * 512:(n + 1) * 512], pm[n][:], b2[:, n * 512:(n + 1) * 512], ALU.add, [pm[n], b2], [mrow])
        for j in range(16):
            ph.tr(pT[:, j, :], mrow[:, j * 128:(j + 1) * 128], identf[0:2, 0:2], [mrow, identf], [pT])
        ph.copy("dve", mod[:, li, 0:16, :], pT[:], [pT], [mod])
        ph.ts("dve", mod[:, li, 8:16, :], mod[:, li, 8:16, :], 1.0, None, ALU.add, None, [mod], [mod])
        for r in range(2):
            for half in range(2):
                ph.mm(pg[:], sel[:, r, :], mrow[:, 2048 + half * 512:2048 + (half + 1) * 512], True, True, [sel, mrow], [pg])
                ph.copy("act", gsb[r][:, half * 512:(half + 1) * 512], pg[:], [pg], [gsb[r]])
            ph.dma("pool", t.gate_d[li, r, :, :], gsb[r][:], R=[gsb[r]])
    ph.run()


def rms_to_hT(ph, G, xt, hT, pTb, li, r, sq, st, si):
    xn = G.xn
    ph.act(G.junk[:], xt[:], AF.Square, [xt], [G.junk, st], scale=1.0 / 32.0, accum_out=st[:, si:si + 1])
    ph.rstd(st[:, si + 1:si + 2], st[:, si:si + 1], st[:, si + 2:si + 3], [st], [st])
    ph.act(xn[:], xt[:], AF.Copy, [xt, st], [xn], scale=st[:, si + 1:si + 2])
    for k in range(8):
        ph.tr(pTb[:, k, :], xn[:, k * 128:(k + 1) * 128], G.identb[:], [xn, G.identb], [pTb])
    for k in range(8):
        ph.ts("dve", hT[:, k, :], pTb[:, k, :], G.mod[:, li, 8 + k, r:r + 1], G.mod[:, li, k, r:r + 1], ALU.mult, ALU.add,
              [pTb, G.mod], [hT])


def phase_sgu(prog, t, G, li, j, tiles):
    es_w = ExitStack()
    w_in = prog.sb(es_w, [128, 8, 6144], BF16, "sgu_win")
    w_out = prog.sb(es_w, [128, 16, D], BF16, "sgu_wout")
    w_sT = prog.sb(es_w, [128, 8, 128], BF16, "sgu_ws")
    b_s = prog.sb(es_w, [128, 8], F32, "sgu_bs")
    gain = prog.sb(es_w, [128, 2048], F32, "sgu_gain")
    gate = [prog.sb(es_w, [128, D], F32, "sgu_gate") for _ in range(2)]
    ph = Phase(prog, "sguw%d" % li)
    stg = [ph.sb([128, 3072], F32, "stg") for _ in range(3)]
    engs = ("act", "dve", "pool")
    n = 0
    for k in range(8):
        for half in range(2):
            s = stg[n % 3]
            ph.dma(ph.dmaq(), s[:], t.sgu_w_in[j, k * 128:(k + 1) * 128, half * 3072:(half + 1) * 3072], W=[s])
            ph.copy(engs[n % 3], w_in[:, k, half * 3072:(half + 1) * 3072], s[:], [s], [w_in])
            n += 1
    for k3 in range(0, 16, 3):
        kk = min(3, 16 - k3)
        s = stg[n % 3]
        ph.dma(ph.dmaq(), s[:, 0:kk * D].rearrange("p (a d) -> p a d", d=D),
               t.sgu_w_out[j, k3 * 128:(k3 + kk) * 128, :].rearrange("(a p) d -> p a d", p=128), W=[s])
        ph.copy(engs[n % 3], w_out[:, k3:k3 + kk, :], s[:, 0:kk * D].rearrange("p (a d) -> p a d", d=D), [s], [w_out])
        n += 1
    s = stg[n % 3]
    ph.dma("sp", s[:, 0:1024].rearrange("p (g q) -> p g q", q=128), t.sgu_w_sT[j, :, :, :], W=[s])
    ph.copy("dve", w_sT[:], s[:, 0:1024].rearrange("p (g q) -> p g q", q=128), [s], [w_sT])
    ph.dma("sp", b_s[:], t.sgu_b_sT[j, :, :], W=[b_s])
    ph.dma("pool", gain[:], t.sgu_gain[j, :].partition_broadcast(128), W=[gain])
    for r in range(2):
        ph.dma("sp", gate[r][:], t.gate_d[li, r, :, :], W=[gate[r]])
    ph.run()
    ph = Phase(prog, "sgu%d" % li)
    xts = [ph.sb([128, D], F32, "xt") for _ in range(2)]
    hT = ph.sb([128, 8, 128], BF16, "hT")
    gu = ph.sb([128, 2048], BF16, "gu")
    gv = ph.sb([128, 2048], F32, "gv")
    sz = ph.sb([128, 2048], BF16, "sz")
    vn = ph.sb([128, 2048], BF16, "vn")
    sT = ph.sb([128, 16, 128], BF16, "sT")
    st = ph.sb([128, 16], F32, "st")
    pTa = ph.ps([128, 8, 128], BF16, "pTa")
    pTb = ph.ps([128, 8, 128], BF16, "pTb")
    pmm = [ph.ps([128, 512], F32, "pmm") for _ in range(2)]
    pmx = [ph.ps([128, 512], F32, "pmx") for _ in range(2)]
    po = [ph.ps([128, 512], F32, "po") for _ in range(2)]
    for ti, (src, dst, r) in enumerate(tiles):
        xt = xts[ti % 2]
        ph.dma("sp", xt[:], src, W=[xt])
        rms_to_hT(ph, G, xt, hT, pTa, li, r, None, st, 0)
        for n in range(12):
            pm = pmm[n % 2]
            for k in range(8):
                ph.mm(pm[:], hT[:, k, :], w_in[:, k, n * 512:(n + 1) * 512], k == 0, k == 7, [hT, w_in], [pm])
            if n < 4:
                ph.act(gu[:, n * 512:(n + 1) * 512], pm[:], AF.Gelu_apprx_tanh, [pm], [gu])
            elif n < 8:
                c = n - 4
                ph.act(gv[:, c * 512:(c + 1) * 512], pm[:], AF.Gelu_apprx_tanh, [pm], [gv])
                ph.act(G.junk[:, 0:512], gv[:, c * 512:(c + 1) * 512], AF.Square, [gv], [G.junk, st],
                       scale=1.0 / 32.0, accum_out=st[:, 4 + c:5 + c])
            else:
                c = n - 8
                ph.act(sz[:, c * 512:(c + 1) * 512], pm[:], AF.Silu, [pm], [sz])
        ph.red("dve", st[:, 8:9], st[:, 4:8], ALU.add, [st], [st])
        ph.ts("dve", st[:, 8:9], st[:, 8:9], 0.5, None, ALU.mult, None, [st], [st])
        ph.rstd(st[:, 9:10], st[:, 8:9], st[:, 10:11], [st], [st])
        ph.stt("dve", vn[:], gv[:], st[:, 9:10], gain[:], ALU.mult, ALU.mult, [gv, st, gain], [vn])
        ph.tt("dve", gu[:], gu[:], sz[:], ALU.mult, [gu, sz], [gu])
        for g in range(8):
            pm = pmx[(g // 2) % 2]
            o = (g % 2) * 256
            ph.mm(pm[:, o:o + 256], w_sT[:, g, :], vn[:, g * 256:(g + 1) * 256], True, True, [w_sT, vn], [pm])
            if g % 2 == 1:
                for gg in (g - 1, g):
                    oo = (gg % 2) * 256
                    ph.stt("dve", sz[:, gg * 256:(gg + 1) * 256], pm[:, oo:oo + 256], b_s[:, gg:gg + 1],
                           gu[:, gg * 256:(gg + 1) * 256], ALU.add, ALU.mult, [pm, b_s, gu], [sz])
        for k in range(16):
            pt = pTa if k < 8 else pTb
            ph.tr(pt[:, k % 8, :], sz[:, k * 128:(k + 1) * 128], G.identb[:], [sz, G.identb], [pt])
        ph.copy("act", sT[:, 0:8, :], pTa[:], [pTa], [sT])
        ph.copy("act", sT[:, 8:16, :], pTb[:], [pTb], [sT])
        for half in range(2):
            for k in range(16):
                ph.mm(po[half][:], sT[:, k, :], w_out[:, k, half * 512:(half + 1) * 512], k == 0, k == 15, [sT, w_out], [po[half]])
        for half in range(2):
            sl = slice(half * 512, (half + 1) * 512)
            ph.tt("dve", gv[:, sl], po[half][:], gate[r][:, sl], ALU.mult, [po[half], gate[r]], [gv])
            ph.tt("dve", xt[:, sl], gv[:, sl], xt[:, sl], ALU.add, [gv, xt], [xt])
        ph.dma("pool", dst, xt[:], R=[xt])
    ph.run()
    es_w.close()


C0 = float(np.exp(-0.5))


class PS2:
    def __init__(self, ph):
        self.h = [ph.ps([128, 512], F32, "ps2") for _ in range(2)]

    def __getitem__(self, key):
        rows, cols = key
        hi = cols.start // 512
        assert (cols.stop - 1) // 512 == hi
        return self.h[hi][rows, cols.start - hi * 512:cols.stop - hi * 512]

HS = (slice(0, 512), slice(512, 1024))
SD = BF16


def rwkv_consts():
    idx = np.arange(128)
    out = {}
    cm = np.zeros((2, 128, 3, 128), np.float32)
    mask4 = np.zeros((2, 128, 512), np.float32)
    maskT = np.zeros((2, 128, 256), np.float32)
    for n in range(2):
        before = (idx[:, None] < idx[None, :]) if n == 0 else (idx[:, None] > idx[None, :])
        incl = before | np.eye(128, dtype=bool)
        cm[n, :, 0, :] = -C0 * incl
        cm[n, :, 1, :] = -C0 * before.T
        cm[n, :, 2, :] = -C0
        mask4[n] = np.concatenate([before, incl, before, incl], 1)
        maskT[n] = np.concatenate([before.T, before.T], 1)
    out["rw_cm"] = cm
    out["rw_mask4"] = mask4
    out["rw_maskT"] = maskT
    ir = np.zeros((64, 16, 64), np.float32)
    for h in range(16):
        ir[:, h, :] = np.eye(64)
    out["identrep"] = ir.reshape(64, 1024)
    return out


def declare_rwkv(nc, t, L):
    def din(name, shape):
        setattr(t, name, nc.dram_tensor(name, list(shape), F32, kind="ExternalInput").ap())

    def scr(name, shape, dt=F32):
        setattr(t, name, nc.dram_tensor(name, list(shape), dt, kind="Internal").ap())
    NTOK = LC + L
    din("rw_cm", [2, 128, 3, 128]); din("rw_mask4", [2, 128, 512]); din("rw_maskT", [2, 128, 256]); din("identrep", [64, 1024])
    din("rw_w_in", [4, D, D]); din("rw_w1cat", [D, 128]); din("rw_a1cat", [D, 128])
    din("rw_w2cat", [128, D]); din("rw_a2cat", [128, D]); din("rw_muT", [128, 6, 8])
    din("rw_w0", [2, D]); din("rw_a0", [2, D]); din("rw_k_k", [D]); din("rw_k_a", [D]); din("rw_r_k", [D])
    din("rw_ln_g", [D]); din("rw_ln_b", [D]); din("rw_w_out", [D, D])
    scr("hTc", [8, 128, LC + 2], BF16); scr("hTl", [8, 128, L + 2], BF16)
    scr("sig_d", [2, NTOK, D]); scr("kdir_d", [2, NTOK, D], BF16); scr("b_d", [2, NTOK, D], BF16)
    scr("kk_d", [NTOK, D], BF16); scr("v_d", [NTOK, D], BF16); scr("r_d", [NTOK, D], BF16); scr("sz_d", [NTOK, D], BF16)
    scr("bon_d", [NTOK, 16]); scr("y_d", [2, NTOK, D])


def phase_rwkv_h(prog, t, G, li, L):
    ph = Phase(prog, "rwh")
    xts = [ph.sb([128, D], F32, "xt") for _ in range(2)]
    hTs = [ph.sb([128, 8, 128], BF16, "hT") for _ in range(2)]
    st = ph.sb([128, 16], F32, "st")
    zt = ph.sb([128, 8, 1], BF16, "zt")
    pTa = ph.ps([128, 8, 128], BF16, "pTa")
    ph.memset("dve", zt[:], 0.0, [zt])
    for (dst, n) in ((t.hTc, LC), (t.hTl, L)):
        ph.dma("sp", dst[:, :, 0:1].rearrange("k p t -> p k t"), zt[:], R=[zt], allow_slow_non_contiguous=True)
        ph.dma("sp", dst[:, :, n + 1:n + 2].rearrange("k p t -> p k t"), zt[:], R=[zt], allow_slow_non_contiguous=True)
    tiles = [(t.ctxs, t.hTc, i, 1) for i in range(LC // 128)] + [(t.out, t.hTl, i, 0) for i in range(L // 128)]
    for ti, (src, dst, i, r) in enumerate(tiles):
        xt = xts[ti % 2]; hT = hTs[ti % 2]
        ph.dma("sp", xt[:], src[i * 128:(i + 1) * 128, :], W=[xt])
        rms_to_hT(ph, G, xt, hT, pTa, li, r, None, st, 0)
        ph.dma("sp", dst[:, :, 1 + i * 128:1 + (i + 1) * 128].rearrange("k p t -> p k t"), hT[:], R=[hT])
    ph.run()


def phase_rwkv_feat(prog, t, G, L):
    es_w = ExitStack()
    W4 = prog.sb(es_w, [128, 4, 8, D], BF16, "rw_W4")
    w1c = prog.sb(es_w, [128, 8, 128], BF16, "rw_w1c")
    a1c = prog.sb(es_w, [128, 8, 128], BF16, "rw_a1c")
    w2c = prog.sb(es_w, [128, D], BF16, "rw_w2c")
    a2c = prog.sb(es_w, [128, D], BF16, "rw_a2c")
    muT = prog.sb(es_w, [128, 6, 8], F32, "rw_mu")
    w0b = prog.sb(es_w, [128, 2, D], F32, "rw_w0b")
    a0b = prog.sb(es_w, [128, 2, D], F32, "rw_a0b")
    kkb_ = prog.sb(es_w, [128, D], F32, "rw_kkb")
    kab = prog.sb(es_w, [128, D], F32, "rw_kab")
    rkb = prog.sb(es_w, [128, D], F32, "rw_rkb")
    ph = Phase(prog, "rwfw")
    stg = [ph.sb([128, 4, D], F32, "stg") for _ in range(2)]
    n = 0
    for c in range(4):
        for k0 in (0, 4):
            s = stg[n % 2]
            ph.dma("sp", s[:], t.rw_w_in[c, k0 * 128:(k0 + 4) * 128, :].rearrange("(a p) d -> p a d", p=128), W=[s])
            ph.copy(("act", "dve")[n % 2], W4[:, c, k0:k0 + 4, :], s[:], [s], [W4])
            n += 1
    for (src, dstb) in ((t.rw_w1cat, w1c), (t.rw_a1cat, a1c)):
        s = stg[n % 2]
        ph.dma("sp", s[:, 0, :].rearrange("p (a d) -> p a d", d=128), src.rearrange("(a p) d -> p a d", p=128), W=[s])
        ph.copy("dve", dstb[:], s[:, 0, :].rearrange("p (a d) -> p a d", d=128), [s], [dstb])
        n += 1
    for (src, dstb) in ((t.rw_w2cat, w2c), (t.rw_a2cat, a2c)):
        s = stg[n % 2]
        ph.dma("sp", s[:, 0, :], src[:, :], W=[s])
        ph.copy("dve", dstb[:], s[:, 0, :], [s], [dstb])
        n += 1
    ph.dma("sp", muT[:], t.rw_muT[:, :, :], W=[muT])
    for nn in range(2):
        ph.dma("sp", w0b[:, nn, :], t.rw_w0[nn, :].partition_broadcast(128), W=[w0b])
        ph.dma("sp", a0b[:, nn, :], t.rw_a0[nn, :].partition_broadcast(128), W=[a0b])
    ph.dma("sp", kkb_[:], t.rw_k_k.partition_broadcast(128), W=[kkb_])
    ph.dma("sp", kab[:], t.rw_k_a.partition_broadcast(128), W=[kab])
    ph.dma("sp", rkb[:], t.rw_r_k.partition_broadcast(128), W=[rkb])
    ph.run()

    ph = Phase(prog, "rwf")
    hws = [ph.sb([128, 8, 130], BF16, "hw") for _ in range(2)]
    xa = ph.sb([128, 8, 128], F32, "xa")
    xx = ph.sb([128, 8, 128], F32, "xx")
    xs = [ph.sb([128, 8, 128], BF16, "xs") for _ in range(6)]
    th = ph.sb([128, 128], BF16, "th")
    alb = ph.sb([128, 128], BF16, "alb")
    o_sig = [ph.sb([128, D], F32, "osig") for _ in range(2)]
    o_kd = [ph.sb([128, D], BF16, "okd") for _ in range(2)]
    o_b = [ph.sb([128, D], BF16, "ob") for _ in range(2)]
    o_kk = ph.sb([128, D], BF16, "okk")
    o_v = ph.sb([128, D], BF16, "ov")
    o_r = ph.sb([128, D], BF16, "or")
    o_sz = ph.sb([128, D], BF16, "osz")
    o_bon = ph.sb([128, 16], F32, "obon")
    rrk = ph.sb([128, D], F32, "rrk")
    tkk = ph.sb([128, D], F32, "tkk")
    kf = ph.sb([128, D], F32, "kf")
    tmp = ph.sb([128, D], F32, "tmp")
    an = ph.sb([128, D], F32, "an")
    kdf = ph.sb([128, D], F32, "kdf")
    st = ph.sb([128, 64], F32, "st")
    pa = [PS2(ph) for _ in range(3)]
    p1 = ph.ps([128, 256], F32, "p1")
    h3 = lambda ap: ap.rearrange("p (h k) -> p h k", k=64)
    tiles = [(t.hTc, i, i) for i in range(LC // 128)] + [(t.hTl, i, LC // 128 + i) for i in range(L // 128)]
    npa = 0
    import os
    CUT = int(os.environ.get("CUT", "99"))
    for ti, (src, i, g) in enumerate(tiles):
        if CUT == 0:
            break
        hw = hws[ti % 2]
        rows = slice(g * 128, (g + 1) * 128)
        ph.dma("sp", hw[:], src[:, :, i * 128:i * 128 + 130].rearrange("k p t -> p k t"), W=[hw])
        ph.tt("dve", xa[:], hw[:, :, 0:128], hw[:, :, 2:130], ALU.add, [hw], [xa])
        ph.stt("dve", xx[:], xa[:], 0.5, hw[:, :, 1:129], ALU.mult, ALU.subtract, [xa, hw], [xx])
        for c in range(6):
            for k in range(8):
                ph.stt("dve", xs[c][:, k, :], xx[:, k, :], muT[:, c, k:k + 1], hw[:, k, 1:129], ALU.mult, ALU.add, [xx, muT, hw], [xs[c]])
        if CUT == 1:
            continue
        for k in range(8):
            ph.mm(p1[:, 0:128], w1c[:, k, :], xs[4][:, k, :], k == 0, k == 7, [w1c, xs[4]], [p1])
        for k in range(8):
            ph.mm(p1[:, 128:256], a1c[:, k, :], xs[5][:, k, :], k == 0, k == 7, [a1c, xs[5]], [p1])
        ph.act(th[:], p1[:, 0:128], AF.Tanh, [p1], [th])
        ph.copy("act", alb[:], p1[:, 128:256], [p1], [alb])
        if CUT == 2:
            continue
        pcs = []
        for c in range(4):
            p = pa[npa % 3]; npa += 1
            for half in range(2):
                for k in range(8):
                    ph.mm(p[:, half * 512:(half + 1) * 512], xs[c][:, k, :], W4[:, c, k, half * 512:(half + 1) * 512], k == 0, k == 7, [xs[c], W4], p.h)
            for hs in HS:
                if CUT == 30:
                    continue
                if c == 0:
                    ph.copy("act", o_r[:, hs], p[:, hs], p.h, [o_r])
                    if CUT != 31:
                        ph.tt("dve", rrk[:, hs], p[:, hs], rkb[:, hs], ALU.mult, p.h + [rkb], [rrk])
                elif c == 1:
                    ph.copy("act", kf[:, hs], p[:, hs], p.h, [kf])
                    if CUT != 31:
                        ph.tt("dve", tkk[:, hs], p[:, hs], kkb_[:, hs], ALU.mult, p.h + [kkb_], [tkk])
                elif c == 2:
                    ph.copy("act", o_v[:, hs], p[:, hs], p.h, [o_v])
                else:
                    ph.act(o_sz[:, hs], p[:, hs], AF.Silu, p.h, [o_sz])
        if CUT in (3, 30, 31):
            continue
        ph.tt("dve", tmp[:], tkk[:], tkk[:], ALU.mult, [tkk], [tmp])
        ph.red("dve", st[:, 0:16], h3(tmp[:]), ALU.add, [tmp], [st])
        ph.ts("dve", st[:, 0:16], st[:, 0:16], 1e-12, None, ALU.add, None, [st], [st])
        ph.act(st[:, 0:16], st[:, 0:16], AF.Sqrt, [st], [st])
        ph.recip(st[:, 16:32], st[:, 0:16], [st], [st])
        ph.tt("dve", h3(o_kk[:]), h3(tkk[:]), st[:, 16:32].unsqueeze(2).broadcast_to([128, 16, 64]), ALU.mult, [tkk, st], [o_kk])
        if CUT == 4:
            continue
        for n in range(2):
            p = pa[npa % 3]; npa += 1
            for half in range(2):
                ph.mm(p[:, half * 512:(half + 1) * 512], th[64 * n:64 * n + 64, :], w2c[64 * n:64 * n + 64, half * 512:(half + 1) * 512], True, True, [th, w2c], p.h)
            for hs in HS:
                ph.tt("dve", tmp[:, hs], p[:, hs], w0b[:, n, hs], ALU.add, p.h + [w0b], [tmp])
            ph.act(o_sig[n][:], tmp[:], AF.Sigmoid, [tmp], [o_sig[n]])
            p = pa[npa % 3]; npa += 1
            for half in range(2):
                ph.mm(p[:, half * 512:(half + 1) * 512], alb[64 * n:64 * n + 64, :], a2c[64 * n:64 * n + 64, half * 512:(half + 1) * 512], True, True, [alb, a2c], p.h)
            for hs in HS:
                ph.tt("dve", tmp[:, hs], p[:, hs], a0b[:, n, hs], ALU.add, p.h + [a0b], [tmp])
            ph.act(an[:], tmp[:], AF.Sigmoid, [tmp], [an])
            ph.stt("dve", tmp[:], an[:], -1.0, kab[:], ALU.add, ALU.mult, [an, kab], [tmp])
            ph.stt("dve", kdf[:], tmp[:], 1.0, kf[:], ALU.add, ALU.mult, [tmp, kf], [kdf])
            ph.copy("act", o_kd[n][:], kdf[:], [kdf], [o_kd[n]])
            ph.tt("dve", o_b[n][:], o_kk[:], an[:], ALU.mult, [o_kk, an], [o_b[n]])
            ph.tt("dve", tmp[:], rrk[:], kdf[:], ALU.mult, [rrk, kdf], [tmp])
            ph.red("dve", st[:, 32 + 16 * n:48 + 16 * n], h3(tmp[:]), ALU.add, [tmp], [st])
        if CUT == 5:
            continue
        ph.tt("dve", o_bon[:], st[:, 32:48], st[:, 48:64], ALU.add, [st], [o_bon])
        for n in range(2):
            ph.dma("sp", t.sig_d[n, rows, :], o_sig[n][:], R=[o_sig[n]])
            ph.dma("sp", t.kdir_d[n, rows, :], o_kd[n][:], R=[o_kd[n]])
            ph.dma("sp", t.b_d[n, rows, :], o_b[n][:], R=[o_b[n]])
        ph.dma("sp", t.kk_d[rows, :], o_kk[:], R=[o_kk])
        ph.dma("sp", t.v_d[rows, :], o_v[:], R=[o_v])
        ph.dma("sp", t.r_d[rows, :], o_r[:], R=[o_r])
        ph.dma("sp", t.sz_d[rows, :], o_sz[:], R=[o_sz])
        ph.dma("sp", t.bon_d[rows, :], o_bon[:], R=[o_bon])
    ph.run()
    es_w.close()


def phase_rwkv_scan(prog, t, G, L):
    ph = Phase(prog, "rws")
    NT = (LC + L) // 128
    nct = LC // 128
    order = [list(range(NT)), list(range(nct - 1, -1, -1)) + list(range(NT - 1, nct - 1, -1))]
    cmf = ph.sb([128, 2, 3, 128], F32, "cm")
    mask4 = ph.sb([128, 2, 512], F32, "mask4")
    maskT = ph.sb([128, 2, 256], F32, "maskT")
    idrep = ph.sb([64, D], F32, "idrep")
    identS = ph.sb([128, 128], SD, "identS")
    for n in range(2):
        ph.dma("sp", cmf[:, n, :, :], t.rw_cm[n, :, :, :], W=[cmf])
        ph.dma("sp", mask4[:, n, :], t.rw_mask4[n, :, :], W=[mask4])
        ph.dma("sp", maskT[:, n, :], t.rw_maskT[n, :, :], W=[maskT])
    ph.dma("sp", idrep[:], t.identrep[:, :], W=[idrep])
    ph.copy("dve", identS[:], G.identf[:], [G.identf], [identS])
    def mk(shape, dt, nm, k=2):
        return [ph.sb(shape, dt, nm) for _ in range(k)]
    i_sig = [mk([128, D], F32, "isig", 1) * 2 for _ in range(2)]
    i_kk = [mk([128, D], BF16, "ikk", 1) * 2 for _ in range(2)]
    i_b = [mk([128, D], BF16, "ib", 1) * 2 for _ in range(2)]
    i_kd = [mk([128, D], BF16, "ikd", 1) * 2 for _ in range(2)]
    i_v = [mk([128, D], BF16, "iv") for _ in range(2)]
    i_r = [mk([128, D], BF16, "ir", 1) * 2 for _ in range(2)]
    Gt = ph.sb([128, D], F32, "Gt")
    t1 = ph.sb([128, D], F32, "t1")
    TMa = ph.sb([128, D], SD, "TMa"); TMr = ph.sb([128, D], SD, "TMr"); TMb = ph.sb([128, D], SD, "TMb"); TMk = ph.sb([128, D], SD, "TMk")
    bh = [ph.sb([128, D], SD, "bh") for _ in range(2)]
    kh = [ph.sb([128, D], SD, "kh") for _ in range(2)]
    DG = [ph.sb([64, D], F32, "DG") for _ in range(2)]
    XT = [[ph.sb([128, 4, 128], SD, "XT") for _ in range(8)] for _ in range(2)]
    Zbig = [ph.sb([128, 8, 2, 2, 64], SD, "Z") for _ in range(2)]
    Zv = [[ph.view(Zbig[n], (slice(None), hp)) for hp in range(8)] for n in range(2)]
    GR = [[ph.sb([128, 512], SD, "GR") for _ in range(2)] for _ in range(8)]
    P0 = [ph.sb([128, 2, 128], SD, "P0") for _ in range(8)]
    QP = [[ph.sb([128, 4, 128], SD, "QP") for _ in range(2)] for _ in range(8)]
    RhT = [[ph.sb([64, 2, 128], SD, "RhT") for _ in range(8)] for _ in range(2)]
    YU = [[ph.sb([128, 128], F32, "YU") for _ in range(8)] for _ in range(2)]
    SU = [[ph.sb([64, 2, 64], F32, "SU") for _ in range(8)] for _ in range(2)]
    ACT_ = [[ph.sb([64, 2, 64], SD, "ACT") for _ in range(8)] for _ in range(2)]
    Ss = [[ph.sb([64, 16, 64], SD, "Ss") for _ in range(2)] for _ in range(2)]
    Sv = [[[ph.view(Ss[n][b], (slice(None), slice(2 * hp, 2 * hp + 2))) for hp in range(8)] for b in range(2)] for n in range(2)]
    Yt = [mk([128, D], F32, "Yt") for _ in range(2)]
    Yv = [[[ph.view(Yt[n][b], (slice(None), slice(hp * 128, (hp + 1) * 128))) for hp in range(8)] for b in range(2)] for n in range(2)]
    pL = PS2(ph)
    pool = [ph.ps([128, 512], F32, "pp") for _ in range(5)]
    pTt = ph.ps([128, 4, 128], SD, "pTt")
    cnt = [0]

    def bank():
        cnt[0] += 1
        return pool[cnt[0] % 5]
    for n in range(2):
        ph.memset("dve", Ss[n][0][:], 0.0, Sv[n][0])
    h4 = lambda ap: ap.rearrange("p (a b k) -> p a b k", b=2, k=64)
    import os
    CUT2 = int(os.environ.get("CUT2", "99"))
    for s in range(NT):
        for n in range(2):
            g = order[n][s]
            rows = slice(g * 128, (g + 1) * 128)
            sb_ = s % 2
            sig = i_sig[n][sb_]; kkt = i_kk[n][sb_]; bt = i_b[n][sb_]; kdt = i_kd[n][sb_]; vt = i_v[n][sb_]; rt = i_r[n][sb_]
            ph.dma("sp", sig[:], t.sig_d[n, rows, :], W=[sig])
            ph.dma("sp", kkt[:], t.kk_d[rows, :], W=[kkt])
            ph.dma("sp", bt[:], t.b_d[n, rows, :], W=[bt])
            ph.dma("sp", kdt[:], t.kdir_d[n, rows, :], W=[kdt])
            ph.dma("sp", vt[:], t.v_d[rows, :], W=[vt])
            ph.dma("sp", rt[:], t.r_d[rows, :], W=[rt])
            for half in range(2):
                ph.mm(pL[:, half * 512:(half + 1) * 512], cmf[:, n, 0, :], sig[:, half * 512:(half + 1) * 512], True, True, [cmf, sig], pL.h)
            for hs in HS:
                ph.act(Gt[:, hs], pL[:, hs], AF.Exp, pL.h, [Gt])
            ph.tt("dve", TMr[:], rt[:], Gt[:], ALU.mult, [rt, Gt], [TMr])
            for hs in HS:
                ph.act(Gt[:, hs], pL[:, hs], AF.Exp, pL.h, [Gt], scale=-1.0)
            ph.tt("dve", TMb[:], bt[:], Gt[:], ALU.mult, [bt, Gt], [TMb])
            ph.tt("dve", TMk[:], kdt[:], Gt[:], ALU.mult, [kdt, Gt], [TMk])
            for hs in HS:
                ph.stt("dve", t1[:, hs], sig[:, hs], C0, pL[:, hs], ALU.mult, ALU.add, [sig] + pL.h, [t1])
            ph.act(Gt[:], t1[:], AF.Exp, [t1], [Gt])
            ph.stt("dve", TMa[:], kkt[:], -1.0, Gt[:], ALU.mult, ALU.mult, [kkt, Gt], [TMa])
            ph.copy("act", Zbig[n][:, :, :, 0, :], h4(TMa[:]), [TMa], Zv[n])
            for half in range(2):
                ph.mm(pL[:, half * 512:(half + 1) * 512], cmf[:, n, 1, :], sig[:, half * 512:(half + 1) * 512], True, True, [cmf, sig], pL.h)
            for hs in HS:
                ph.act(Gt[:, hs], pL[:, hs], AF.Exp, pL.h, [Gt])
            ph.tt("dve", bh[n][:], bt[:], Gt[:], ALU.mult, [bt, Gt], [bh[n]])
            ph.tt("dve", kh[n][:], kdt[:], Gt[:], ALU.mult, [kdt, Gt], [kh[n]])
            for half in range(2):
                ph.mm(pL[:, half * 512:(half + 1) * 512], cmf[:, n, 2, :], sig[:, half * 512:(half + 1) * 512], True, True, [cmf, sig], pL.h)
            for hs in HS:
                ph.act(Gt[0:64, hs], pL[0:64, hs], AF.Exp, pL.h, [Gt])
            ph.tt("dve", DG[n][:], Gt[0:64, :], idrep[:], ALU.mult, [Gt, idrep], [DG[n]])
            for hp in range(8):
                cs = slice(hp * 128, (hp + 1) * 128)
                for j, TM in enumerate((TMa, TMr, TMb, TMk)):
                    ph.tr(pTt[:, j, :], TM[:, cs], identS[:], [TM, identS], [pTt])
                ph.copy("act", XT[n][hp][:], pTt[:], [pTt], [XT[n][hp]])
            if CUT2 == 1:
                continue
            for hp in range(8):
                X = XT[n][hp]
                for hh in range(2):
                    b0 = 64 * hh
                    h = 2 * hp + hh
                    g1 = bank()
                    AR = X[b0:b0 + 64, 0:2, :].rearrange("p a t -> p (a t)")
                    ph.mm(g1[:, 0:256], X[b0:b0 + 64, 2, :], AR, True, True, [X], [g1])
                    ph.mm(g1[:, 256:512], X[b0:b0 + 64, 3, :], AR, True, True, [X], [g1])
                    ph.tt("dve", GR[hp][hh][:], g1[:], mask4[:, n, :], ALU.mult, [g1, mask4], [GR[hp][hh]])
                    g3 = bank()
                    ph.mm(g3[:, 0:128], X[b0:b0 + 64, 0, :], X[b0:b0 + 64, 2, :], True, True, [X], [g3])
                    ph.tt("dve", P0[hp][:, hh, :], g3[:, 0:128], maskT[:, n, 0:128], ALU.mult, [g3, maskT], [P0[hp]])
                    ph.mm(g3[:, 128:192], GR[hp][hh][:, 256:384], vt[:, h * 64:(h + 1) * 64], True, True, [GR[hp][hh], vt], [g3])
                    ph.copy("act", Zv[n][hp][:, hh, 1, :], g3[:, 128:192], [g3], [Zv[n][hp]])
            if CUT2 in (2, 20, 21, 22):
                continue
            for j in range(7):
                for hp in range(8):
                    if j == 0:
                        Qs = [GR[hp][hh][:, 0:128] for hh in range(2)]
                        Ps = [P0[hp][:, hh, :] for hh in range(2)]
                        qb = [GR[hp][0], GR[hp][1], P0[hp]]
                    else:
                        cur = QP[hp][(j - 1) % 2]
                        Qs = [cur[:, hh, :] for hh in range(2)]
                        Ps = [cur[:, 2 + hh, :] for hh in range(2)]
                        qb = [cur]
                    zp = bank()
                    for hh in range(2):
                        ph.mm(zp[:, hh * 128:(hh + 1) * 128], Qs[hh], Zv[n][hp][:, hh, :, :].rearrange("p a k -> p (a k)"), True, True, qb + [Zv[n][hp]], [zp])
                    zv = Zv[n][hp][:, :, :, :].rearrange("p b a k -> p (b a k)")
                    ph.tt("dve", zv, zv, zp[:, 0:256], ALU.add, [zp, Zv[n][hp]], [Zv[n][hp]])
                    if j < 6:
                        nx = QP[hp][j % 2]
                        qp = bank()
                        for hh in range(2):
                            ph.mm(qp[:, hh * 128:(hh + 1) * 128], Ps[hh], Qs[hh], True, True, qb, [qp])
                            ph.mm(qp[:, 256 + hh * 128:256 + (hh + 1) * 128], Qs[hh], Ps[hh], True, True, qb, [qp])
                        ph.copy("act", nx[:].rearrange("p a b -> p (a b)"), qp[:], [qp], [nx])
            if CUT2 == 3:
                continue
            for hp in range(8):
                f1 = bank()
                for hh in range(2):
                    h = 2 * hp + hh
                    nbr = GR[hp][hh][:, 128:256]
                    ph.mm(f1[0:64, hh * 128:(hh + 1) * 128], Zv[n][hp][:, hh, 0, :], nbr, True, False, [Zv[n][hp], GR[hp][hh]], [f1])
                    ph.mm(f1[0:64, hh * 128:(hh + 1) * 128], TMr[:, h * 64:(h + 1) * 64], identS[:], False, True, [TMr, identS], [f1])
                ph.copy("act", RhT[n][hp][:].rearrange("p a b -> p (a b)"), f1[0:64, 0:256], [f1], [RhT[n][hp]])
                f2 = bank()
                for hh in range(2):
                    h = 2 * hp + hh
                    nbr = GR[hp][hh][:, 128:256]
                    nkr = GR[hp][hh][:, 384:512]
                    cu = Zv[n][hp][:, hh, 1, :]
                    ph.mm(f2[:, hh * 64:(hh + 1) * 64], nbr, cu, True, False, [GR[hp][hh], Zv[n][hp]], [f2])
                    ph.mm(f2[:, hh * 64:(hh + 1) * 64], nkr, vt[:, h * 64:(h + 1) * 64], False, True, [GR[hp][hh], vt], [f2])
                    ph.mm(f2[0:64, 128 + hh * 64:128 + (hh + 1) * 64], bh[n][:, h * 64:(h + 1) * 64], cu, True, False, [bh[n], Zv[n][hp]], [f2])
                    ph.mm(f2[0:64, 128 + hh * 64:128 + (hh + 1) * 64], kh[n][:, h * 64:(h + 1) * 64], vt[:, h * 64:(h + 1) * 64], False, True, [kh[n], vt], [f2])
                    ph.mm(f2[0:64, 256 + hh * 64:256 + (hh + 1) * 64], Zv[n][hp][:, hh, 0, :], bh[n][:, h * 64:(h + 1) * 64], True, True, [Zv[n][hp], bh[n]], [f2])
                ph.copy("act", YU[n][hp][:], f2[:, 0:128], [f2], [YU[n][hp]])
                ph.copy("act", SU[n][hp][:].rearrange("p a b -> p (a b)"), f2[0:64, 128:256], [f2], [SU[n][hp]])
                ph.tt("dve", ACT_[n][hp][:].rearrange("p a b -> p (a b)"), f2[0:64, 256:384], DG[n][:, hp * 128:(hp + 1) * 128], ALU.add, [f2, DG[n]], [ACT_[n][hp]])
            if CUT2 == 4:
                continue
            Sc = Sv[n][s % 2]; Sn = Sv[n][(s + 1) % 2]
            Y = Yv[n][s % 2]
            for hp in range(8):
                f4 = bank()
                for hh in range(2):
                    ph.mm(f4[:, hh * 64:(hh + 1) * 64], RhT[n][hp][:, hh, :], Sc[hp][:, hh, :], True, True, [RhT[n][hp], Sc[hp]], [f4])
                    ph.mm(f4[0:64, 128 + hh * 64:128 + (hh + 1) * 64], ACT_[n][hp][:, hh, :], Sc[hp][:, hh, :], True, True, [ACT_[n][hp], Sc[hp]], [f4])
                ph.tt("dve", Y[hp][:], f4[:, 0:128], YU[n][hp][:], ALU.add, [f4, YU[n][hp]], [Y[hp]])
                ph.tt("dve", Sn[hp][:].rearrange("p a b -> p (a b)"), f4[0:64, 128:256], SU[n][hp][:].rearrange("p a b -> p (a b)"), ALU.add, [f4, SU[n][hp]], [Sn[hp]])
            ph.dma("sp", t.y_d[n, rows, :], Yt[n][s % 2][:], R=Y, sbuf=Yt[n][s % 2])
    ph.run()


def phase_rwkv_out(prog, t, G, li, L):
    es_w = ExitStack()
    w_out = prog.sb(es_w, [128, 8, D], BF16, "rwo_w")
    lng = prog.sb(es_w, [128, D], F32, "rwo_g")
    lnb = prog.sb(es_w, [128, D], F32, "rwo_b")
    gate = [prog.sb(es_w, [128, D], F32, "rwo_gate") for _ in range(2)]
    ph = Phase(prog, "rwow")
    stg = [ph.sb([128, 4, D], F32, "stg") for _ in range(2)]
    for n, k0 in enumerate((0, 4)):
        s = stg[n]
        ph.dma("sp", s[:], t.rw_w_out[k0 * 128:(k0 + 4) * 128, :].rearrange("(a p) d -> p a d", p=128), W=[s])
        ph.copy(("act", "dve")[n], w_out[:, k0:k0 + 4, :], s[:], [s], [w_out])
    ph.dma("sp", lng[:], t.rw_ln_g.partition_broadcast(128), W=[lng])
    ph.dma("sp", lnb[:], t.rw_ln_b.partition_broadcast(128), W=[lnb])
    for r in range(2):
        ph.dma("sp", gate[r][:], t.gate_d[li, r, :, :], W=[gate[r]])
    ph.run()
    ph = Phase(prog, "rwo")
    y0 = [ph.sb([128, D], F32, "y0") for _ in range(2)]
    y1 = [ph.sb([128, D], F32, "y1") for _ in range(2)]
    vt = [ph.sb([128, D], BF16, "v") for _ in range(2)]
    szt = [ph.sb([128, D], BF16, "sz") for _ in range(2)]
    bon = [ph.sb([128, 16], F32, "bon") for _ in range(2)]
    xts = [ph.sb([128, D], F32, "xt") for _ in range(2)]
    tmp = ph.sb([128, D], F32, "tmp")
    ob = ph.sb([128, D], BF16, "ob")
    oT = ph.sb([128, 8, 128], BF16, "oT")
    st = ph.sb([128, 64], F32, "st")
    pT = ph.ps([128, 8, 128], BF16, "pT")
    po = [ph.ps([128, 512], F32, "po") for _ in range(2)]
    h3 = lambda ap: ap.rearrange("p (h k) -> p h k", k=64)
    bc = lambda ap: ap.unsqueeze(2).broadcast_to([128, 16, 64])
    tiles = [(t.ctxs, i, i, 1) for i in range(LC // 128)] + [(t.out, i, LC // 128 + i, 0) for i in range(L // 128)]
    for ti, (xs_, i, g, r) in enumerate(tiles):
        rows = slice(g * 128, (g + 1) * 128)
        b_ = ti % 2
        ya = y0[b_]; yb = y1[b_]; v = vt[b_]; sz = szt[b_]; bo = bon[b_]; xt = xts[b_]
        ph.dma("sp", ya[:], t.y_d[0, rows, :], W=[ya])
        ph.dma("sp", yb[:], t.y_d[1, rows, :], W=[yb])
        ph.dma("sp", v[:], t.v_d[rows, :], W=[v])
        ph.dma("sp", sz[:], t.sz_d[rows, :], W=[sz])
        ph.dma("sp", bo[:], t.bon_d[rows, :], W=[bo])
        ph.dma("sp", xt[:], xs_[i * 128:(i + 1) * 128, :], W=[xt])
        ph.tt("dve", ya[:], ya[:], yb[:], ALU.add, [ya, yb], [ya])
        ph.red("dve", st[:, 0:16], h3(ya[:]), ALU.add, [ya], [st])
        ph.ts("dve", st[:, 0:16], st[:, 0:16], 1.0 / 64, None, ALU.mult, None, [st], [st])
        ph.tt("dve", h3(ya[:]), h3(ya[:]), bc(st[:, 0:16]), ALU.subtract, [ya, st], [ya])
        ph.tt("dve", tmp[:], ya[:], ya[:], ALU.mult, [ya], [tmp])
        ph.red("dve", st[:, 16:32], h3(tmp[:]), ALU.add, [tmp], [st])
        ph.ts("dve", st[:, 16:32], st[:, 16:32], 1.0 / 64, 64e-5, ALU.mult, ALU.add, [st], [st])
        ph.act(st[:, 16:32], st[:, 16:32], AF.Sqrt, [st], [st])
        ph.recip(st[:, 32:48], st[:, 16:32], [st], [st])
        ph.tt("dve", h3(ya[:]), h3(ya[:]), bc(st[:, 32:48]), ALU.mult, [ya, st], [ya])
        ph.tt("dve", ya[:], ya[:], lng[:], ALU.mult, [ya, lng], [ya])
        ph.tt("dve", ya[:], ya[:], lnb[:], ALU.add, [ya, lnb], [ya])
        ph.tt("dve", h3(tmp[:]), h3(v[:]), bc(bo[:]), ALU.mult, [v, bo], [tmp])
        ph.tt("dve", ya[:], ya[:], tmp[:], ALU.add, [ya, tmp], [ya])
        ph.tt("dve", ob[:], ya[:], sz[:], ALU.mult, [ya, sz], [ob])
        for k in range(8):
            ph.tr(pT[:, k, :], ob[:, k * 128:(k + 1) * 128], G.identb[:], [ob, G.identb], [pT])
        ph.copy("act", oT[:], pT[:], [pT], [oT])
        for half in range(2):
            for k in range(8):
                ph.mm(po[half][:], oT[:, k, :], w_out[:, k, half * 512:(half + 1) * 512], k == 0, k == 7, [oT, w_out], [po[half]])
        for half in range(2):
            sl = slice(half * 512, (half + 1) * 512)
            ph.tt("dve", tmp[:, sl], po[half][:], gate[r][:, sl], ALU.mult, [po[half], gate[r]], [tmp])
            ph.tt("dve", xt[:, sl], tmp[:, sl], xt[:, sl], ALU.add, [tmp, xt], [xt])
        ph.dma("sp", xs_[i * 128:(i + 1) * 128, :], xt[:], R=[xt])
    ph.run()
    es_w.close()

MLA_SCALE_ = 192 ** -0.5


def declare_mla(nc, t, L):
    def din(name, shape):
        setattr(t, name, nc.dram_tensor(name, list(shape), F32, kind="ExternalInput").ap())

    def scr(name, shape, dt=F32):
        setattr(t, name, nc.dram_tensor(name, list(shape), dt, kind="Internal").ap())
    NTOK = LC + L
    din("ml_w_in", [D, 3136]); din("ml_w_uq", [768, 3072]); din("ml_w_uk", [256, 2048]); din("ml_w_uv", [256, 2048])
    din("ml_qn", [768]); din("ml_kvn", [256]); din("ml_gq", [192]); din("ml_gk", [192]); din("ml_w_out", [2048, D])
    din("rope_cs", [L, 64])
    scr("QT_d", [16, 2, 128, L], BF16); scr("KnT_d", [16, 128, NTOK], BF16); scr("KrT_d", [64, NTOK], BF16)
    scr("V_d", [NTOK, 2048], BF16); scr("rk_d", [NTOK, 16]); scr("msz_d", [L, 2048], BF16); scr("o_d", [L, 2048])


def phase_mla_proj(prog, t, G, li, L):
    es_w = ExitStack()
    w_in = prog.sb(es_w, [128, 8, 3136], BF16, "ml_win")
    w_uq = prog.sb(es_w, [128, 6, 3072], BF16, "ml_wuq")
    w_uk = prog.sb(es_w, [128, 2, 2048], BF16, "ml_wuk")
    w_uv = prog.sb(es_w, [128, 2, 2048], BF16, "ml_wuv")
    qnb = prog.sb(es_w, [128, 768], F32, "ml_qnb")
    kvnb = prog.sb(es_w, [128, 256], F32, "ml_kvnb")
    gq = prog.sb(es_w, [128, 192], F32, "ml_gq")
    gk = prog.sb(es_w, [128, 192], F32, "ml_gk")
    ones = prog.sb(es_w, [128, 1], BF16, "ml_ones")
    ph = Phase(prog, "mlw")
    stg = [ph.sb([128, 3136], F32, "stg") for _ in range(2)]
    n = 0
    engs = ("act", "dve")
    for k in range(8):
        s = stg[n % 2]
        ph.dma("sp", s[:], t.ml_w_in[k * 128:(k + 1) * 128, :], W=[s])
        ph.copy(engs[n % 2], w_in[:, k, :], s[:], [s], [w_in]); n += 1
    for k in range(6):
        s = stg[n % 2]
        ph.dma("sp", s[:, 0:3072], t.ml_w_uq[k * 128:(k + 1) * 128, :], W=[s])
        ph.copy(engs[n % 2], w_uq[:, k, :], s[:, 0:3072], [s], [w_uq]); n += 1
    for (src, dst) in ((t.ml_w_uk, w_uk), (t.ml_w_uv, w_uv)):
        for k in range(2):
            s = stg[n % 2]
            ph.dma("sp", s[:, 0:2048], src[k * 128:(k + 1) * 128, :], W=[s])
            ph.copy(engs[n % 2], dst[:, k, :], s[:, 0:2048], [s], [dst]); n += 1
    ph.dma("sp", qnb[:], t.ml_qn.partition_broadcast(128), W=[qnb])
    ph.dma("sp", kvnb[:], t.ml_kvn.partition_broadcast(128), W=[kvnb])
    ph.dma("sp", gq[:], t.ml_gq.partition_broadcast(128), W=[gq])
    ph.dma("sp", gk[:], t.ml_gk.partition_broadcast(128), W=[gk])
    ph.tt("dve", gq[:, 0:128], gq[:, 0:128], gk[:, 0:128], ALU.mult, [gq, gk], [gq])
    ph.memset("dve", ones[:], 1.0, [ones])
    ph.run()

    ph = Phase(prog, "mlp")
    xts = [ph.sb([128, D], F32, "xt") for _ in range(2)]
    hT = ph.sb([128, 8, 128], BF16, "hT")
    cqkv = ph.sb([128, 1088], F32, "cqkv")
    cn = ph.sb([128, 1024], BF16, "cn")
    cT = ph.sb([128, 8, 128], BF16, "cT")
    q = ph.sb([128, 3072], F32, "q")
    tmp = ph.sb([128, 3072], F32, "tmp")
    Qt = ph.sb([128, 16, 256], BF16, "Qt")
    QTt = ph.sb([128, 32, 128], BF16, "QTt")
    KnT = ph.sb([128, 16, 128], BF16, "KnT")
    sq = ph.sb([128, 16, 128], BF16, "sq")
    Vt = ph.sb([128, 2048], BF16, "Vt")
    szt = ph.sb([128, 2048], BF16, "szt")
    cs = [ph.sb([128, 64], F32, "cs") for _ in range(2)]
    krg = ph.sb([128, 64], F32, "krg")
    KRt = ph.sb([128, 64], BF16, "KRt")
    KRT = ph.sb([64, 128], BF16, "KRT")
    rk = ph.sb([128, 16], F32, "rk")
    st = ph.sb([128, 64], F32, "st")
    r4 = ph.sb([128, 4, 16, 32], F32, "r4")
    pT = [ph.ps([128, 8, 128], BF16, "pT") for _ in range(2)]
    pc = [ph.ps([128, 512], F32, "pc") for _ in range(3)]
    pk = [ph.ps([128, 4, 128], F32, "pk") for _ in range(2)]
    pss = ph.ps([128, 16], F32, "pss")
    npc = [0]

    def nxt():
        npc[0] += 1
        return pc[npc[0] % 3]
    tiles = [(t.ctxs, i, i, 1) for i in range(LC // 128)] + [(t.out, i, LC // 128 + i, 0) for i in range(L // 128)]
    for ti, (src, i, g, r) in enumerate(tiles):
        lat = (r == 0)
        rows = slice(g * 128, (g + 1) * 128)
        xt = xts[ti % 2]
        ph.dma("sp", xt[:], src[i * 128:(i + 1) * 128, :], W=[xt])
        if lat:
            csb = cs[ti % 2]
            ph.dma("sp", csb[:], t.rope_cs[i * 128:(i + 1) * 128, :], W=[csb])
        rms_to_hT(ph, G, xt, hT, pT[0], li, r, None, st, 0)
        if lat:
            chunks = [(0, 512), (512, 1024), (1024, 1088)]
        else:
            chunks = [(768, 1088)]
        for (c0, c1) in chunks:
            p = nxt()
            for k in range(8):
                ph.mm(p[:, 0:c1 - c0], hT[:, k, :], w_in[:, k, c0:c1], k == 0, k == 7, [hT, w_in], [p])
            ph.copy("act", cqkv[:, c0:c1], p[:, 0:c1 - c0], [p], [cqkv])
        if lat:
            for zc in range(4):
                p = nxt()
                c0 = 1088 + zc * 512
                for k in range(8):
                    ph.mm(p[:], hT[:, k, :], w_in[:, k, c0:c0 + 512], k == 0, k == 7, [hT, w_in], [p])
                ph.act(szt[:, zc * 512:(zc + 1) * 512], p[:], AF.Silu, [p], [szt])
            ph.dma("sp", t.msz_d[i * 128:(i + 1) * 128, :], szt[:], R=[szt])
            ph.act(G.junk[:, 0:768], cqkv[:, 0:768], AF.Square, [cqkv], [G.junk, st], scale=float(768 ** -0.5), accum_out=st[:, 4:5])
            ph.rstd(st[:, 5:6], st[:, 4:5], st[:, 6:7], [st], [st])
            ph.stt("dve", cn[:, 0:768], cqkv[:, 0:768], st[:, 5:6], qnb[:], ALU.mult, ALU.mult, [cqkv, st, qnb], [cn])
        ph.act(G.junk[:, 0:256], cqkv[:, 768:1024], AF.Square, [cqkv], [G.junk, st], scale=1.0 / 16.0, accum_out=st[:, 7:8])
        ph.rstd(st[:, 8:9], st[:, 7:8], st[:, 9:10], [st], [st])
        ph.stt("dve", cn[:, 768:1024], cqkv[:, 768:1024], st[:, 8:9], kvnb[:], ALU.mult, ALU.mult, [cqkv, st, kvnb], [cn])
        ph.act(G.junk[:, 0:64], cqkv[:, 1024:1088], AF.Square, [cqkv], [G.junk, st], accum_out=st[:, 10:11])
        k0 = 0 if lat else 6
        for k in range(k0, 8):
            ph.tr(pT[1][:, k, :], cn[:, k * 128:(k + 1) * 128], G.identb[:], [cn, G.identb], [pT[1]])
        ph.copy("act", cT[:, k0:8, :], pT[1][:, k0:8, :], [pT[1]], [cT])
        if lat:
            for n in range(6):
                p = nxt()
                for k in range(6):
                    ph.mm(p[:], cT[:, k, :], w_uq[:, k, n * 512:(n + 1) * 512], k == 0, k == 5, [cT, w_uq], [p])
                ph.copy("act", q[:, n * 512:(n + 1) * 512], p[:], [p], [q])
            q3 = q[:].rearrange("p (h d) -> p h d", d=192)
            t3 = tmp[:].rearrange("p (h d) -> p h d", d=192)
            ph.tt("dve", tmp[:], q[:], q[:], ALU.mult, [q], [tmp])
            ph.red("dve", st[:, 16:32], t3, ALU.add, [tmp], [st])
            ph.ts("dve", st[:, 16:32], st[:, 16:32], 1.0 / 192, EPS, ALU.mult, ALU.add, [st], [st])
            ph.act(st[:, 16:32], st[:, 16:32], AF.Sqrt, [st], [st])
            ph.recip(st[:, 32:48], st[:, 16:32], [st], [st])
            ph.tt("dve", q3, q3, st[:, 32:48].unsqueeze(2).broadcast_to([128, 16, 192]), ALU.mult, [q, st], [q])
            ph.tt("dve", q3, q3, gq[:].unsqueeze(1).broadcast_to([128, 16, 192]), ALU.mult, [q, gq], [q])
            cb = csb[:, 0:32].unsqueeze(1).broadcast_to([128, 16, 32])
            sb_ = csb[:, 32:64].unsqueeze(1).broadcast_to([128, 16, 32])
            x1 = q3[:, :, 128:160]; x2 = q3[:, :, 160:192]
            ph.tt("dve", r4[:, 0], x1, cb, ALU.mult, [q, csb], [r4])
            ph.tt("dve", r4[:, 1], x2, sb_, ALU.mult, [q, csb], [r4])
            ph.tt("dve", r4[:, 2], x1, sb_, ALU.mult, [q, csb], [r4])
            ph.tt("dve", r4[:, 3], x2, cb, ALU.mult, [q, csb], [r4])
            ph.tt("dve", Qt[:, :, 128:160], r4[:, 0], r4[:, 1], ALU.subtract, [r4], [Qt])
            ph.tt("dve", Qt[:, :, 160:192], r4[:, 2], r4[:, 3], ALU.add, [r4], [Qt])
            ph.copy("act", Qt[:, :, 0:128], q3[:, :, 0:128], [q], [Qt])
            ph.copy("act", Qt[:, :, 192:256], q3[:, :, 128:192], [q], [Qt])
            Qf = Qt[:].rearrange("p h d -> p (h d)")
            for j in range(4):
                pt = pT[j % 2]
                for k in range(8):
                    c = j * 8 + k
                    ph.tr(pt[:, k, :], Qf[:, c * 128:(c + 1) * 128], G.identb[:], [Qt, G.identb], [pt])
                ph.copy(("act", "dve")[j % 2], QTt[:, j * 8:(j + 1) * 8, :], pt[:], [pt], [QTt])
            ph.dma("sp", t.QT_d[:, :, :, i * 128:(i + 1) * 128].rearrange("h e d t -> d (h e) t"), QTt[:], R=[QTt])
        for h4 in range(4):
            p = pk[h4 % 2]
            for hh in range(4):
                h = 4 * h4 + hh
                for kc in range(2):
                    ph.mm(p[:, hh, :], w_uk[:, kc, h * 128:(h + 1) * 128], cT[:, 6 + kc, :], kc == 0, kc == 1, [w_uk, cT], [p])
            ph.copy("act", KnT[:, 4 * h4:4 * h4 + 4, :], p[:], [p], [KnT])
            ph.act(sq[:, 4 * h4:4 * h4 + 4, :], p[:], AF.Square, [p], [sq])
        for h in range(16):
            ph.mm(pss[:, h:h + 1], sq[:, h, :], ones[:], True, True, [sq, ones], [pss])
        ph.ts("dve", rk[:], pss[:], st[:, 10:11], 1.0 / 192, ALU.add, ALU.mult, [pss, st], [rk])
        ph.ts("dve", rk[:], rk[:], EPS, None, ALU.add, None, [rk], [rk])
        ph.act(rk[:], rk[:], AF.Sqrt, [rk], [rk])
        ph.recip(st[:, 48:64], rk[:], [rk], [st])
        ph.ts("dve", rk[:], st[:, 48:64], MLA_SCALE_, None, ALU.mult, None, [st], [rk])
        ph.dma("sp", t.rk_d[rows, :], rk[:], R=[rk])
        ph.dma("sp", t.KnT_d[:, :, rows].rearrange("h d t -> d h t"), KnT[:], R=[KnT])
        for n in range(4):
            p = nxt()
            for kc in range(2):
                ph.mm(p[:], cT[:, 6 + kc, :], w_uv[:, kc, n * 512:(n + 1) * 512], kc == 0, kc == 1, [cT, w_uv], [p])
            ph.copy("act", Vt[:, n * 512:(n + 1) * 512], p[:], [p], [Vt])
        ph.dma("sp", t.V_d[rows, :], Vt[:], R=[Vt])
        ph.tt("dve", krg[:], cqkv[:, 1024:1088], gk[:, 128:192], ALU.mult, [cqkv, gk], [krg])
        if lat:
            ph.tt("dve", r4[:, 0, 0, :], krg[:, 0:32], csb[:, 0:32], ALU.mult, [krg, csb], [r4])
            ph.tt("dve", r4[:, 0, 1, :], krg[:, 32:64], csb[:, 32:64], ALU.mult, [krg, csb], [r4])
            ph.tt("dve", r4[:, 0, 2, :], krg[:, 0:32], csb[:, 32:64], ALU.mult, [krg, csb], [r4])
            ph.tt("dve", r4[:, 0, 3, :], krg[:, 32:64], csb[:, 0:32], ALU.mult, [krg, csb], [r4])
            ph.tt("dve", KRt[:, 0:32], r4[:, 0, 0, :], r4[:, 0, 1, :], ALU.subtract, [r4], [KRt])
            ph.tt("dve", KRt[:, 32:64], r4[:, 0, 2, :], r4[:, 0, 3, :], ALU.add, [r4], [KRt])
        else:
            ph.copy("dve", KRt[:], krg[:], [krg], [KRt])
        ph.tr(pT[0][0:64, 0, :], KRt[:], G.identb[:], [KRt, G.identb], [pT[0]])
        ph.copy("act", KRT[:], pT[0][0:64, 0, :], [pT[0]], [KRT])
        ph.dma("sp", t.KrT_d[:, rows], KRT[:], R=[KRT])
    ph.run()
    es_w.close()


def phase_mla_attn(prog, t, G, L):
    ph = Phase(prog, "mla")
    NTOK = LC + L
    NKT = NTOK // 128
    nct = LC // 128
    NQG = L // 512
    KrT = ph.sb([128, NTOK], BF16, "KrT")
    rk = ph.sb([128, NKT, 16], F32, "rk")
    KnT = [ph.sb([128, NTOK], BF16, "KnT") for _ in range(2)]
    Vx = [ph.sb([128, NKT, 129], BF16, "Vx") for _ in range(2)]
    Qn = [ph.sb([128, 512], BF16, "Qn") for _ in range(2)]
    Qr = [ph.sb([128, 512], BF16, "Qr") for _ in range(2)]
    pTs = [ph.sb([128, 512], BF16, "pTs") for _ in range(3)]
    ot = [ph.sb([128, 4, 128], F32, "ot") for _ in range(2)]
    rden = ph.sb([128, 8], F32, "rden")
    ps = [ph.ps([128, 512], F32, "ps") for _ in range(3)]
    po = [ph.ps([128, 512], F32, "po") for _ in range(4)]
    ph.dma("sp", KrT[0:64, :], t.KrT_d[:, :], W=[KrT])
    ph.dma("sp", KrT[64:128, :], t.KrT_d[:, :], W=[KrT])
    ph.dma("sp", rk[:], t.rk_d.rearrange("(kt p) h -> p kt h", p=128), W=[rk])
    for b in range(2):
        ph.memset("dve", Vx[b][:, :, 128:129], 1.0, [Vx[b]])
    it = 0
    for h in range(16):
        kn = KnT[h % 2]; vx = Vx[h % 2]
        ph.dma("sp", kn[:], t.KnT_d[h, :, :], W=[kn])
        ph.dma("sp", vx[:, :, 0:128], t.V_d[:, h * 128:(h + 1) * 128].rearrange("(kt p) d -> p kt d", p=128), W=[vx])
        for qg in range(NQG):
            qn = Qn[it % 2]; qr = Qr[it % 2]; o = ot[it % 2]
            it += 1
            ph.dma("sp", qn[:], t.QT_d[h, 0, :, qg * 512:(qg + 1) * 512], W=[qn])
            ph.dma("sp", qr[:], t.QT_d[h, 1, :, qg * 512:(qg + 1) * 512], W=[qr])

            def qk(kt):
                p = ps[kt % 3]
                ph.mm(p[:], kn[:, kt * 128:(kt + 1) * 128], qn[:], True, False, [kn, qn], [p])
                b0 = 64 if kt < nct else 0
                ph.mm(p[:], KrT[b0:b0 + 64, kt * 128:(kt + 1) * 128], qr[b0:b0 + 64, :], False, True, [KrT, qr], [p])
            qk(0)
            for kt in range(NKT):
                if kt + 1 < NKT:
                    qk(kt + 1)
                p = ps[kt % 3]; pt = pTs[kt % 3]
                ph.act(pt[:], p[:], AF.Exp, [p, rk], [pt], scale=rk[:, kt, h:h + 1])
                for qs in range(4):
                    ph.mm(po[qs][:, 0:129], pt[:, qs * 128:(qs + 1) * 128], vx[:, kt, :], kt == 0, kt == NKT - 1, [pt, vx], [po[qs]])
            for qs in range(4):
                ph.recip(rden[:, qs:qs + 1], po[qs][:, 128:129], [po[qs]], [rden])
                ph.ts("dve", o[:, qs, :], po[qs][:, 0:128], rden[:, qs:qs + 1], None, ALU.mult, None, [po[qs], rden], [o])
            ph.dma("sp", t.o_d[qg * 512:(qg + 1) * 512, h * 128:(h + 1) * 128].rearrange("(a p) d -> p a d", p=128), o[:], R=[o])
    ph.run()


def phase_mla_out(prog, t, G, li, L):
    es_w = ExitStack()
    w_out = prog.sb(es_w, [128, 16, D], BF16, "mlo_w")
    gate = prog.sb(es_w, [128, D], F32, "mlo_gate")
    ph = Phase(prog, "mlow")
    stg = [ph.sb([128, 4, D], F32, "stg") for _ in range(2)]
    for n in range(4):
        s = stg[n % 2]
        ph.dma("sp", s[:], t.ml_w_out[n * 512:(n + 1) * 512, :].rearrange("(a p) d -> p a d", p=128), W=[s])
        ph.copy(("act", "dve")[n % 2], w_out[:, 4 * n:4 * n + 4, :], s[:], [s], [w_out])
    ph.dma("sp", gate[:], t.gate_d[li, 0, :, :], W=[gate])
    ph.run()
    ph = Phase(prog, "mlo")
    ots = [ph.sb([128, 2048], F32, "o") for _ in range(2)]
    szs = [ph.sb([128, 2048], BF16, "sz") for _ in range(2)]
    xts = [ph.sb([128, D], F32, "xt") for _ in range(2)]
    ob = ph.sb([128, 2048], BF16, "ob")
    oT = ph.sb([128, 16, 128], BF16, "oT")
    tmp = ph.sb([128, D], F32, "tmp")
    pT = [ph.ps([128, 8, 128], BF16, "pT") for _ in range(2)]
    po = [ph.ps([128, 512], F32, "po") for _ in range(2)]
    for i in range(L // 128):
        rows = slice(i * 128, (i + 1) * 128)
        o = ots[i % 2]; sz = szs[i % 2]; xt = xts[i % 2]
        ph.dma("sp", o[:], t.o_d[rows, :], W=[o])
        ph.dma("sp", sz[:], t.msz_d[rows, :], W=[sz])
        ph.dma("sp", xt[:], t.out[rows, :], W=[xt])
        ph.tt("dve", ob[:], o[:], sz[:], ALU.mult, [o, sz], [ob])
        for k in range(16):
            ph.tr(pT[k // 8][:, k % 8, :], ob[:, k * 128:(k + 1) * 128], G.identb[:], [ob, G.identb], [pT[k // 8]])
        for j in range(2):
            ph.copy("act", oT[:, 8 * j:8 * j + 8, :], pT[j][:], [pT[j]], [oT])
        for half in range(2):
            for k in range(16):
                ph.mm(po[half][:], oT[:, k, :], w_out[:, k, half * 512:(half + 1) * 512], k == 0, k == 15, [oT, w_out], [po[half]])
        for half in range(2):
            sl = slice(half * 512, (half + 1) * 512)
            ph.tt("dve", tmp[:, sl], po[half][:], gate[:, sl], ALU.mult, [po[half], gate], [tmp])
            ph.tt("dve", xt[:, sl], tmp[:, sl], xt[:, sl], ALU.add, [tmp, xt], [xt])
        ph.dma("sp", t.out[rows, :], xt[:], R=[xt])
    ph.run()
    es_w.close()


def build(L=L_FULL, nlayers=4, dbg=False):
    nc = bass.Bass("TRN2", target_bir_lowering=False)
    t = declare(nc, L)
    declare_rwkv(nc, t, L)
    declare_mla(nc, t, L)
    if dbg:
        t.dbg_ctx = nc.dram_tensor("dbg_ctx", [LC, D], F32, kind="ExternalOutput").ap()
        t.dbg_y = nc.dram_tensor("dbg_y", [2, LC + L, D], F32, kind="ExternalOutput").ap()
    NT = L // 128
    with ExitStack() as es:
        prog = Prog(nc, es)
        G = T()
        G.mod = prog.sb(es, [128, 4, 24, 2], F32, "mod")
        G.identb = prog.sb(es, [128, 128], BF16, "identb")
        G.identf = prog.sb(es, [128, 128], F32, "identf")
        G.junk = prog.sb(es, [128, D], BF16, "junk")
        G.xn = prog.sb(es, [128, D], BF16, "xn")
        ph = Phase(prog, "init")
        ph.dma("sp", G.identf[:], t.ident[:, :], W=[G.identf])
        ph.copy("dve", G.identb[:], G.identf[:], [G.identf], [G.identb])
        ph.run()
        phase_ada(prog, t, G)
        ctx_tiles = [(t.ctx[i * 128:(i + 1) * 128, :], t.ctxs[i * 128:(i + 1) * 128, :], 1) for i in range(LC // 128)]
        x_tiles0 = [(t.x[i * 128:(i + 1) * 128, :], t.out[i * 128:(i + 1) * 128, :], 0) for i in range(NT)]
        phase_sgu(prog, t, G, 0, 0, ctx_tiles + x_tiles0)
        import os
        STOP = int(os.environ.get("STOP", "9"))
        if nlayers >= 2:
            phase_rwkv_h(prog, t, G, 1, L)
            if STOP >= 2:
                phase_rwkv_feat(prog, t, G, L)
            if STOP >= 3:
                phase_rwkv_scan(prog, t, G, L)
            if STOP >= 4:
                phase_rwkv_out(prog, t, G, 1, L)
        if nlayers >= 3:
            phase_mla_proj(prog, t, G, 2, L)
            if STOP >= 6:
                phase_mla_attn(prog, t, G, L)
            if STOP >= 7:
                phase_mla_out(prog, t, G, 2, L)
        if nlayers >= 4:
            phase_sgu(prog, t, G, 3, 1, [(t.out[i * 128:(i + 1) * 128, :], t.out[i * 128:(i + 1) * 128, :], 0) for i in range(NT)])
        if dbg:
            ph = Phase(prog, "dbg")
            d1 = Buf(None, "d1"); d2 = Buf(None, "d2")
            ph.dma("sp", t.dbg_ctx[:, :], t.ctxs[:, :], R=[d1])
            if nlayers >= 2:
                for n in range(2):
                    ph.dma("sp", t.dbg_y[n, :, :], t.y_d[n, :, :], R=[d2])
            ph.run()
        build.n_instr = prog.n_instr
    return nc


def host_inputs(inp, b, L=L_FULL):
    f = lambda a: np.ascontiguousarray(a, dtype=np.float32)
    cond = np.stack([inp["c"][b], inp["c_ctx"]], 0)
    m = {}
    m["x"] = f(inp["x"][b][:L])
    m["ctx"] = f(inp["ctx"][b])
    m["condT"] = f(cond.reshape(2, 8, 128).transpose(2, 1, 0))
    m["ada_w"] = f(inp["ada_w"])
    m["ada_b2"] = f(np.broadcast_to(inp["ada_b"][:, None, :], (4, 2, 3 * D)))
    m["ident"] = np.eye(128, dtype=np.float32)
    sel = np.zeros((2, 2, 128), np.float32); sel[0, 0] = 1; sel[1, 1] = 1
    m["sel"] = sel
    m["sgu_w_in"] = f(inp["sgu_w_in"]); m["sgu_w_out"] = f(inp["sgu_w_out"])
    m["sgu_w_sT"] = f(inp["sgu_w_s"].transpose(0, 3, 1, 2))
    m["sgu_b_sT"] = f(inp["sgu_b_s"].transpose(0, 2, 1))
    m["sgu_gain"] = f(inp["sgu_gain"])
    m.update(rwkv_consts())
    m["rw_w_in"] = f(inp["rwkv_w_in"][0])
    m["rw_w1cat"] = f(np.concatenate([inp["rwkv_w_lora1"][0, 0], inp["rwkv_w_lora1"][0, 1]], 1))
    m["rw_a1cat"] = f(np.concatenate([inp["rwkv_a_lora1"][0, 0], inp["rwkv_a_lora1"][0, 1]], 1))
    m["rw_w2cat"] = f(inp["rwkv_w_lora2"][0].reshape(128, D))
    m["rw_a2cat"] = f(inp["rwkv_a_lora2"][0].reshape(128, D))
    m["rw_muT"] = f(inp["rwkv_mu"][0].reshape(6, 8, 128).transpose(2, 0, 1))
    m["rw_w0"] = f(inp["rwkv_w0"][0]); m["rw_a0"] = f(inp["rwkv_a0"][0])
    m["rw_k_k"] = f(inp["rwkv_k_k"][0]); m["rw_k_a"] = f(inp["rwkv_k_a"][0]); m["rw_r_k"] = f(inp["rwkv_r_k"][0].reshape(D))
    m["rw_ln_g"] = f(inp["rwkv_ln_gain"][0]); m["rw_ln_b"] = f(inp["rwkv_ln_bias"][0]); m["rw_w_out"] = f(inp["rwkv_w_out"][0])
    m["ml_w_in"] = f(inp["mla_w_in"][0]); m["ml_w_uq"] = f(inp["mla_w_uq"][0])
    ukv = inp["mla_w_ukv"][0].reshape(256, 16, 2, 128)
    m["ml_w_uk"] = f(ukv[:, :, 0, :].reshape(256, 2048)); m["ml_w_uv"] = f(ukv[:, :, 1, :].reshape(256, 2048))
    m["ml_qn"] = f(inp["mla_q_norm"][0]); m["ml_kvn"] = f(inp["mla_kv_norm"][0])
    m["ml_gq"] = f(inp["mla_qk_gain_q"][0]); m["ml_gk"] = f(inp["mla_qk_gain_k"][0]); m["ml_w_out"] = f(inp["mla_w_out"][0])
    pos = np.arange(L)
    inv = (10000.0 ** (-np.arange(16, dtype=np.float32) / 16)).astype(np.float32)
    ang = np.concatenate([(pos // 64).astype(np.float32)[:, None] * inv, (pos % 64).astype(np.float32)[:, None] * inv], -1).astype(np.float32)
    m["rope_cs"] = f(np.concatenate([np.cos(ang), np.sin(ang)], -1))
    return m


def kernel(**inputs):
    inp = {k: np.asarray(v) for k, v in inputs.items()}
    nc = build()
    in_maps = [host_inputs(inp, b) for b in range(8)]
    res = run_bass_kernel_spmd(nc, in_maps, core_ids=list(range(8)))
    return np.stack([r["out"] for r in res.results], 0).astype(np.float32)
```

```python
import bisect
from contextlib import ExitStack
import numpy as np
import concourse.bass as bass
import concourse.mybir as mybir
from concourse.bass_utils import run_bass_kernel_spmd

F32 = mybir.dt.float32
BF16 = mybir.dt.bfloat16
ALU = mybir.AluOpType
AF = mybir.ActivationFunctionType
AX = mybir.AxisListType

D = 1024
L_FULL = 8192
LC = 256
EPS = 1e-6


class Sem:
    def __init__(self, h):
        self.h = h
        self.count = 0


class Buf:
    def __init__(self, ap, name=""):
        self.ap = ap
        self.name = name
        self.w_evs = []
        self.r_evs = []
        self.dsem = None
        self.dsem_ph = None
        self.is_psum = False

    def __getitem__(self, k):
        return self.ap[k]


class Op:
    __slots__ = ("fn", "waits", "inc", "dma")

    def __init__(self, fn):
        self.fn = fn
        self.waits = []
        self.inc = False
        self.dma = None


class EngRec:
    def __init__(self, name, sem):
        self.name = name
        self.sem = sem
        self.ops = []
        self.mark_idx = []
        self.mark_tick = []
        self.seen = {}

    def ensure_tick(self, idx):
        i = bisect.bisect_left(self.mark_idx, idx)
        if i < len(self.mark_idx):
            return self.mark_tick[i]
        self.ops[idx].inc = True
        self.sem.count += 1
        self.mark_idx.append(idx)
        self.mark_tick.append(self.sem.count)
        return self.sem.count


class Prog:
    def __init__(self, nc, es, n_dma_sems=56):
        self.nc = nc
        self.es = es
        self.esem = {e: Sem(es.enter_context(nc.semaphore("s_" + e))) for e in ("pe", "act", "dve", "pool")}
        self.dsems = [Sem(es.enter_context(nc.semaphore("d%d" % i))) for i in range(n_dma_sems)]
        self.n_instr = 0
        self.nalloc = 0

    def sb(self, es, shape, dtype, name="t"):
        self.nalloc += 1
        nm = "%s%d" % (name, self.nalloc)
        t = es.enter_context(self.nc.sbuf_tensor(nm, list(shape), dtype))
        return Buf(t[tuple(slice(None) for _ in shape)], nm)

    def ps(self, es, shape, dtype, name="p"):
        self.nalloc += 1
        nm = "%s%d" % (name, self.nalloc)
        t = es.enter_context(self.nc.psum_tensor(nm, list(shape), dtype))
        bf = Buf(t[tuple(slice(None) for _ in shape)], nm)
        bf.is_psum = True
        return bf


class Phase:
    def __init__(self, prog, name):
        self.p = prog
        self.nc = prog.nc
        self.name = name
        self.es = ExitStack()
        self.eng = {e: EngRec(e, prog.esem.get(e)) for e in ("pe", "act", "dve", "pool", "sp")}
        self.free_dsems = list(prog.dsems)
        self.rr = 0

    def sb(self, shape, dtype, name="t"):
        return self.p.sb(self.es, shape, dtype, self.name + "_" + name)

    def ps(self, shape, dtype, name="p"):
        return self.p.ps(self.es, shape, dtype, self.name + "_" + name)

    def view(self, buf, key, name=""):
        return Buf(buf.ap[key], name or buf.name + "_v")

    def _wait_for(self, er, op, ev):
        if ev[-1] is not self:
            return
        if ev[0] == "c":
            fe, idx = ev[1], ev[2]
            if fe is er and er.name == "pe":
                return
            val = fe.ensure_tick(idx)
            sem = fe.sem
        else:
            sem, val = ev[1], ev[2]
        key = id(sem)
        if er.seen.get(key, 0) >= val:
            return
        er.seen[key] = val
        op.waits.append((sem.h, val))

    def _deps(self, er, op, ev, R, W):
        for b in R:
            for e in b.w_evs:
                self._wait_for(er, op, e)
            if b.is_psum:
                for e in b.r_evs:
                    if e[0] == "c" and e[1] is not er:
                        self._wait_for(er, op, e)
        for b in W:
            for e in b.w_evs:
                self._wait_for(er, op, e)
            for e in b.r_evs:
                self._wait_for(er, op, e)
        for b in R:
            if any(b is w for w in W):
                continue
            if ev[0] == "c":
                b.r_evs = [e for e in b.r_evs if not (e[0] == "c" and e[1] is ev[1]) and e[-1] is self]
            b.r_evs.append(ev)
        for b in W:
            b.w_evs = [ev]
            b.r_evs = []

    def op(self, eng, fn, R=(), W=()):
        er = self.eng[eng]
        o = Op(fn)
        er.ops.append(o)
        ev = ("c", er, len(er.ops) - 1, self)
        self._deps(er, o, ev, R, W)
        return o

    def dma(self, q, out, in_, R=(), W=(), sbuf=None, **kw):
        q = "sp"
        er = self.eng[q]
        b = sbuf or (W[0] if W else R[0])
        if b.dsem is None or b.dsem_ph is not self:
            b.dsem = self.free_dsems.pop()
            b.dsem_ph = self
        sem = b.dsem
        sem.count += 16
        o = Op(lambda e: e.dma_start(out=out, in_=in_, **kw))
        o.dma = sem.h
        er.ops.append(o)
        ev = ("d", sem, sem.count, self)
        self._deps(er, o, ev, R, W)
        return o

    def dmaq(self):
        self.rr += 1
        return ("sp", "pool")[self.rr % 2]

    def act(self, out, in_, func, R, W, **kw):
        return self.op("act", lambda e: e.activation(out=out, in_=in_, func=func, **kw), R, W)

    def mm(self, out, lhsT, rhs, start, stop, R, W):
        return self.op("pe", lambda e: e.matmul(out, lhsT=lhsT, rhs=rhs, start=start, stop=stop), R, W)

    def tr(self, out, in_, ident, R, W):
        return self.op("pe", lambda e: e.transpose(out=out, in_=in_, identity=ident), R, W)

    def tt(self, eng, out, in0, in1, op, R, W):
        return self.op(eng, lambda e: e.tensor_tensor(out=out, in0=in0, in1=in1, op=op), R, W)

    def ts(self, eng, out, in0, s1, s2, op0, op1, R, W, **kw):
        if op1 is None:
            return self.op(eng, lambda e: e.tensor_scalar(out=out, in0=in0, scalar1=s1, scalar2=None, op0=op0, **kw), R, W)
        return self.op(eng, lambda e: e.tensor_scalar(out=out, in0=in0, scalar1=s1, scalar2=s2, op0=op0, op1=op1, **kw), R, W)

    def stt(self, eng, out, in0, scalar, in1, op0, op1, R, W):
        return self.op(eng, lambda e: e.scalar_tensor_tensor(out=out, in0=in0, scalar=scalar, in1=in1, op0=op0, op1=op1), R, W)

    def copy(self, eng, out, in_, R, W):
        if eng == "act":
            return self.op("act", lambda e: e.activation(out=out, in_=in_, func=AF.Copy), R, W)
        return self.op(eng, lambda e: e.tensor_copy(out=out, in_=in_), R, W)

    def red(self, eng, out, in_, op, R, W, axis=AX.X):
        return self.op(eng, lambda e: e.tensor_reduce(out=out, in_=in_, axis=axis, op=op), R, W)

    def memset(self, eng, ap, val, W):
        return self.op(eng, lambda e: e.memset(ap, val), (), W)

    def recip(self, out, in_, R, W):
        return self.op("dve", lambda e: e.reciprocal(out=out, in_=in_), R, W)

    def rstd(self, out, mean, tmp, R, W):
        self.ts("dve", tmp, mean, EPS, None, ALU.add, None, R, W)
        self.act(tmp, tmp, AF.Sqrt, W, W)
        self.recip(out, tmp, W, W)

    def run(self):
        used = [s for s in self.p.dsems if s not in self.free_dsems]
        er = self.eng["sp"]
        o = Op(None)
        for s in used:
            self._wait_for(er, o, ("d", s, s.count, self))
        er.ops.append(o)
        with self.nc.Block() as block:
            def mk(e):
                er = self.eng[e]
                sem_h = er.sem.h if er.sem is not None else None

                def body(engine):
                    for o in er.ops:
                        for (sh, val) in o.waits:
                            engine.wait_ge(sh, val)
                        if o.fn is None:
                            continue
                        ins = o.fn(engine)
                        if o.dma is not None:
                            ins.then_inc(o.dma, 16)
                        elif o.inc:
                            ins.then_inc(sem_h, 1)
                return body
            block.tensor(mk("pe"))
            block.scalar(mk("act"))
            block.vector(mk("dve"))
            block.gpsimd(mk("pool"))
            block.sync(mk("sp"))
        for e in self.eng.values():
            self.p.n_instr += len(e.ops)
        self.es.close()


class T:
    pass


def declare(nc, L):
    t = T()

    def din(name, shape):
        setattr(t, name, nc.dram_tensor(name, list(shape), F32, kind="ExternalInput").ap())

    def scr(name, shape, dt=F32):
        setattr(t, name, nc.dram_tensor(name, list(shape), dt, kind="Internal").ap())

    din("x", [L, D]); din("ctx", [LC, D]); din("condT", [128, 8, 2])
    din("ada_w", [4, D, 3 * D]); din("ada_b2", [4, 2, 3 * D])
    din("ident", [128, 128]); din("sel", [2, 2, 128])
    din("sgu_w_in", [2, D, 6144]); din("sgu_w_out", [2, 2048, D]); din("sgu_w_sT", [2, 128, 8, 128])
    din("sgu_b_sT", [2, 128, 8]); din("sgu_gain", [2, 2048])
    t.out = nc.dram_tensor("out", [L, D], F32, kind="ExternalOutput").ap()
    scr("ctxs", [LC, D])
    scr("gate_d", [4, 2, 128, D])
    return t


def phase_ada(prog, t, G):
    ph = Phase(prog, "ada")
    scond = ph.sb([128, 8, 2], F32)
    ph.dma("sp", scond[:], t.condT[:, :, :], W=[scond])
    ph.act(scond[:], scond[:], AF.Silu, [scond], [scond])
    identf = ph.sb([128, 128], F32)
    ph.dma("pool", identf[:], t.ident[:, :], W=[identf])
    sel = ph.sb([2, 2, 128], F32)
    ph.dma("pool", sel[:], t.sel[:, :, :], W=[sel])
    wk = [ph.sb([128, 3 * D], F32, "wk") for _ in range(8)]
    b2 = ph.sb([2, 3 * D], F32)
    mrow = ph.sb([2, 3 * D], F32)
    gsb = [ph.sb([128, D], F32, "gsb") for _ in range(2)]
    pm = [ph.ps([2, 512], F32, "pm") for _ in range(6)]
    pT = ph.ps([128, 16, 2], F32, "pT")
    pg = ph.ps([128, 512], F32, "pg")
    mod = G.mod
    for li in range(4):
        for k in range(8):
            ph.dma(ph.dmaq(), wk[k][:], t.ada_w[li, k * 128:(k + 1) * 128, :], W=[wk[k]])
        ph.dma("sp", b2[:], t.ada_b2[li, :, :], W=[b2])
        for k in range(8):
            for n in range(6):
                ph.mm(pm[n][:], scond[:, k, :], wk[k][:, n * 512:(n + 1) * 512], k == 0, k == 7, [scond, wk[k]], [pm[n]])
        for n in range(6):
            ph.tt("dve", mrow[:, n * 512:(n + 1) * 512], pm[n][:], b2[:, n * 512:(n + 1) * 512], ALU.add, [pm[n], b2], [mrow])
        for j in range(16):
            ph.tr(pT[:, j, :], mrow[:, j * 128:(j + 1) * 128], identf[0:2, 0:2], [mrow, identf], [pT])
        ph.copy("dve", mod[:, li, 0:16, :], pT[:], [pT], [mod])
        ph.ts("dve", mod[:, li, 8:16, :], mod[:, li, 8:16, :], 1.0, None, ALU.add, None, [mod], [mod])
        for r in range(2):
            for half in range(2):
                ph.mm(pg[:], sel[:, r, :], mrow[:, 2048 + half * 512:2048 + (half + 1) * 512], True, True, [sel, mrow], [pg])
                ph.copy("act", gsb[r][:, half * 512:(half + 1) * 512], pg[:], [pg], [gsb[r]])
            ph.dma("pool", t.gate_d[li, r, :, :], gsb[r][:], R=[gsb[r]])
    ph.run()


def rms_to_hT(ph, G, xt, hT, pTb, li, r, sq, st, si):
    xn = G.xn
    ph.act(G.junk[:], xt[:], AF.Square, [xt], [G.junk, st], scale=1.0 / 32.0, accum_out=st[:, si:si + 1])
    ph.rstd(st[:, si + 1:si + 2], st[:, si:si + 1], st[:, si + 2:si + 3], [st], [st])
    ph.act(xn[:], xt[:], AF.Copy, [xt, st], [xn], scale=st[:, si + 1:si + 2])
    for k in range(8):
        ph.tr(pTb[:, k, :], xn[:, k * 128:(k + 1) * 128], G.identb[:], [xn, G.identb], [pTb])
    for k in range(8):
        ph.ts("dve", hT[:, k, :], pTb[:, k, :], G.mod[:, li, 8 + k, r:r + 1], G.mod[:, li, k, r:r + 1], ALU.mult, ALU.add,
              [pTb, G.mod], [hT])


def phase_sgu(prog, t, G, li, j, tiles):
    es_w = ExitStack()
    w_in = prog.sb(es_w, [128, 8, 6144], BF16, "sgu_win")
    w_out = prog.sb(es_w, [128, 16, D], BF16, "sgu_wout")
    w_sT = prog.sb(es_w, [128, 8, 128], BF16, "sgu_ws")
    b_s = prog.sb(es_w, [128, 8], F32, "sgu_bs")
    gain = prog.sb(es_w, [128, 2048], F32, "sgu_gain")
    gate = [prog.sb(es_w, [128, D], F32, "sgu_gate") for _ in range(2)]
    ph = Phase(prog, "sguw%d" % li)
    stg = [ph.sb([128, 3072], F32, "stg") for _ in range(3)]
    engs = ("act", "dve", "pool")
    n = 0
    for k in range(8):
        for half in range(2):
            s = stg[n % 3]
            ph.dma(ph.dmaq(), s[:], t.sgu_w_in[j, k * 128:(k + 1) * 128, half * 3072:(half + 1) * 3072], W=[s])
            ph.copy(engs[n % 3], w_in[:, k, half * 3072:(half + 1) * 3072], s[:], [s], [w_in])
            n += 1
    for k3 in range(0, 16, 3):
        kk = min(3, 16 - k3)
        s = stg[n % 3]
        ph.dma(ph.dmaq(), s[:, 0:kk * D].rearrange("p (a d) -> p a d", d=D),
               t.sgu_w_out[j, k3 * 128:(k3 + kk) * 128, :].rearrange("(a p) d -> p a d", p=128), W=[s])
        ph.copy(engs[n % 3], w_out[:, k3:k3 + kk, :], s[:, 0:kk * D].rearrange("p (a d) -> p a d", d=D), [s], [w_out])
        n += 1
    s = stg[n % 3]
    ph.dma("sp", s[:, 0:1024].rearrange("p (g q) -> p g q", q=128), t.sgu_w_sT[j, :, :, :], W=[s])
    ph.copy("dve", w_sT[:], s[:, 0:1024].rearrange("p (g q) -> p g q", q=128), [s], [w_sT])
    ph.dma("sp", b_s[:], t.sgu_b_sT[j, :, :], W=[b_s])
    ph.dma("pool", gain[:], t.sgu_gain[j, :].partition_broadcast(128), W=[gain])
    for r in range(2):
        ph.dma("sp", gate[r][:], t.gate_d[li, r, :, :], W=[gate[r]])
    ph.run()
    ph = Phase(prog, "sgu%d" % li)
    xts = [ph.sb([128, D], F32, "xt") for _ in range(2)]
    hT = ph.sb([128, 8, 128], BF16, "hT")
    gu = ph.sb([128, 2048], BF16, "gu")
    gv = ph.sb([128, 2048], F32, "gv")
    sz = ph.sb([128, 2048], BF16, "sz")
    vn = ph.sb([128, 2048], BF16, "vn")
    sT = ph.sb([128, 16, 128], BF16, "sT")
    st = ph.sb([128, 16], F32, "st")
    pTa = ph.ps([128, 8, 128], BF16, "pTa")
    pTb = ph.ps([128, 8, 128], BF16, "pTb")
    pmm = [ph.ps([128, 512], F32, "pmm") for _ in range(2)]
    pmx = [ph.ps([128, 512], F32, "pmx") for _ in range(2)]
    po = [ph.ps([128, 512], F32, "po") for _ in range(2)]
    def load(ti):
        ph.dma("sp", xts[ti % 2][:], tiles[ti][0], W=[xts[ti % 2]])
    load(0)
    for ti, (src, dst, r) in enumerate(tiles):
        xt = xts[ti % 2]
        if ti + 1 < len(tiles):
            load(ti + 1)
        rms_to_hT(ph, G, xt, hT, pTa, li, r, None, st, 0)
        for n in range(12):
            pm = pmm[n % 2]
            for k in range(8):
                ph.mm(pm[:], hT[:, k, :], w_in[:, k, n * 512:(n + 1) * 512], k == 0, k == 7, [hT, w_in], [pm])
            if n < 4:
                ph.act(gu[:, n * 512:(n + 1) * 512], pm[:], AF.Gelu_apprx_tanh, [pm], [gu])
            elif n < 8:
                c = n - 4
                ph.act(gv[:, c * 512:(c + 1) * 512], pm[:], AF.Gelu_apprx_tanh, [pm], [gv])
                ph.act(G.junk[:, 0:512], gv[:, c * 512:(c + 1) * 512], AF.Square, [gv], [G.junk, st],
                       scale=1.0 / 32.0, accum_out=st[:, 4 + c:5 + c])
            else:
                c = n - 8
                ph.act(sz[:, c * 512:(c + 1) * 512], pm[:], AF.Silu, [pm], [sz])
        ph.red("dve", st[:, 8:9], st[:, 4:8], ALU.add, [st], [st])
        ph.ts("dve", st[:, 8:9], st[:, 8:9], 0.5, None, ALU.mult, None, [st], [st])
        ph.rstd(st[:, 9:10], st[:, 8:9], st[:, 10:11], [st], [st])
        ph.stt("dve", vn[:], gv[:], st[:, 9:10], gain[:], ALU.mult, ALU.mult, [gv, st, gain], [vn])
        ph.tt("dve", gu[:], gu[:], sz[:], ALU.mult, [gu, sz], [gu])
        for g in range(8):
            pm = pmx[(g // 2) % 2]
            o = (g % 2) * 256
            ph.mm(pm[:, o:o + 256], w_sT[:, g, :], vn[:, g * 256:(g + 1) * 256], True, True, [w_sT, vn], [pm])
            if g % 2 == 1:
                for gg in (g - 1, g):
                    oo = (gg % 2) * 256
                    ph.stt("dve", sz[:, gg * 256:(gg + 1) * 256], pm[:, oo:oo + 256], b_s[:, gg:gg + 1],
                           gu[:, gg * 256:(gg + 1) * 256], ALU.add, ALU.mult, [pm, b_s, gu], [sz])
        for k in range(16):
            pt = pTa if k < 8 else pTb
            ph.tr(pt[:, k % 8, :], sz[:, k * 128:(k + 1) * 128], G.identb[:], [sz, G.identb], [pt])
        ph.copy("act", sT[:, 0:8, :], pTa[:], [pTa], [sT])
        ph.copy("act", sT[:, 8:16, :], pTb[:], [pTb], [sT])
        for half in range(2):
            for k in range(16):
                ph.mm(po[half][:], sT[:, k, :], w_out[:, k, half * 512:(half + 1) * 512], k == 0, k == 15, [sT, w_out], [po[half]])
        for half in range(2):
            sl = slice(half * 512, (half + 1) * 512)
            ph.tt("dve", gv[:, sl], po[half][:], gate[r][:, sl], ALU.mult, [po[half], gate[r]], [gv])
            ph.tt("dve", xt[:, sl], gv[:, sl], xt[:, sl], ALU.add, [gv, xt], [xt])
        ph.dma("pool", dst, xt[:], R=[xt])
    ph.run()
    es_w.close()


C0 = float(np.exp(-0.5))


class PS2:
    def __init__(self, ph):
        self.h = [ph.ps([128, 512], F32, "ps2") for _ in range(2)]

    def __getitem__(self, key):
        rows, cols = key
        hi = cols.start // 512
        assert (cols.stop - 1) // 512 == hi
        return self.h[hi][rows, cols.start - hi * 512:cols.stop - hi * 512]

HS = (slice(0, 512), slice(512, 1024))
SD = BF16


def rwkv_consts():
    idx = np.arange(128)
    out = {}
    cm = np.zeros((2, 128, 3, 128), np.float32)
    mask4 = np.zeros((2, 128, 512), np.float32)
    maskT = np.zeros((2, 128, 256), np.float32)
    for n in range(2):
        before = (idx[:, None] < idx[None, :]) if n == 0 else (idx[:, None] > idx[None, :])
        incl = before | np.eye(128, dtype=bool)
        cm[n, :, 0, :] = -C0 * incl
        cm[n, :, 1, :] = -C0 * before.T
        cm[n, :, 2, :] = -C0
        mask4[n] = np.concatenate([before, incl, before, incl], 1)
        maskT[n] = np.concatenate([before.T, before.T], 1)
    out["rw_cm"] = cm
    out["rw_mask4"] = mask4
    out["rw_maskT"] = maskT
    ir = np.zeros((64, 16, 64), np.float32)
    for h in range(16):
        ir[:, h, :] = np.eye(64)
    out["identrep"] = ir.reshape(64, 1024)
    return out


def declare_rwkv(nc, t, L):
    def din(name, shape):
        setattr(t, name, nc.dram_tensor(name, list(shape), F32, kind="ExternalInput").ap())

    def scr(name, shape, dt=F32):
        setattr(t, name, nc.dram_tensor(name, list(shape), dt, kind="Internal").ap())
    NTOK = LC + L
    din("rw_cm", [2, 128, 3, 128]); din("rw_mask4", [2, 128, 512]); din("rw_maskT", [2, 128, 256]); din("identrep", [64, 1024])
    din("rw_w_in", [4, D, D]); din("rw_w1cat", [D, 128]); din("rw_a1cat", [D, 128])
    din("rw_w2cat", [128, D]); din("rw_a2cat", [128, D]); din("rw_muT", [128, 6, 8])
    din("rw_w0", [2, D]); din("rw_a0", [2, D]); din("rw_k_k", [D]); din("rw_k_a", [D]); din("rw_r_k", [D])
    din("rw_ln_g", [D]); din("rw_ln_b", [D]); din("rw_w_out", [D, D])
    scr("hTc", [8, 128, LC + 2], BF16); scr("hTl", [8, 128, L + 2], BF16)
    scr("sig_d", [2, NTOK, D]); scr("kdir_d", [2, NTOK, D], BF16); scr("b_d", [2, NTOK, D], BF16)
    scr("kk_d", [NTOK, D], BF16); scr("v_d", [NTOK, D], BF16); scr("r_d", [NTOK, D], BF16); scr("sz_d", [NTOK, D], BF16)
    scr("bon_d", [NTOK, 16]); scr("y_d", [2, NTOK, D])


def phase_rwkv_h(prog, t, G, li, L):
    ph = Phase(prog, "rwh")
    xts = [ph.sb([128, D], F32, "xt") for _ in range(2)]
    hTs = [ph.sb([128, 8, 128], BF16, "hT") for _ in range(2)]
    st = ph.sb([128, 16], F32, "st")
    zt = ph.sb([128, 8, 1], BF16, "zt")
    pTa = ph.ps([128, 8, 128], BF16, "pTa")
    ph.memset("dve", zt[:], 0.0, [zt])
    for (dst, n) in ((t.hTc, LC), (t.hTl, L)):
        ph.dma("sp", dst[:, :, 0:1].rearrange("k p t -> p k t"), zt[:], R=[zt], allow_slow_non_contiguous=True)
        ph.dma("sp", dst[:, :, n + 1:n + 2].rearrange("k p t -> p k t"), zt[:], R=[zt], allow_slow_non_contiguous=True)
    tiles = [(t.ctxs, t.hTc, i, 1) for i in range(LC // 128)] + [(t.out, t.hTl, i, 0) for i in range(L // 128)]
    def load(ti):
        src, dst, i, r = tiles[ti]
        ph.dma("sp", xts[ti % 2][:], src[i * 128:(i + 1) * 128, :], W=[xts[ti % 2]])
    load(0)
    for ti, (src, dst, i, r) in enumerate(tiles):
        xt = xts[ti % 2]; hT = hTs[ti % 2]
        if ti + 1 < len(tiles):
            load(ti + 1)
        rms_to_hT(ph, G, xt, hT, pTa, li, r, None, st, 0)
        ph.dma("sp", dst[:, :, 1 + i * 128:1 + (i + 1) * 128].rearrange("k p t -> p k t"), hT[:], R=[hT])
    ph.run()


def phase_rwkv_feat(prog, t, G, L):
    es_w = ExitStack()
    W4 = prog.sb(es_w, [128, 4, 8, D], BF16, "rw_W4")
    w1c = prog.sb(es_w, [128, 8, 128], BF16, "rw_w1c")
    a1c = prog.sb(es_w, [128, 8, 128], BF16, "rw_a1c")
    w2c = prog.sb(es_w, [128, D], BF16, "rw_w2c")
    a2c = prog.sb(es_w, [128, D], BF16, "rw_a2c")
    muT = prog.sb(es_w, [128, 6, 8], F32, "rw_mu")
    w0b = prog.sb(es_w, [128, 2, D], F32, "rw_w0b")
    a0b = prog.sb(es_w, [128, 2, D], F32, "rw_a0b")
    kkb_ = prog.sb(es_w, [128, D], F32, "rw_kkb")
    kab = prog.sb(es_w, [128, D], F32, "rw_kab")
    rkb = prog.sb(es_w, [128, D], F32, "rw_rkb")
    ph = Phase(prog, "rwfw")
    stg = [ph.sb([128, 4, D], F32, "stg") for _ in range(2)]
    n = 0
    for c in range(4):
        for k0 in (0, 4):
            s = stg[n % 2]
            ph.dma("sp", s[:], t.rw_w_in[c, k0 * 128:(k0 + 4) * 128, :].rearrange("(a p) d -> p a d", p=128), W=[s])
            ph.copy(("act", "dve")[n % 2], W4[:, c, k0:k0 + 4, :], s[:], [s], [W4])
            n += 1
    for (src, dstb) in ((t.rw_w1cat, w1c), (t.rw_a1cat, a1c)):
        s = stg[n % 2]
        ph.dma("sp", s[:, 0, :].rearrange("p (a d) -> p a d", d=128), src.rearrange("(a p) d -> p a d", p=128), W=[s])
        ph.copy("dve", dstb[:], s[:, 0, :].rearrange("p (a d) -> p a d", d=128), [s], [dstb])
        n += 1
    for (src, dstb) in ((t.rw_w2cat, w2c), (t.rw_a2cat, a2c)):
        s = stg[n % 2]
        ph.dma("sp", s[:, 0, :], src[:, :], W=[s])
        ph.copy("dve", dstb[:], s[:, 0, :], [s], [dstb])
        n += 1
    ph.dma("sp", muT[:], t.rw_muT[:, :, :], W=[muT])
    for nn in range(2):
        ph.dma("sp", w0b[:, nn, :], t.rw_w0[nn, :].partition_broadcast(128), W=[w0b])
        ph.dma("sp", a0b[:, nn, :], t.rw_a0[nn, :].partition_broadcast(128), W=[a0b])
    ph.dma("sp", kkb_[:], t.rw_k_k.partition_broadcast(128), W=[kkb_])
    ph.dma("sp", kab[:], t.rw_k_a.partition_broadcast(128), W=[kab])
    ph.dma("sp", rkb[:], t.rw_r_k.partition_broadcast(128), W=[rkb])
    ph.run()

    ph = Phase(prog, "rwf")
    hws = [ph.sb([128, 8, 130], BF16, "hw") for _ in range(2)]
    xa = ph.sb([128, 8, 128], F32, "xa")
    xx = ph.sb([128, 8, 128], F32, "xx")
    xs = [ph.sb([128, 8, 128], BF16, "xs") for _ in range(6)]
    th = ph.sb([128, 128], BF16, "th")
    alb = ph.sb([128, 128], BF16, "alb")
    o_sig = [ph.sb([128, D], F32, "osig") for _ in range(2)]
    o_kd = [ph.sb([128, D], BF16, "okd") for _ in range(2)]
    o_b = [ph.sb([128, D], BF16, "ob") for _ in range(2)]
    o_kk = ph.sb([128, D], BF16, "okk")
    o_v = ph.sb([128, D], BF16, "ov")
    o_r = ph.sb([128, D], BF16, "or")
    o_sz = ph.sb([128, D], BF16, "osz")
    o_bon = ph.sb([128, 16], F32, "obon")
    rrk = ph.sb([128, D], F32, "rrk")
    tkk = ph.sb([128, D], F32, "tkk")
    kf = ph.sb([128, D], F32, "kf")
    tmp = ph.sb([128, D], F32, "tmp")
    an = ph.sb([128, D], F32, "an")
    kdf = ph.sb([128, D], F32, "kdf")
    st = ph.sb([128, 64], F32, "st")
    pa = [PS2(ph) for _ in range(3)]
    p1 = ph.ps([128, 256], F32, "p1")
    h3 = lambda ap: ap.rearrange("p (h k) -> p h k", k=64)
    tiles = [(t.hTc, i, i) for i in range(LC // 128)] + [(t.hTl, i, LC // 128 + i) for i in range(L // 128)]
    npa = 0
    import os
    CUT = int(os.environ.get("CUT", "99"))
    for ti, (src, i, g) in enumerate(tiles):
        if CUT == 0:
            break
        hw = hws[ti % 2]
        rows = slice(g * 128, (g + 1) * 128)
        if ti == 0:
            ph.dma("sp", hw[:], src[:, :, i * 128:i * 128 + 130].rearrange("k p t -> p k t"), W=[hw])
        if ti + 1 < len(tiles):
            s2, i2, g2 = tiles[ti + 1]
            ph.dma("sp", hws[(ti + 1) % 2][:], s2[:, :, i2 * 128:i2 * 128 + 130].rearrange("k p t -> p k t"), W=[hws[(ti + 1) % 2]])
        ph.tt("dve", xa[:], hw[:, :, 0:128], hw[:, :, 2:130], ALU.add, [hw], [xa])
        ph.stt("dve", xx[:], xa[:], 0.5, hw[:, :, 1:129], ALU.mult, ALU.subtract, [xa, hw], [xx])
        for c in range(6):
            for k in range(8):
                ph.stt("dve", xs[c][:, k, :], xx[:, k, :], muT[:, c, k:k + 1], hw[:, k, 1:129], ALU.mult, ALU.add, [xx, muT, hw], [xs[c]])
        if CUT == 1:
            continue
        for k in range(8):
            ph.mm(p1[:, 0:128], w1c[:, k, :], xs[4][:, k, :], k == 0, k == 7, [w1c, xs[4]], [p1])
        for k in range(8):
            ph.mm(p1[:, 128:256], a1c[:, k, :], xs[5][:, k, :], k == 0, k == 7, [a1c, xs[5]], [p1])
        ph.act(th[:], p1[:, 0:128], AF.Tanh, [p1], [th])
        ph.copy("act", alb[:], p1[:, 128:256], [p1], [alb])
        if CUT == 2:
            continue
        pcs = []
        for c in range(4):
            p = pa[npa % 3]; npa += 1
            for half in range(2):
                for k in range(8):
                    ph.mm(p[:, half * 512:(half + 1) * 512], xs[c][:, k, :], W4[:, c, k, half * 512:(half + 1) * 512], k == 0, k == 7, [xs[c], W4], p.h)
            for hs in HS:
                if CUT == 30:
                    continue
                if c == 0:
                    ph.copy("act", o_r[:, hs], p[:, hs], p.h, [o_r])
                    if CUT != 31:
                        ph.tt("dve", rrk[:, hs], p[:, hs], rkb[:, hs], ALU.mult, p.h + [rkb], [rrk])
                elif c == 1:
                    ph.copy("act", kf[:, hs], p[:, hs], p.h, [kf])
                    if CUT != 31:
                        ph.tt("dve", tkk[:, hs], p[:, hs], kkb_[:, hs], ALU.mult, p.h + [kkb_], [tkk])
                elif c == 2:
                    ph.copy("act", o_v[:, hs], p[:, hs], p.h, [o_v])
                else:
                    ph.act(o_sz[:, hs], p[:, hs], AF.Silu, p.h, [o_sz])
        if CUT in (3, 30, 31):
            continue
        ph.tt("dve", tmp[:], tkk[:], tkk[:], ALU.mult, [tkk], [tmp])
        ph.red("dve", st[:, 0:16], h3(tmp[:]), ALU.add, [tmp], [st])
        ph.ts("dve", st[:, 0:16], st[:, 0:16], 1e-12, None, ALU.add, None, [st], [st])
        ph.act(st[:, 0:16], st[:, 0:16], AF.Sqrt, [st], [st])
        ph.recip(st[:, 16:32], st[:, 0:16], [st], [st])
        ph.tt("dve", h3(o_kk[:]), h3(tkk[:]), st[:, 16:32].unsqueeze(2).broadcast_to([128, 16, 64]), ALU.mult, [tkk, st], [o_kk])
        if CUT == 4:
            continue
        for n in range(2):
            p = pa[npa % 3]; npa += 1
            for half in range(2):
                ph.mm(p[:, half * 512:(half + 1) * 512], th[64 * n:64 * n + 64, :], w2c[64 * n:64 * n + 64, half * 512:(half + 1) * 512], True, True, [th, w2c], p.h)
            for hs in HS:
                ph.tt("dve", tmp[:, hs], p[:, hs], w0b[:, n, hs], ALU.add, p.h + [w0b], [tmp])
            ph.act(o_sig[n][:], tmp[:], AF.Sigmoid, [tmp], [o_sig[n]])
            p = pa[npa % 3]; npa += 1
            for half in range(2):
                ph.mm(p[:, half * 512:(half + 1) * 512], alb[64 * n:64 * n + 64, :], a2c[64 * n:64 * n + 64, half * 512:(half + 1) * 512], True, True, [alb, a2c], p.h)
            for hs in HS:
                ph.tt("dve", tmp[:, hs], p[:, hs], a0b[:, n, hs], ALU.add, p.h + [a0b], [tmp])
            ph.act(an[:], tmp[:], AF.Sigmoid, [tmp], [an])
            ph.stt("dve", tmp[:], an[:], -1.0, kab[:], ALU.add, ALU.mult, [an, kab], [tmp])
            ph.stt("dve", kdf[:], tmp[:], 1.0, kf[:], ALU.add, ALU.mult, [tmp, kf], [kdf])
            ph.copy("act", o_kd[n][:], kdf[:], [kdf], [o_kd[n]])
            ph.tt("dve", o_b[n][:], o_kk[:], an[:], ALU.mult, [o_kk, an], [o_b[n]])
            ph.tt("dve", tmp[:], rrk[:], kdf[:], ALU.mult, [rrk, kdf], [tmp])
            ph.red("dve", st[:, 32 + 16 * n:48 + 16 * n], h3(tmp[:]), ALU.add, [tmp], [st])
        if CUT == 5:
            continue
        ph.tt("dve", o_bon[:], st[:, 32:48], st[:, 48:64], ALU.add, [st], [o_bon])
        for n in range(2):
            ph.dma("sp", t.sig_d[n, rows, :], o_sig[n][:], R=[o_sig[n]])
            ph.dma("sp", t.kdir_d[n, rows, :], o_kd[n][:], R=[o_kd[n]])
            ph.dma("sp", t.b_d[n, rows, :], o_b[n][:], R=[o_b[n]])
        ph.dma("sp", t.kk_d[rows, :], o_kk[:], R=[o_kk])
        ph.dma("sp", t.v_d[rows, :], o_v[:], R=[o_v])
        ph.dma("sp", t.r_d[rows, :], o_r[:], R=[o_r])
        ph.dma("sp", t.sz_d[rows, :], o_sz[:], R=[o_sz])
        ph.dma("sp", t.bon_d[rows, :], o_bon[:], R=[o_bon])
    ph.run()
    es_w.close()


def phase_rwkv_scan(prog, t, G, L):
    ph = Phase(prog, "rws")
    NT = (LC + L) // 128
    nct = LC // 128
    order = [list(range(NT)), list(range(nct - 1, -1, -1)) + list(range(NT - 1, nct - 1, -1))]
    cmf = ph.sb([128, 2, 3, 128], F32, "cm")
    mask4 = ph.sb([128, 2, 512], F32, "mask4")
    maskT = ph.sb([128, 2, 256], F32, "maskT")
    idrep = ph.sb([64, D], F32, "idrep")
    identS = ph.sb([128, 128], SD, "identS")
    for n in range(2):
        ph.dma("sp", cmf[:, n, :, :], t.rw_cm[n, :, :, :], W=[cmf])
        ph.dma("sp", mask4[:, n, :], t.rw_mask4[n, :, :], W=[mask4])
        ph.dma("sp", maskT[:, n, :], t.rw_maskT[n, :, :], W=[maskT])
    ph.dma("sp", idrep[:], t.identrep[:, :], W=[idrep])
    ph.copy("dve", identS[:], G.identf[:], [G.identf], [identS])
    def mk(shape, dt, nm, k=2):
        return [ph.sb(shape, dt, nm) for _ in range(k)]
    i_sig = [mk([128, D], F32, "isig", 1) * 2 for _ in range(2)]
    i_kk = [mk([128, D], BF16, "ikk", 1) * 2 for _ in range(2)]
    i_b = [mk([128, D], BF16, "ib", 1) * 2 for _ in range(2)]
    i_kd = [mk([128, D], BF16, "ikd", 1) * 2 for _ in range(2)]
    i_v = [mk([128, D], BF16, "iv") for _ in range(2)]
    i_r = [mk([128, D], BF16, "ir", 1) * 2 for _ in range(2)]
    Gt = ph.sb([128, D], F32, "Gt")
    t1 = ph.sb([128, D], F32, "t1")
    TMa = ph.sb([128, D], SD, "TMa"); TMr = ph.sb([128, D], SD, "TMr"); TMb = ph.sb([128, D], SD, "TMb"); TMk = ph.sb([128, D], SD, "TMk")
    bh = [ph.sb([128, D], SD, "bh") for _ in range(2)]
    kh = [ph.sb([128, D], SD, "kh") for _ in range(2)]
    DG = [ph.sb([64, D], F32, "DG") for _ in range(2)]
    XT = [[ph.sb([128, 4, 128], SD, "XT") for _ in range(8)] for _ in range(2)]
    Zbig = [ph.sb([128, 8, 2, 2, 64], SD, "Z") for _ in range(2)]
    Zv = [[ph.view(Zbig[n], (slice(None), hp)) for hp in range(8)] for n in range(2)]
    GR = [[ph.sb([128, 512], SD, "GR") for _ in range(2)] for _ in range(8)]
    P0 = [ph.sb([128, 2, 128], SD, "P0") for _ in range(8)]
    QP = [[ph.sb([128, 4, 128], SD, "QP") for _ in range(2)] for _ in range(8)]
    RhT = [[ph.sb([64, 2, 128], SD, "RhT") for _ in range(8)] for _ in range(2)]
    YU = [[ph.sb([128, 128], F32, "YU") for _ in range(8)] for _ in range(2)]
    SU = [[ph.sb([64, 2, 64], F32, "SU") for _ in range(8)] for _ in range(2)]
    ACT_ = [[ph.sb([64, 2, 64], SD, "ACT") for _ in range(8)] for _ in range(2)]
    Ss = [[ph.sb([64, 16, 64], SD, "Ss") for _ in range(2)] for _ in range(2)]
    Sv = [[[ph.view(Ss[n][b], (slice(None), slice(2 * hp, 2 * hp + 2))) for hp in range(8)] for b in range(2)] for n in range(2)]
    Yt = [mk([128, D], F32, "Yt") for _ in range(2)]
    Yv = [[[ph.view(Yt[n][b], (slice(None), slice(hp * 128, (hp + 1) * 128))) for hp in range(8)] for b in range(2)] for n in range(2)]
    pL = PS2(ph)
    pool = [ph.ps([128, 512], F32, "pp") for _ in range(5)]
    pTt = ph.ps([128, 4, 128], SD, "pTt")
    cnt = [0]

    def bank():
        cnt[0] += 1
        return pool[cnt[0] % 5]
    for n in range(2):
        ph.memset("dve", Ss[n][0][:], 0.0, Sv[n][0])
    h4 = lambda ap: ap.rearrange("p (a b k) -> p a b k", b=2, k=64)
    import os
    CUT2 = int(os.environ.get("CUT2", "99"))
    for s in range(NT):
        for n in range(2):
            g = order[n][s]
            rows = slice(g * 128, (g + 1) * 128)
            sb_ = s % 2
            sig = i_sig[n][sb_]; kkt = i_kk[n][sb_]; bt = i_b[n][sb_]; kdt = i_kd[n][sb_]; vt = i_v[n][sb_]; rt = i_r[n][sb_]
            ph.dma("sp", sig[:], t.sig_d[n, rows, :], W=[sig])
            ph.dma("sp", kkt[:], t.kk_d[rows, :], W=[kkt])
            ph.dma("sp", bt[:], t.b_d[n, rows, :], W=[bt])
            ph.dma("sp", kdt[:], t.kdir_d[n, rows, :], W=[kdt])
            ph.dma("sp", vt[:], t.v_d[rows, :], W=[vt])
            ph.dma("sp", rt[:], t.r_d[rows, :], W=[rt])
            for half in range(2):
                ph.mm(pL[:, half * 512:(half + 1) * 512], cmf[:, n, 0, :], sig[:, half * 512:(half + 1) * 512], True, True, [cmf, sig], pL.h)
            for hs in HS:
                ph.act(Gt[:, hs], pL[:, hs], AF.Exp, pL.h, [Gt])
            ph.tt("dve", TMr[:], rt[:], Gt[:], ALU.mult, [rt, Gt], [TMr])
            for hs in HS:
                ph.act(Gt[:, hs], pL[:, hs], AF.Exp, pL.h, [Gt], scale=-1.0)
            ph.tt("dve", TMb[:], bt[:], Gt[:], ALU.mult, [bt, Gt], [TMb])
            ph.tt("dve", TMk[:], kdt[:], Gt[:], ALU.mult, [kdt, Gt], [TMk])
            for hs in HS:
                ph.stt("dve", t1[:, hs], sig[:, hs], C0, pL[:, hs], ALU.mult, ALU.add, [sig] + pL.h, [t1])
            ph.act(Gt[:], t1[:], AF.Exp, [t1], [Gt])
            ph.stt("dve", TMa[:], kkt[:], -1.0, Gt[:], ALU.mult, ALU.mult, [kkt, Gt], [TMa])
            ph.copy("act", Zbig[n][:, :, :, 0, :], h4(TMa[:]), [TMa], Zv[n])
            for half in range(2):
                ph.mm(pL[:, half * 512:(half + 1) * 512], cmf[:, n, 1, :], sig[:, half * 512:(half + 1) * 512], True, True, [cmf, sig], pL.h)
            for hs in HS:
                ph.act(Gt[:, hs], pL[:, hs], AF.Exp, pL.h, [Gt])
            ph.tt("dve", bh[n][:], bt[:], Gt[:], ALU.mult, [bt, Gt], [bh[n]])
            ph.tt("dve", kh[n][:], kdt[:], Gt[:], ALU.mult, [kdt, Gt], [kh[n]])
            for half in range(2):
                ph.mm(pL[:, half * 512:(half + 1) * 512], cmf[:, n, 2, :], sig[:, half * 512:(half + 1) * 512], True, True, [cmf, sig], pL.h)
            for hs in HS:
                ph.act(Gt[0:64, hs], pL[0:64, hs], AF.Exp, pL.h, [Gt])
            ph.tt("dve", DG[n][:], Gt[0:64, :], idrep[:], ALU.mult, [Gt, idrep], [DG[n]])
            for hp in range(8):
                cs = slice(hp * 128, (hp + 1) * 128)
                for j, TM in enumerate((TMa, TMr, TMb, TMk)):
                    ph.tr(pTt[:, j, :], TM[:, cs], identS[:], [TM, identS], [pTt])
                ph.copy("act", XT[n][hp][:], pTt[:], [pTt], [XT[n][hp]])
            if CUT2 == 1:
                continue
            for hp in range(8):
                X = XT[n][hp]
                for hh in range(2):
                    b0 = 64 * hh
                    h = 2 * hp + hh
                    g1 = bank()
                    AR = X[b0:b0 + 64, 0:2, :].rearrange("p a t -> p (a t)")
                    ph.mm(g1[:, 0:256], X[b0:b0 + 64, 2, :], AR, True, True, [X], [g1])
                    ph.mm(g1[:, 256:512], X[b0:b0 + 64, 3, :], AR, True, True, [X], [g1])
                    ph.tt("dve", GR[hp][hh][:], g1[:], mask4[:, n, :], ALU.mult, [g1, mask4], [GR[hp][hh]])
                    g3 = bank()
                    ph.mm(g3[:, 0:128], X[b0:b0 + 64, 0, :], X[b0:b0 + 64, 2, :], True, True, [X], [g3])
                    ph.tt("dve", P0[hp][:, hh, :], g3[:, 0:128], maskT[:, n, 0:128], ALU.mult, [g3, maskT], [P0[hp]])
                    ph.mm(g3[:, 128:192], GR[hp][hh][:, 256:384], vt[:, h * 64:(h + 1) * 64], True, True, [GR[hp][hh], vt], [g3])
                    ph.copy("act", Zv[n][hp][:, hh, 1, :], g3[:, 128:192], [g3], [Zv[n][hp]])
            if CUT2 in (2, 20, 21, 22):
                continue
            for j in range(7):
                for hp in range(8):
                    if j == 0:
                        Qs = [GR[hp][hh][:, 0:128] for hh in range(2)]
                        Ps = [P0[hp][:, hh, :] for hh in range(2)]
                        qb = [GR[hp][0], GR[hp][1], P0[hp]]
                    else:
                        cur = QP[hp][(j - 1) % 2]
                        Qs = [cur[:, hh, :] for hh in range(2)]
                        Ps = [cur[:, 2 + hh, :] for hh in range(2)]
                        qb = [cur]
                    zp = bank()
                    for hh in range(2):
                        ph.mm(zp[:, hh * 128:(hh + 1) * 128], Qs[hh], Zv[n][hp][:, hh, :, :].rearrange("p a k -> p (a k)"), True, True, qb + [Zv[n][hp]], [zp])
                    zv = Zv[n][hp][:, :, :, :].rearrange("p b a k -> p (b a k)")
                    ph.tt("dve", zv, zv, zp[:, 0:256], ALU.add, [zp, Zv[n][hp]], [Zv[n][hp]])
                    if j < 6:
                        nx = QP[hp][j % 2]
                        qp = bank()
                        for hh in range(2):
                            ph.mm(qp[:, hh * 128:(hh + 1) * 128], Ps[hh], Qs[hh], True, True, qb, [qp])
                            ph.mm(qp[:, 256 + hh * 128:256 + (hh + 1) * 128], Qs[hh], Ps[hh], True, True, qb, [qp])
                        ph.copy("act", nx[:].rearrange("p a b -> p (a b)"), qp[:], [qp], [nx])
            if CUT2 == 3:
                continue
            for hp in range(8):
                f1 = bank()
                for hh in range(2):
                    h = 2 * hp + hh
                    nbr = GR[hp][hh][:, 128:256]
                    ph.mm(f1[0:64, hh * 128:(hh + 1) * 128], Zv[n][hp][:, hh, 0, :], nbr, True, False, [Zv[n][hp], GR[hp][hh]], [f1])
                    ph.mm(f1[0:64, hh * 128:(hh + 1) * 128], TMr[:, h * 64:(h + 1) * 64], identS[:], False, True, [TMr, identS], [f1])
                ph.copy("act", RhT[n][hp][:].rearrange("p a b -> p (a b)"), f1[0:64, 0:256], [f1], [RhT[n][hp]])
                f2 = bank()
                for hh in range(2):
                    h = 2 * hp + hh
                    nbr = GR[hp][hh][:, 128:256]
                    nkr = GR[hp][hh][:, 384:512]
                    cu = Zv[n][hp][:, hh, 1, :]
                    ph.mm(f2[:, hh * 64:(hh + 1) * 64], nbr, cu, True, False, [GR[hp][hh], Zv[n][hp]], [f2])
                    ph.mm(f2[:, hh * 64:(hh + 1) * 64], nkr, vt[:, h * 64:(h + 1) * 64], False, True, [GR[hp][hh], vt], [f2])
                    ph.mm(f2[0:64, 128 + hh * 64:128 + (hh + 1) * 64], bh[n][:, h * 64:(h + 1) * 64], cu, True, False, [bh[n], Zv[n][hp]], [f2])
                    ph.mm(f2[0:64, 128 + hh * 64:128 + (hh + 1) * 64], kh[n][:, h * 64:(h + 1) * 64], vt[:, h * 64:(h + 1) * 64], False, True, [kh[n], vt], [f2])
                    ph.mm(f2[0:64, 256 + hh * 64:256 + (hh + 1) * 64], Zv[n][hp][:, hh, 0, :], bh[n][:, h * 64:(h + 1) * 64], True, True, [Zv[n][hp], bh[n]], [f2])
                ph.copy("act", YU[n][hp][:], f2[:, 0:128], [f2], [YU[n][hp]])
                ph.copy("act", SU[n][hp][:].rearrange("p a b -> p (a b)"), f2[0:64, 128:256], [f2], [SU[n][hp]])
                ph.tt("dve", ACT_[n][hp][:].rearrange("p a b -> p (a b)"), f2[0:64, 256:384], DG[n][:, hp * 128:(hp + 1) * 128], ALU.add, [f2, DG[n]], [ACT_[n][hp]])
            if CUT2 == 4:
                continue
            Sc = Sv[n][s % 2]; Sn = Sv[n][(s + 1) % 2]
            Y = Yv[n][s % 2]
            for hp in range(8):
                f4 = bank()
                for hh in range(2):
                    ph.mm(f4[:, hh * 64:(hh + 1) * 64], RhT[n][hp][:, hh, :], Sc[hp][:, hh, :], True, True, [RhT[n][hp], Sc[hp]], [f4])
                    ph.mm(f4[0:64, 128 + hh * 64:128 + (hh + 1) * 64], ACT_[n][hp][:, hh, :], Sc[hp][:, hh, :], True, True, [ACT_[n][hp], Sc[hp]], [f4])
                ph.tt("dve", Y[hp][:], f4[:, 0:128], YU[n][hp][:], ALU.add, [f4, YU[n][hp]], [Y[hp]])
                ph.tt("dve", Sn[hp][:].rearrange("p a b -> p (a b)"), f4[0:64, 128:256], SU[n][hp][:].rearrange("p a b -> p (a b)"), ALU.add, [f4, SU[n][hp]], [Sn[hp]])
            ph.dma("sp", t.y_d[n, rows, :], Yt[n][s % 2][:], R=Y, sbuf=Yt[n][s % 2])
    ph.run()


def phase_rwkv_out(prog, t, G, li, L):
    es_w = ExitStack()
    w_out = prog.sb(es_w, [128, 8, D], BF16, "rwo_w")
    lng = prog.sb(es_w, [128, D], F32, "rwo_g")
    lnb = prog.sb(es_w, [128, D], F32, "rwo_b")
    gate = [prog.sb(es_w, [128, D], F32, "rwo_gate") for _ in range(2)]
    ph = Phase(prog, "rwow")
    stg = [ph.sb([128, 4, D], F32, "stg") for _ in range(2)]
    for n, k0 in enumerate((0, 4)):
        s = stg[n]
        ph.dma("sp", s[:], t.rw_w_out[k0 * 128:(k0 + 4) * 128, :].rearrange("(a p) d -> p a d", p=128), W=[s])
        ph.copy(("act", "dve")[n], w_out[:, k0:k0 + 4, :], s[:], [s], [w_out])
    ph.dma("sp", lng[:], t.rw_ln_g.partition_broadcast(128), W=[lng])
    ph.dma("sp", lnb[:], t.rw_ln_b.partition_broadcast(128), W=[lnb])
    for r in range(2):
        ph.dma("sp", gate[r][:], t.gate_d[li, r, :, :], W=[gate[r]])
    ph.run()
    ph = Phase(prog, "rwo")
    y0 = [ph.sb([128, D], F32, "y0") for _ in range(2)]
    y1 = [ph.sb([128, D], F32, "y1") for _ in range(2)]
    vt = [ph.sb([128, D], BF16, "v") for _ in range(2)]
    szt = [ph.sb([128, D], BF16, "sz") for _ in range(2)]
    bon = [ph.sb([128, 16], F32, "bon") for _ in range(2)]
    xts = [ph.sb([128, D], F32, "xt") for _ in range(2)]
    tmp = ph.sb([128, D], F32, "tmp")
    ob = ph.sb([128, D], BF16, "ob")
    oT = ph.sb([128, 8, 128], BF16, "oT")
    st = ph.sb([128, 64], F32, "st")
    pT = ph.ps([128, 8, 128], BF16, "pT")
    po = [ph.ps([128, 512], F32, "po") for _ in range(2)]
    h3 = lambda ap: ap.rearrange("p (h k) -> p h k", k=64)
    bc = lambda ap: ap.unsqueeze(2).broadcast_to([128, 16, 64])
    tiles = [(t.ctxs, i, i, 1) for i in range(LC // 128)] + [(t.out, i, LC // 128 + i, 0) for i in range(L // 128)]
    def load(ti):
        xs_, i, g, r = tiles[ti]
        rows = slice(g * 128, (g + 1) * 128)
        b_ = ti % 2
        ph.dma("sp", y0[b_][:], t.y_d[0, rows, :], W=[y0[b_]])
        ph.dma("sp", y1[b_][:], t.y_d[1, rows, :], W=[y1[b_]])
        ph.dma("sp", vt[b_][:], t.v_d[rows, :], W=[vt[b_]])
        ph.dma("sp", szt[b_][:], t.sz_d[rows, :], W=[szt[b_]])
        ph.dma("sp", bon[b_][:], t.bon_d[rows, :], W=[bon[b_]])
        ph.dma("sp", xts[b_][:], xs_[i * 128:(i + 1) * 128, :], W=[xts[b_]])
    load(0)
    for ti, (xs_, i, g, r) in enumerate(tiles):
        rows = slice(g * 128, (g + 1) * 128)
        b_ = ti % 2
        ya = y0[b_]; yb = y1[b_]; v = vt[b_]; sz = szt[b_]; bo = bon[b_]; xt = xts[b_]
        if ti + 1 < len(tiles):
            load(ti + 1)
        ph.tt("dve", ya[:], ya[:], yb[:], ALU.add, [ya, yb], [ya])
        ph.red("dve", st[:, 0:16], h3(ya[:]), ALU.add, [ya], [st])
        ph.ts("dve", st[:, 0:16], st[:, 0:16], 1.0 / 64, None, ALU.mult, None, [st], [st])
        ph.tt("dve", h3(ya[:]), h3(ya[:]), bc(st[:, 0:16]), ALU.subtract, [ya, st], [ya])
        ph.tt("dve", tmp[:], ya[:], ya[:], ALU.mult, [ya], [tmp])
        ph.red("dve", st[:, 16:32], h3(tmp[:]), ALU.add, [tmp], [st])
        ph.ts("dve", st[:, 16:32], st[:, 16:32], 1.0 / 64, 64e-5, ALU.mult, ALU.add, [st], [st])
        ph.act(st[:, 16:32], st[:, 16:32], AF.Sqrt, [st], [st])
        ph.recip(st[:, 32:48], st[:, 16:32], [st], [st])
        ph.tt("dve", h3(ya[:]), h3(ya[:]), bc(st[:, 32:48]), ALU.mult, [ya, st], [ya])
        ph.tt("dve", ya[:], ya[:], lng[:], ALU.mult, [ya, lng], [ya])
        ph.tt("dve", ya[:], ya[:], lnb[:], ALU.add, [ya, lnb], [ya])
        ph.tt("dve", h3(tmp[:]), h3(v[:]), bc(bo[:]), ALU.mult, [v, bo], [tmp])
        ph.tt("dve", ya[:], ya[:], tmp[:], ALU.add, [ya, tmp], [ya])
        ph.tt("dve", ob[:], ya[:], sz[:], ALU.mult, [ya, sz], [ob])
        for k in range(8):
            ph.tr(pT[:, k, :], ob[:, k * 128:(k + 1) * 128], G.identb[:], [ob, G.identb], [pT])
        ph.copy("act", oT[:], pT[:], [pT], [oT])
        for half in range(2):
            for k in range(8):
                ph.mm(po[half][:], oT[:, k, :], w_out[:, k, half * 512:(half + 1) * 512], k == 0, k == 7, [oT, w_out], [po[half]])
        for half in range(2):
            sl = slice(half * 512, (half + 1) * 512)
            ph.tt("dve", tmp[:, sl], po[half][:], gate[r][:, sl], ALU.mult, [po[half], gate[r]], [tmp])
            ph.tt("dve", xt[:, sl], tmp[:, sl], xt[:, sl], ALU.add, [tmp, xt], [xt])
        ph.dma("sp", xs_[i * 128:(i + 1) * 128, :], xt[:], R=[xt])
    ph.run()
    es_w.close()

MLA_SCALE_ = 192 ** -0.5


def declare_mla(nc, t, L):
    def din(name, shape):
        setattr(t, name, nc.dram_tensor(name, list(shape), F32, kind="ExternalInput").ap())

    def scr(name, shape, dt=F32):
        setattr(t, name, nc.dram_tensor(name, list(shape), dt, kind="Internal").ap())
    NTOK = LC + L
    din("ml_w_in", [D, 3136]); din("ml_w_uq", [768, 3072]); din("ml_w_uk", [256, 2048]); din("ml_w_uv", [256, 2048])
    din("ml_qn", [768]); din("ml_kvn", [256]); din("ml_gq", [192]); din("ml_gk", [192]); din("ml_w_out", [2048, D])
    din("rope_cs", [L, 64])
    scr("QT_d", [16, 2, 128, L], BF16); scr("KnT_d", [16, 128, NTOK], BF16); scr("KrT_d", [64, NTOK], BF16)
    scr("V_d", [NTOK, 2048], BF16); scr("rk_d", [NTOK, 16]); scr("msz_d", [L, 2048], BF16); scr("o_d", [L, 2048])


def phase_mla_proj(prog, t, G, li, L):
    es_w = ExitStack()
    w_in = prog.sb(es_w, [128, 8, 3136], BF16, "ml_win")
    w_uq = prog.sb(es_w, [128, 6, 3072], BF16, "ml_wuq")
    w_uk = prog.sb(es_w, [128, 2, 2048], BF16, "ml_wuk")
    w_uv = prog.sb(es_w, [128, 2, 2048], BF16, "ml_wuv")
    qnb = prog.sb(es_w, [128, 768], F32, "ml_qnb")
    kvnb = prog.sb(es_w, [128, 256], F32, "ml_kvnb")
    gq = prog.sb(es_w, [128, 192], F32, "ml_gq")
    gk = prog.sb(es_w, [128, 192], F32, "ml_gk")
    ones = prog.sb(es_w, [128, 1], BF16, "ml_ones")
    ph = Phase(prog, "mlw")
    stg = [ph.sb([128, 3136], F32, "stg") for _ in range(2)]
    n = 0
    engs = ("act", "dve")
    for k in range(8):
        s = stg[n % 2]
        ph.dma("sp", s[:], t.ml_w_in[k * 128:(k + 1) * 128, :], W=[s])
        ph.copy(engs[n % 2], w_in[:, k, :], s[:], [s], [w_in]); n += 1
    for k in range(6):
        s = stg[n % 2]
        ph.dma("sp", s[:, 0:3072], t.ml_w_uq[k * 128:(k + 1) * 128, :], W=[s])
        ph.copy(engs[n % 2], w_uq[:, k, :], s[:, 0:3072], [s], [w_uq]); n += 1
    for (src, dst) in ((t.ml_w_uk, w_uk), (t.ml_w_uv, w_uv)):
        for k in range(2):
            s = stg[n % 2]
            ph.dma("sp", s[:, 0:2048], src[k * 128:(k + 1) * 128, :], W=[s])
            ph.copy(engs[n % 2], dst[:, k, :], s[:, 0:2048], [s], [dst]); n += 1
    ph.dma("sp", qnb[:], t.ml_qn.partition_broadcast(128), W=[qnb])
    ph.dma("sp", kvnb[:], t.ml_kvn.partition_broadcast(128), W=[kvnb])
    ph.dma("sp", gq[:], t.ml_gq.partition_broadcast(128), W=[gq])
    ph.dma("sp", gk[:], t.ml_gk.partition_broadcast(128), W=[gk])
    ph.tt("dve", gq[:, 0:128], gq[:, 0:128], gk[:, 0:128], ALU.mult, [gq, gk], [gq])
    ph.memset("dve", ones[:], 1.0, [ones])
    ph.run()

    ph = Phase(prog, "mlp")
    xts = [ph.sb([128, D], F32, "xt") for _ in range(2)]
    hT = ph.sb([128, 8, 128], BF16, "hT")
    cqkv = ph.sb([128, 1088], F32, "cqkv")
    cn = ph.sb([128, 1024], BF16, "cn")
    cT = ph.sb([128, 8, 128], BF16, "cT")
    q = ph.sb([128, 3072], F32, "q")
    tmp = ph.sb([128, 3072], F32, "tmp")
    Qt = ph.sb([128, 16, 256], BF16, "Qt")
    QTt = ph.sb([128, 32, 128], BF16, "QTt")
    KnT = ph.sb([128, 16, 128], BF16, "KnT")
    sq = ph.sb([128, 16, 128], BF16, "sq")
    Vt = ph.sb([128, 2048], BF16, "Vt")
    szt = ph.sb([128, 2048], BF16, "szt")
    cs = [ph.sb([128, 64], F32, "cs") for _ in range(2)]
    krg = ph.sb([128, 64], F32, "krg")
    KRt = ph.sb([128, 64], BF16, "KRt")
    KRT = ph.sb([64, 128], BF16, "KRT")
    rk = ph.sb([128, 16], F32, "rk")
    st = ph.sb([128, 64], F32, "st")
    r4 = ph.sb([128, 4, 16, 32], F32, "r4")
    pT = [ph.ps([128, 8, 128], BF16, "pT") for _ in range(2)]
    pc = [ph.ps([128, 512], F32, "pc") for _ in range(3)]
    pk = [ph.ps([128, 4, 128], F32, "pk") for _ in range(2)]
    pss = ph.ps([128, 16], F32, "pss")
    npc = [0]

    def nxt():
        npc[0] += 1
        return pc[npc[0] % 3]
    tiles = [(t.ctxs, i, i, 1) for i in range(LC // 128)] + [(t.out, i, LC // 128 + i, 0) for i in range(L // 128)]
    for ti, (src, i, g, r) in enumerate(tiles):
        lat = (r == 0)
        rows = slice(g * 128, (g + 1) * 128)
        xt = xts[ti % 2]
        csb = cs[ti % 2]

        def load(tj):
            s2, i2, g2, r2 = tiles[tj]
            ph.dma("sp", xts[tj % 2][:], s2[i2 * 128:(i2 + 1) * 128, :], W=[xts[tj % 2]])
            if r2 == 0:
                ph.dma("sp", cs[tj % 2][:], t.rope_cs[i2 * 128:(i2 + 1) * 128, :], W=[cs[tj % 2]])
        if ti == 0:
            load(0)
        if ti + 1 < len(tiles):
            load(ti + 1)
        rms_to_hT(ph, G, xt, hT, pT[0], li, r, None, st, 0)
        if lat:
            chunks = [(0, 512), (512, 1024), (1024, 1088)]
        else:
            chunks = [(768, 1088)]
        for (c0, c1) in chunks:
            p = nxt()
            for k in range(8):
                ph.mm(p[:, 0:c1 - c0], hT[:, k, :], w_in[:, k, c0:c1], k == 0, k == 7, [hT, w_in], [p])
            ph.copy("act", cqkv[:, c0:c1], p[:, 0:c1 - c0], [p], [cqkv])
        if lat:
            for zc in range(4):
                p = nxt()
                c0 = 1088 + zc * 512
                for k in range(8):
                    ph.mm(p[:], hT[:, k, :], w_in[:, k, c0:c0 + 512], k == 0, k == 7, [hT, w_in], [p])
                ph.act(szt[:, zc * 512:(zc + 1) * 512], p[:], AF.Silu, [p], [szt])
            ph.dma("sp", t.msz_d[i * 128:(i + 1) * 128, :], szt[:], R=[szt])
            ph.act(G.junk[:, 0:768], cqkv[:, 0:768], AF.Square, [cqkv], [G.junk, st], scale=float(768 ** -0.5), accum_out=st[:, 4:5])
            ph.rstd(st[:, 5:6], st[:, 4:5], st[:, 6:7], [st], [st])
            ph.stt("dve", cn[:, 0:768], cqkv[:, 0:768], st[:, 5:6], qnb[:], ALU.mult, ALU.mult, [cqkv, st, qnb], [cn])
        ph.act(G.junk[:, 0:256], cqkv[:, 768:1024], AF.Square, [cqkv], [G.junk, st], scale=1.0 / 16.0, accum_out=st[:, 7:8])
        ph.rstd(st[:, 8:9], st[:, 7:8], st[:, 9:10], [st], [st])
        ph.stt("dve", cn[:, 768:1024], cqkv[:, 768:1024], st[:, 8:9], kvnb[:], ALU.mult, ALU.mult, [cqkv, st, kvnb], [cn])
        ph.act(G.junk[:, 0:64], cqkv[:, 1024:1088], AF.Square, [cqkv], [G.junk, st], accum_out=st[:, 10:11])
        k0 = 0 if lat else 6
        for k in range(k0, 8):
            ph.tr(pT[1][:, k, :], cn[:, k * 128:(k + 1) * 128], G.identb[:], [cn, G.identb], [pT[1]])
        ph.copy("act", cT[:, k0:8, :], pT[1][:, k0:8, :], [pT[1]], [cT])
        if lat:
            for n in range(6):
                p = nxt()
                for k in range(6):
                    ph.mm(p[:], cT[:, k, :], w_uq[:, k, n * 512:(n + 1) * 512], k == 0, k == 5, [cT, w_uq], [p])
                ph.copy("act", q[:, n * 512:(n + 1) * 512], p[:], [p], [q])
            q3 = q[:].rearrange("p (h d) -> p h d", d=192)
            t3 = tmp[:].rearrange("p (h d) -> p h d", d=192)
            ph.tt("dve", tmp[:], q[:], q[:], ALU.mult, [q], [tmp])
            ph.red("dve", st[:, 16:32], t3, ALU.add, [tmp], [st])
            ph.ts("dve", st[:, 16:32], st[:, 16:32], 1.0 / 192, EPS, ALU.mult, ALU.add, [st], [st])
            ph.act(st[:, 16:32], st[:, 16:32], AF.Sqrt, [st], [st])
            ph.recip(st[:, 32:48], st[:, 16:32], [st], [st])
            ph.tt("dve", q3, q3, st[:, 32:48].unsqueeze(2).broadcast_to([128, 16, 192]), ALU.mult, [q, st], [q])
            ph.tt("dve", q3, q3, gq[:].unsqueeze(1).broadcast_to([128, 16, 192]), ALU.mult, [q, gq], [q])
            cb = csb[:, 0:32].unsqueeze(1).broadcast_to([128, 16, 32])
            sb_ = csb[:, 32:64].unsqueeze(1).broadcast_to([128, 16, 32])
            x1 = q3[:, :, 128:160]; x2 = q3[:, :, 160:192]
            ph.tt("dve", r4[:, 0], x1, cb, ALU.mult, [q, csb], [r4])
            ph.tt("dve", r4[:, 1], x2, sb_, ALU.mult, [q, csb], [r4])
            ph.tt("dve", r4[:, 2], x1, sb_, ALU.mult, [q, csb], [r4])
            ph.tt("dve", r4[:, 3], x2, cb, ALU.mult, [q, csb], [r4])
            ph.tt("dve", Qt[:, :, 128:160], r4[:, 0], r4[:, 1], ALU.subtract, [r4], [Qt])
            ph.tt("dve", Qt[:, :, 160:192], r4[:, 2], r4[:, 3], ALU.add, [r4], [Qt])
            ph.copy("act", Qt[:, :, 0:128], q3[:, :, 0:128], [q], [Qt])
            ph.copy("act", Qt[:, :, 192:256], q3[:, :, 128:192], [q], [Qt])
            Qf = Qt[:].rearrange("p h d -> p (h d)")
            for j in range(4):
                pt = pT[j % 2]
                for k in range(8):
                    c = j * 8 + k
                    ph.tr(pt[:, k, :], Qf[:, c * 128:(c + 1) * 128], G.identb[:], [Qt, G.identb], [pt])
                ph.copy(("act", "dve")[j % 2], QTt[:, j * 8:(j + 1) * 8, :], pt[:], [pt], [QTt])
            ph.dma("sp", t.QT_d[:, :, :, i * 128:(i + 1) * 128].rearrange("h e d t -> d (h e) t"), QTt[:], R=[QTt])
        for h4 in range(4):
            p = pk[h4 % 2]
            for hh in range(4):
                h = 4 * h4 + hh
                for kc in range(2):
                    ph.mm(p[:, hh, :], w_uk[:, kc, h * 128:(h + 1) * 128], cT[:, 6 + kc, :], kc == 0, kc == 1, [w_uk, cT], [p])
            ph.copy("act", KnT[:, 4 * h4:4 * h4 + 4, :], p[:], [p], [KnT])
            ph.act(sq[:, 4 * h4:4 * h4 + 4, :], p[:], AF.Square, [p], [sq])
        for h in range(16):
            ph.mm(pss[:, h:h + 1], sq[:, h, :], ones[:], True, True, [sq, ones], [pss])
        ph.ts("dve", rk[:], pss[:], st[:, 10:11], 1.0 / 192, ALU.add, ALU.mult, [pss, st], [rk])
        ph.ts("dve", rk[:], rk[:], EPS, None, ALU.add, None, [rk], [rk])
        ph.act(rk[:], rk[:], AF.Sqrt, [rk], [rk])
        ph.recip(st[:, 48:64], rk[:], [rk], [st])
        ph.ts("dve", rk[:], st[:, 48:64], MLA_SCALE_, None, ALU.mult, None, [st], [rk])
        ph.dma("sp", t.rk_d[rows, :], rk[:], R=[rk])
        ph.dma("sp", t.KnT_d[:, :, rows].rearrange("h d t -> d h t"), KnT[:], R=[KnT])
        for n in range(4):
            p = nxt()
            for kc in range(2):
                ph.mm(p[:], cT[:, 6 + kc, :], w_uv[:, kc, n * 512:(n + 1) * 512], kc == 0, kc == 1, [cT, w_uv], [p])
            ph.copy("act", Vt[:, n * 512:(n + 1) * 512], p[:], [p], [Vt])
        ph.dma("sp", t.V_d[rows, :], Vt[:], R=[Vt])
        ph.tt("dve", krg[:], cqkv[:, 1024:1088], gk[:, 128:192], ALU.mult, [cqkv, gk], [krg])
        if lat:
            ph.tt("dve", r4[:, 0, 0, :], krg[:, 0:32], csb[:, 0:32], ALU.mult, [krg, csb], [r4])
            ph.tt("dve", r4[:, 0, 1, :], krg[:, 32:64], csb[:, 32:64], ALU.mult, [krg, csb], [r4])
            ph.tt("dve", r4[:, 0, 2, :], krg[:, 0:32], csb[:, 32:64], ALU.mult, [krg, csb], [r4])
            ph.tt("dve", r4[:, 0, 3, :], krg[:, 32:64], csb[:, 0:32], ALU.mult, [krg, csb], [r4])
            ph.tt("dve", KRt[:, 0:32], r4[:, 0, 0, :], r4[:, 0, 1, :], ALU.subtract, [r4], [KRt])
            ph.tt("dve", KRt[:, 32:64], r4[:, 0, 2, :], r4[:, 0, 3, :], ALU.add, [r4], [KRt])
        else:
            ph.copy("dve", KRt[:], krg[:], [krg], [KRt])
        ph.tr(pT[0][0:64, 0, :], KRt[:], G.identb[:], [KRt, G.identb], [pT[0]])
        ph.copy("act", KRT[:], pT[0][0:64, 0, :], [pT[0]], [KRT])
        ph.dma("sp", t.KrT_d[:, rows], KRT[:], R=[KRT])
    ph.run()
    es_w.close()


def phase_mla_attn(prog, t, G, L):
    ph = Phase(prog, "mla")
    NTOK = LC + L
    NKT = NTOK // 128
    nct = LC // 128
    NQG = L // 512
    KrT = ph.sb([128, NTOK], BF16, "KrT")
    rk = ph.sb([128, NKT, 16], F32, "rk")
    KnT = [ph.sb([128, NTOK], BF16, "KnT") for _ in range(2)]
    Vx = [ph.sb([128, NKT, 129], BF16, "Vx") for _ in range(2)]
    Qn = [ph.sb([128, 512], BF16, "Qn") for _ in range(2)]
    Qr = [ph.sb([128, 512], BF16, "Qr") for _ in range(2)]
    pTs = [ph.sb([128, 512], BF16, "pTs") for _ in range(3)]
    ot = [ph.sb([128, 4, 128], F32, "ot") for _ in range(2)]
    rden = ph.sb([128, 8], F32, "rden")
    ps = [ph.ps([128, 512], F32, "ps") for _ in range(3)]
    po = [ph.ps([128, 512], F32, "po") for _ in range(4)]
    ph.dma("sp", KrT[0:64, :], t.KrT_d[:, :], W=[KrT])
    ph.dma("sp", KrT[64:128, :], t.KrT_d[:, :], W=[KrT])
    ph.dma("sp", rk[:], t.rk_d.rearrange("(kt p) h -> p kt h", p=128), W=[rk])
    for b in range(2):
        ph.memset("dve", Vx[b][:, :, 128:129], 1.0, [Vx[b]])
    def load_head(h):
        ph.dma("sp", KnT[h % 2][:], t.KnT_d[h, :, :], W=[KnT[h % 2]])
        ph.dma("sp", Vx[h % 2][:, :, 0:128], t.V_d[:, h * 128:(h + 1) * 128].rearrange("(kt p) d -> p kt d", p=128), W=[Vx[h % 2]])

    def load_q(j):
        h, qg = divmod(j, NQG)
        ph.dma("sp", Qn[j % 2][:], t.QT_d[h, 0, :, qg * 512:(qg + 1) * 512], W=[Qn[j % 2]])
        ph.dma("sp", Qr[j % 2][:], t.QT_d[h, 1, :, qg * 512:(qg + 1) * 512], W=[Qr[j % 2]])
    load_head(0)
    load_q(0)
    it = 0
    for h in range(16):
        kn = KnT[h % 2]; vx = Vx[h % 2]
        for qg in range(NQG):
            qn = Qn[it % 2]; qr = Qr[it % 2]; o = ot[it % 2]
            it += 1
            if it < 16 * NQG:
                if it % NQG == 0:
                    load_head(it // NQG)
                load_q(it)

            def qk(kt):
                p = ps[kt % 3]
                ph.mm(p[:], kn[:, kt * 128:(kt + 1) * 128], qn[:], True, False, [kn, qn], [p])
                b0 = 64 if kt < nct else 0
                ph.mm(p[:], KrT[b0:b0 + 64, kt * 128:(kt + 1) * 128], qr[b0:b0 + 64, :], False, True, [KrT, qr], [p])
            qk(0)
            qk(1)
            for kt in range(NKT):
                if kt + 2 < NKT:
                    qk(kt + 2)
                p = ps[kt % 3]; pt = pTs[kt % 3]
                ph.act(pt[:], p[:], AF.Exp, [p, rk], [pt], scale=rk[:, kt, h:h + 1])
                for qs in range(4):
                    ph.mm(po[qs][:, 0:129], pt[:, qs * 128:(qs + 1) * 128], vx[:, kt, :], kt == 0, kt == NKT - 1, [pt, vx], [po[qs]])
            for qs in range(4):
                ph.recip(rden[:, qs:qs + 1], po[qs][:, 128:129], [po[qs]], [rden])
                ph.ts("dve", o[:, qs, :], po[qs][:, 0:128], rden[:, qs:qs + 1], None, ALU.mult, None, [po[qs], rden], [o])
            ph.dma("sp", t.o_d[qg * 512:(qg + 1) * 512, h * 128:(h + 1) * 128].rearrange("(a p) d -> p a d", p=128), o[:], R=[o])
    ph.run()


def phase_mla_out(prog, t, G, li, L):
    es_w = ExitStack()
    w_out = prog.sb(es_w, [128, 16, D], BF16, "mlo_w")
    gate = prog.sb(es_w, [128, D], F32, "mlo_gate")
    ph = Phase(prog, "mlow")
    stg = [ph.sb([128, 4, D], F32, "stg") for _ in range(2)]
    for n in range(4):
        s = stg[n % 2]
        ph.dma("sp", s[:], t.ml_w_out[n * 512:(n + 1) * 512, :].rearrange("(a p) d -> p a d", p=128), W=[s])
        ph.copy(("act", "dve")[n % 2], w_out[:, 4 * n:4 * n + 4, :], s[:], [s], [w_out])
    ph.dma("sp", gate[:], t.gate_d[li, 0, :, :], W=[gate])
    ph.run()
    ph = Phase(prog, "mlo")
    ots = [ph.sb([128, 2048], F32, "o") for _ in range(2)]
    szs = [ph.sb([128, 2048], BF16, "sz") for _ in range(2)]
    xts = [ph.sb([128, D], F32, "xt") for _ in range(2)]
    ob = ph.sb([128, 2048], BF16, "ob")
    oT = ph.sb([128, 16, 128], BF16, "oT")
    tmp = ph.sb([128, D], F32, "tmp")
    pT = [ph.ps([128, 8, 128], BF16, "pT") for _ in range(2)]
    po = [ph.ps([128, 512], F32, "po") for _ in range(2)]
    def load(i):
        rows = slice(i * 128, (i + 1) * 128)
        ph.dma("sp", ots[i % 2][:], t.o_d[rows, :], W=[ots[i % 2]])
        ph.dma("sp", szs[i % 2][:], t.msz_d[rows, :], W=[szs[i % 2]])
        ph.dma("sp", xts[i % 2][:], t.out[rows, :], W=[xts[i % 2]])
    load(0)
    for i in range(L // 128):
        rows = slice(i * 128, (i + 1) * 128)
        o = ots[i % 2]; sz = szs[i % 2]; xt = xts[i % 2]
        if i + 1 < L // 128:
            load(i + 1)
        ph.tt("dve", ob[:], o[:], sz[:], ALU.mult, [o, sz], [ob])
        for k in range(16):
            ph.tr(pT[k // 8][:, k % 8, :], ob[:, k * 128:(k + 1) * 128], G.identb[:], [ob, G.identb], [pT[k // 8]])
        for j in range(2):
            ph.copy("act", oT[:, 8 * j:8 * j + 8, :], pT[j][:], [pT[j]], [oT])
        for half in range(2):
            for k in range(16):
                ph.mm(po[half][:], oT[:, k, :], w_out[:, k, half * 512:(half + 1) * 512], k == 0, k == 15, [oT, w_out], [po[half]])
        for half in range(2):
            sl = slice(half * 512, (half + 1) * 512)
            ph.tt("dve", tmp[:, sl], po[half][:], gate[:, sl], ALU.mult, [po[half], gate], [tmp])
            ph.tt("dve", xt[:, sl], tmp[:, sl], xt[:, sl], ALU.add, [tmp, xt], [xt])
        ph.dma("sp", t.out[rows, :], xt[:], R=[xt])
    ph.run()
    es_w.close()


def build(L=L_FULL, nlayers=4, dbg=False):
    nc = bass.Bass("TRN2", target_bir_lowering=False)
    t = declare(nc, L)
    declare_rwkv(nc, t, L)
    declare_mla(nc, t, L)
    if dbg:
        t.dbg_ctx = nc.dram_tensor("dbg_ctx", [LC, D], F32, kind="ExternalOutput").ap()
        t.dbg_y = nc.dram_tensor("dbg_y", [2, LC + L, D], F32, kind="ExternalOutput").ap()
    NT = L // 128
    with ExitStack() as es:
        prog = Prog(nc, es)
        G = T()
        G.mod = prog.sb(es, [128, 4, 24, 2], F32, "mod")
        G.identb = prog.sb(es, [128, 128], BF16, "identb")
        G.identf = prog.sb(es, [128, 128], F32, "identf")
        G.junk = prog.sb(es, [128, D], BF16, "junk")
        G.xn = prog.sb(es, [128, D], BF16, "xn")
        ph = Phase(prog, "init")
        ph.dma("sp", G.identf[:], t.ident[:, :], W=[G.identf])
        ph.copy("dve", G.identb[:], G.identf[:], [G.identf], [G.identb])
        ph.run()
        phase_ada(prog, t, G)
        ctx_tiles = [(t.ctx[i * 128:(i + 1) * 128, :], t.ctxs[i * 128:(i + 1) * 128, :], 1) for i in range(LC // 128)]
        x_tiles0 = [(t.x[i * 128:(i + 1) * 128, :], t.out[i * 128:(i + 1) * 128, :], 0) for i in range(NT)]
        phase_sgu(prog, t, G, 0, 0, ctx_tiles + x_tiles0)
        import os
        STOP = int(os.environ.get("STOP", "9"))
        if nlayers >= 2:
            phase_rwkv_h(prog, t, G, 1, L)
            if STOP >= 2:
                phase_rwkv_feat(prog, t, G, L)
            if STOP >= 3:
                phase_rwkv_scan(prog, t, G, L)
            if STOP >= 4:
                phase_rwkv_out(prog, t, G, 1, L)
        if nlayers >= 3:
            phase_mla_proj(prog, t, G, 2, L)
            if STOP >= 6:
                phase_mla_attn(prog, t, G, L)
            if STOP >= 7:
                phase_mla_out(prog, t, G, 2, L)
        if nlayers >= 4:
            phase_sgu(prog, t, G, 3, 1, [(t.out[i * 128:(i + 1) * 128, :], t.out[i * 128:(i + 1) * 128, :], 0) for i in range(NT)])
        if dbg:
            ph = Phase(prog, "dbg")
            d1 = Buf(None, "d1"); d2 = Buf(None, "d2")
            ph.dma("sp", t.dbg_ctx[:, :], t.ctxs[:, :], R=[d1])
            if nlayers >= 2:
                for n in range(2):
                    ph.dma("sp", t.dbg_y[n, :, :], t.y_d[n, :, :], R=[d2])
            ph.run()
        build.n_instr = prog.n_instr
    return nc


def host_inputs(inp, b, L=L_FULL):
    f = lambda a: np.ascontiguousarray(a, dtype=np.float32)
    cond = np.stack([inp["c"][b], inp["c_ctx"]], 0)
    m = {}
    m["x"] = f(inp["x"][b][:L])
    m["ctx"] = f(inp["ctx"][b])
    m["condT"] = f(cond.reshape(2, 8, 128).transpose(2, 1, 0))
    m["ada_w"] = f(inp["ada_w"])
    m["ada_b2"] = f(np.broadcast_to(inp["ada_b"][:, None, :], (4, 2, 3 * D)))
    m["ident"] = np.eye(128, dtype=np.float32)
    sel = np.zeros((2, 2, 128), np.float32); sel[0, 0] = 1; sel[1, 1] = 1
    m["sel"] = sel
    m["sgu_w_in"] = f(inp["sgu_w_in"]); m["sgu_w_out"] = f(inp["sgu_w_out"])
    m["sgu_w_sT"] = f(inp["sgu_w_s"].transpose(0, 3, 1, 2))
    m["sgu_b_sT"] = f(inp["sgu_b_s"].transpose(0, 2, 1))
    m["sgu_gain"] = f(inp["sgu_gain"])
    m.update(rwkv_consts())
    m["rw_w_in"] = f(inp["rwkv_w_in"][0])
    m["rw_w1cat"] = f(np.concatenate([inp["rwkv_w_lora1"][0, 0], inp["rwkv_w_lora1"][0, 1]], 1))
    m["rw_a1cat"] = f(np.concatenate([inp["rwkv_a_lora1"][0, 0], inp["rwkv_a_lora1"][0, 1]], 1))
    m["rw_w2cat"] = f(inp["rwkv_w_lora2"][0].reshape(128, D))
    m["rw_a2cat"] = f(inp["rwkv_a_lora2"][0].reshape(128, D))
    m["rw_muT"] = f(inp["rwkv_mu"][0].reshape(6, 8, 128).transpose(2, 0, 1))
    m["rw_w0"] = f(inp["rwkv_w0"][0]); m["rw_a0"] = f(inp["rwkv_a0"][0])
    m["rw_k_k"] = f(inp["rwkv_k_k"][0]); m["rw_k_a"] = f(inp["rwkv_k_a"][0]); m["rw_r_k"] = f(inp["rwkv_r_k"][0].reshape(D))
    m["rw_ln_g"] = f(inp["rwkv_ln_gain"][0]); m["rw_ln_b"] = f(inp["rwkv_ln_bias"][0]); m["rw_w_out"] = f(inp["rwkv_w_out"][0])
    m["ml_w_in"] = f(inp["mla_w_in"][0]); m["ml_w_uq"] = f(inp["mla_w_uq"][0])
    ukv = inp["mla_w_ukv"][0].reshape(256, 16, 2, 128)
    m["ml_w_uk"] = f(ukv[:, :, 0, :].reshape(256, 2048)); m["ml_w_uv"] = f(ukv[:, :, 1, :].reshape(256, 2048))
    m["ml_qn"] = f(inp["mla_q_norm"][0]); m["ml_kvn"] = f(inp["mla_kv_norm"][0])
    m["ml_gq"] = f(inp["mla_qk_gain_q"][0]); m["ml_gk"] = f(inp["mla_qk_gain_k"][0]); m["ml_w_out"] = f(inp["mla_w_out"][0])
    pos = np.arange(L)
    inv = (10000.0 ** (-np.arange(16, dtype=np.float32) / 16)).astype(np.float32)
    ang = np.concatenate([(pos // 64).astype(np.float32)[:, None] * inv, (pos % 64).astype(np.float32)[:, None] * inv], -1).astype(np.float32)
    m["rope_cs"] = f(np.concatenate([np.cos(ang), np.sin(ang)], -1))
    return m


def kernel(**inputs):
    inp = {k: np.asarray(v) for k, v in inputs.items()}
    nc = build()
    in_maps = [host_inputs(inp, b) for b in range(8)]
    res = run_bass_kernel_spmd(nc, in_maps, core_ids=list(range(8)))
    return np.stack([r["out"] for r in res.results], 0).astype(np.float32)
```

```python
import bisect
from contextlib import ExitStack
import numpy as np
import concourse.bass as bass
import concourse.mybir as mybir
from concourse.bass_utils import run_bass_kernel_spmd

F32 = mybir.dt.float32
BF16 = mybir.dt.bfloat16
ALU = mybir.AluOpType
AF = mybir.ActivationFunctionType
AX = mybir.AxisListType

D = 1024
L_FULL = 8192
LC = 256
EPS = 1e-6


class Sem:
    def __init__(self, h):
        self.h = h
        self.count = 0


class Buf:
    def __init__(self, ap, name=""):
        self.ap = ap
        self.name = name
        self.w_evs = []
        self.r_evs = []
        self.dsem = None
        self.dsem_ph = None
        self.is_psum = False

    def __getitem__(self, k):
        return self.ap[k]


class Op:
    __slots__ = ("fn", "waits", "inc", "dma")

    def __init__(self, fn):
        self.fn = fn
        self.waits = []
        self.inc = False
        self.dma = None


class EngRec:
    def __init__(self, name, sem):
        self.name = name
        self.sem = sem
        self.ops = []
        self.mark_idx = []
        self.mark_tick = []
        self.seen = {}

    def ensure_tick(self, idx):
        i = bisect.bisect_left(self.mark_idx, idx)
        if i < len(self.mark_idx):
            return self.mark_tick[i]
        self.ops[idx].inc = True
        self.sem.count += 1
        self.mark_idx.append(idx)
        self.mark_tick.append(self.sem.count)
        return self.sem.count


class Prog:
    def __init__(self, nc, es, n_dma_sems=56):
        self.nc = nc
        self.es = es
        self.esem = {e: Sem(es.enter_context(nc.semaphore("s_" + e))) for e in ("pe", "act", "dve", "pool")}
        self.dsems = [Sem(es.enter_context(nc.semaphore("d%d" % i))) for i in range(n_dma_sems)]
        self.n_instr = 0
        self.nalloc = 0

    def sb(self, es, shape, dtype, name="t"):
        self.nalloc += 1
        nm = "%s%d" % (name, self.nalloc)
        t = es.enter_context(self.nc.sbuf_tensor(nm, list(shape), dtype))
        return Buf(t[tuple(slice(None) for _ in shape)], nm)

    def ps(self, es, shape, dtype, name="p"):
        self.nalloc += 1
        nm = "%s%d" % (name, self.nalloc)
        t = es.enter_context(self.nc.psum_tensor(nm, list(shape), dtype))
        bf = Buf(t[tuple(slice(None) for _ in shape)], nm)
        bf.is_psum = True
        return bf


class Phase:
    def __init__(self, prog, name):
        self.p = prog
        self.nc = prog.nc
        self.name = name
        self.es = ExitStack()
        self.eng = {e: EngRec(e, prog.esem.get(e)) for e in ("pe", "act", "dve", "pool", "sp")}
        self.free_dsems = list(prog.dsems)
        self.rr = 0

    def sb(self, shape, dtype, name="t"):
        return self.p.sb(self.es, shape, dtype, self.name + "_" + name)

    def ps(self, shape, dtype, name="p"):
        return self.p.ps(self.es, shape, dtype, self.name + "_" + name)

    def view(self, buf, key, name=""):
        return Buf(buf.ap[key], name or buf.name + "_v")

    def _wait_for(self, er, op, ev):
        if ev[-1] is not self:
            return
        if ev[0] == "c":
            fe, idx = ev[1], ev[2]
            if fe is er and er.name == "pe":
                return
            val = fe.ensure_tick(idx)
            sem = fe.sem
        else:
            sem, val = ev[1], ev[2]
        key = id(sem)
        if er.seen.get(key, 0) >= val:
            return
        er.seen[key] = val
        op.waits.append((sem.h, val))

    def _deps(self, er, op, ev, R, W):
        for b in R:
            for e in b.w_evs:
                self._wait_for(er, op, e)
            if b.is_psum:
                for e in b.r_evs:
                    if e[0] == "c" and e[1] is not er:
                        self._wait_for(er, op, e)
        for b in W:
            for e in b.w_evs:
                self._wait_for(er, op, e)
            for e in b.r_evs:
                self._wait_for(er, op, e)
        for b in R:
            if any(b is w for w in W):
                continue
            if ev[0] == "c":
                b.r_evs = [e for e in b.r_evs if not (e[0] == "c" and e[1] is ev[1]) and e[-1] is self]
            b.r_evs.append(ev)
        for b in W:
            b.w_evs = [ev]
            b.r_evs = []

    def op(self, eng, fn, R=(), W=()):
        er = self.eng[eng]
        o = Op(fn)
        er.ops.append(o)
        ev = ("c", er, len(er.ops) - 1, self)
        self._deps(er, o, ev, R, W)
        return o

    def dma(self, q, out, in_, R=(), W=(), sbuf=None, **kw):
        q = "sp"
        er = self.eng[q]
        b = sbuf or (W[0] if W else R[0])
        if b.dsem is None or b.dsem_ph is not self:
            b.dsem = self.free_dsems.pop()
            b.dsem_ph = self
        sem = b.dsem
        sem.count += 16
        o = Op(lambda e: e.dma_start(out=out, in_=in_, **kw))
        o.dma = sem.h
        er.ops.append(o)
        ev = ("d", sem, sem.count, self)
        self._deps(er, o, ev, R, W)
        return o

    def dmaq(self):
        self.rr += 1
        return ("sp", "pool")[self.rr % 2]

    def act(self, out, in_, func, R, W, **kw):
        return self.op("act", lambda e: e.activation(out=out, in_=in_, func=func, **kw), R, W)

    def mm(self, out, lhsT, rhs, start, stop, R, W):
        return self.op("pe", lambda e: e.matmul(out, lhsT=lhsT, rhs=rhs, start=start, stop=stop), R, W)

    def tr(self, out, in_, ident, R, W):
        return self.op("pe", lambda e: e.transpose(out=out, in_=in_, identity=ident), R, W)

    def tt(self, eng, out, in0, in1, op, R, W):
        return self.op(eng, lambda e: e.tensor_tensor(out=out, in0=in0, in1=in1, op=op), R, W)

    def ts(self, eng, out, in0, s1, s2, op0, op1, R, W, **kw):
        if op1 is None:
            return self.op(eng, lambda e: e.tensor_scalar(out=out, in0=in0, scalar1=s1, scalar2=None, op0=op0, **kw), R, W)
        return self.op(eng, lambda e: e.tensor_scalar(out=out, in0=in0, scalar1=s1, scalar2=s2, op0=op0, op1=op1, **kw), R, W)

    def stt(self, eng, out, in0, scalar, in1, op0, op1, R, W):
        return self.op(eng, lambda e: e.scalar_tensor_tensor(out=out, in0=in0, scalar=scalar, in1=in1, op0=op0, op1=op1), R, W)

    def copy(self, eng, out, in_, R, W):
        if eng == "act":
            return self.op("act", lambda e: e.activation(out=out, in_=in_, func=AF.Copy), R, W)
        return self.op(eng, lambda e: e.tensor_copy(out=out, in_=in_), R, W)

    def red(self, eng, out, in_, op, R, W, axis=AX.X):
        return self.op(eng, lambda e: e.tensor_reduce(out=out, in_=in_, axis=axis, op=op), R, W)

    def memset(self, eng, ap, val, W):
        return self.op(eng, lambda e: e.memset(ap, val), (), W)

    def recip(self, out, in_, R, W):
        return self.op("dve", lambda e: e.reciprocal(out=out, in_=in_), R, W)

    def rstd(self, out, mean, tmp, R, W):
        self.ts("dve", tmp, mean, EPS, None, ALU.add, None, R, W)
        self.act(tmp, tmp, AF.Sqrt, W, W)
        self.recip(out, tmp, W, W)

    def run(self):
        used = [s for s in self.p.dsems if s not in self.free_dsems]
        er = self.eng["sp"]
        o = Op(None)
        for s in used:
            self._wait_for(er, o, ("d", s, s.count, self))
        er.ops.append(o)
        with self.nc.Block() as block:
            def mk(e):
                er = self.eng[e]
                sem_h = er.sem.h if er.sem is not None else None

                def body(engine):
                    for o in er.ops:
                        for (sh, val) in o.waits:
                            engine.wait_ge(sh, val)
                        if o.fn is None:
                            continue
                        ins = o.fn(engine)
                        if o.dma is not None:
                            ins.then_inc(o.dma, 16)
                        elif o.inc:
                            ins.then_inc(sem_h, 1)
                return body
            block.tensor(mk("pe"))
            block.scalar(mk("act"))
            block.vector(mk("dve"))
            block.gpsimd(mk("pool"))
            block.sync(mk("sp"))
        for e in self.eng.values():
            self.p.n_instr += len(e.ops)
        self.es.close()


class T:
    pass


def declare(nc, L):
    t = T()

    def din(name, shape):
        setattr(t, name, nc.dram_tensor(name, list(shape), F32, kind="ExternalInput").ap())

    def scr(name, shape, dt=F32):
        setattr(t, name, nc.dram_tensor(name, list(shape), dt, kind="Internal").ap())

    din("x", [L, D]); din("ctx", [LC, D]); din("condT", [128, 8, 2])
    din("ada_w", [4, D, 3 * D]); din("ada_b2", [4, 2, 3 * D])
    din("ident", [128, 128]); din("sel", [2, 2, 128])
    din("sgu_w_in", [2, D, 6144]); din("sgu_w_out", [2, 2048, D]); din("sgu_w_sT", [2, 128, 8, 128])
    din("sgu_b_sT", [2, 128, 8]); din("sgu_gain", [2, 2048])
    t.out = nc.dram_tensor("out", [L, D], F32, kind="ExternalOutput").ap()
    scr("ctxs", [LC, D])
    scr("gate_d", [4, 2, 128, D])
    return t


def phase_ada(prog, t, G):
    ph = Phase(prog, "ada")
    scond = ph.sb([128, 8, 2], F32)
    ph.dma("sp", scond[:], t.condT[:, :, :], W=[scond])
    ph.act(scond[:], scond[:], AF.Silu, [scond], [scond])
    identf = ph.sb([128, 128], F32)
    ph.dma("pool", identf[:], t.ident[:, :], W=[identf])
    sel = ph.sb([2, 2, 128], F32)
    ph.dma("pool", sel[:], t.sel[:, :, :], W=[sel])
    wk = [ph.sb([128, 3 * D], F32, "wk") for _ in range(8)]
    b2 = ph.sb([2, 3 * D], F32)
    mrow = ph.sb([2, 3 * D], F32)
    gsb = [ph.sb([128, D], F32, "gsb") for _ in range(2)]
    pm = [ph.ps([2, 512], F32, "pm") for _ in range(6)]
    pT = ph.ps([128, 16, 2], F32, "pT")
    pg = ph.ps([128, 512], F32, "pg")
    mod = G.mod
    for li in range(4):
        for k in range(8):
            ph.dma(ph.dmaq(), wk[k][:], t.ada_w[li, k * 128:(k + 1) * 128, :], W=[wk[k]])
        ph.dma("sp", b2[:], t.ada_b2[li, :, :], W=[b2])
        for k in range(8):
            for n in range(6):
                ph.mm(pm[n][:], scond[:, k, :], wk[k][:, n * 512:(n + 1) * 512], k == 0, k == 7, [scond, wk[k]], [pm[n]])
        for n in range(6):
            ph.tt("dve", mrow[:, n * 512:(n + 1) * 512], pm[n][:], b2[:, n * 512:(n + 1) * 512], ALU.add, [pm[n], b2], [mrow])
        for j in range(16):
            ph.tr(pT[:, j, :], mrow[:, j * 128:(j + 1) * 128], identf[0:2, 0:2], [mrow, identf], [pT])
        ph.copy("dve", mod[:, li, 0:16, :], pT[:], [pT], [mod])
        ph.ts("dve", mod[:, li, 8:16, :], mod[:, li, 8:16, :], 1.0, None, ALU.add, None, [mod], [mod])
        for r in range(2):
            for half in range(2):
                ph.mm(pg[:], sel[:, r, :], mrow[:, 2048 + half * 512:2048 + (half + 1) * 512], True, True, [sel, mrow], [pg])
                ph.copy("act", gsb[r][:, half * 512:(half + 1) * 512], pg[:], [pg], [gsb[r]])
            ph.dma("pool", t.gate_d[li, r, :, :], gsb[r][:], R=[gsb[r]])
    ph.run()


def rms_to_hT(ph, G, xt, hT, pTb, li, r, sq, st, si):
    xn = G.xn
    ph.act(G.junk[:], xt[:], AF.Square, [xt], [G.junk, st], scale=1.0 / 32.0, accum_out=st[:, si:si + 1])
    ph.rstd(st[:, si + 1:si + 2], st[:, si:si + 1], st[:, si + 2:si + 3], [st], [st])
    ph.act(xn[:], xt[:], AF.Copy, [xt, st], [xn], scale=st[:, si + 1:si + 2])
    for k in range(8):
        ph.tr(pTb[:, k, :], xn[:, k * 128:(k + 1) * 128], G.identb[:], [xn, G.identb], [pTb])
    for k in range(8):
        ph.ts("dve", hT[:, k, :], pTb[:, k, :], G.mod[:, li, 8 + k, r:r + 1], G.mod[:, li, k, r:r + 1], ALU.mult, ALU.add,
              [pTb, G.mod], [hT])


def phase_sgu(prog, t, G, li, j, tiles):
    es_w = ExitStack()
    w_in = prog.sb(es_w, [128, 8, 6144], BF16, "sgu_win")
    w_out = prog.sb(es_w, [128, 16, D], BF16, "sgu_wout")
    w_sT = prog.sb(es_w, [128, 8, 128], BF16, "sgu_ws")
    b_s = prog.sb(es_w, [128, 8], F32, "sgu_bs")
    gain = prog.sb(es_w, [128, 2048], F32, "sgu_gain")
    gate = [prog.sb(es_w, [128, D], F32, "sgu_gate") for _ in range(2)]
    ph = Phase(prog, "sguw%d" % li)
    stg = [ph.sb([128, 3072], F32, "stg") for _ in range(3)]
    engs = ("act", "dve", "pool")
    n = 0
    for k in range(8):
        for half in range(2):
            s = stg[n % 3]
            ph.dma(ph.dmaq(), s[:], t.sgu_w_in[j, k * 128:(k + 1) * 128, half * 3072:(half + 1) * 3072], W=[s])
            ph.copy(engs[n % 3], w_in[:, k, half * 3072:(half + 1) * 3072], s[:], [s], [w_in])
            n += 1
    for k3 in range(0, 16, 3):
        kk = min(3, 16 - k3)
        s = stg[n % 3]
        ph.dma(ph.dmaq(), s[:, 0:kk * D].rearrange("p (a d) -> p a d", d=D),
               t.sgu_w_out[j, k3 * 128:(k3 + kk) * 128, :].rearrange("(a p) d -> p a d", p=128), W=[s])
        ph.copy(engs[n % 3], w_out[:, k3:k3 + kk, :], s[:, 0:kk * D].rearrange("p (a d) -> p a d", d=D), [s], [w_out])
        n += 1
    s = stg[n % 3]
    ph.dma("sp", s[:, 0:1024].rearrange("p (g q) -> p g q", q=128), t.sgu_w_sT[j, :, :, :], W=[s])
    ph.copy("dve", w_sT[:], s[:, 0:1024].rearrange("p (g q) -> p g q", q=128), [s], [w_sT])
    ph.dma("sp", b_s[:], t.sgu_b_sT[j, :, :], W=[b_s])
    ph.dma("pool", gain[:], t.sgu_gain[j, :].partition_broadcast(128), W=[gain])
    for r in range(2):
        ph.dma("sp", gate[r][:], t.gate_d[li, r, :, :], W=[gate[r]])
    ph.run()
    ph = Phase(prog, "sgu%d" % li)
    xts = [ph.sb([128, D], F32, "xt") for _ in range(3)]
    hTs = [ph.sb([128, 8, 128], BF16, "hT") for _ in range(2)]
    gus = [ph.sb([128, 2048], BF16, "gu") for _ in range(2)]
    gv = ph.sb([128, 2048], F32, "gv")
    szs = [ph.sb([128, 2048], BF16, "sz") for _ in range(2)]
    vns = [ph.sb([128, 2048], BF16, "vn") for _ in range(2)]
    sT = ph.sb([128, 16, 128], BF16, "sT")
    st = ph.sb([128, 16], F32, "st")
    pTa = ph.ps([128, 8, 128], BF16, "pTa")
    pTb = ph.ps([128, 8, 128], BF16, "pTb")
    pbank = [ph.ps([128, 512], F32, "pb") for _ in range(6)]
    pcnt = [0]

    def nb():
        pcnt[0] += 1
        return pbank[pcnt[0] % 6]

    def load(ti):
        ph.dma("sp", xts[ti % 3][:], tiles[ti][0], W=[xts[ti % 3]])

    def stage_n(ti):
        src, dst, r = tiles[ti]
        rms_to_hT(ph, G, xts[ti % 3], hTs[ti % 2], pTa, li, r, None, st, 0)

    def stage_a(ti):
        src, dst, r = tiles[ti]
        gu = gus[ti % 2]; sz = szs[ti % 2]; vn = vns[ti % 2]; hT = hTs[ti % 2]
        for n in range(12):
            pm = nb()
            for k in range(8):
                ph.mm(pm[:], hT[:, k, :], w_in[:, k, n * 512:(n + 1) * 512], k == 0, k == 7, [hT, w_in], [pm])
            if n < 4:
                ph.act(gu[:, n * 512:(n + 1) * 512], pm[:], AF.Gelu_apprx_tanh, [pm], [gu])
            elif n < 8:
                c = n - 4
                ph.act(gv[:, c * 512:(c + 1) * 512], pm[:], AF.Gelu_apprx_tanh, [pm], [gv])
                ph.act(G.junk[:, 0:512], gv[:, c * 512:(c + 1) * 512], AF.Square, [gv], [G.junk, st],
                       scale=1.0 / 32.0, accum_out=st[:, 4 + c:5 + c])
            else:
                c = n - 8
                ph.act(sz[:, c * 512:(c + 1) * 512], pm[:], AF.Silu, [pm], [sz])
        ph.red("dve", st[:, 8:9], st[:, 4:8], ALU.add, [st], [st])
        ph.ts("dve", st[:, 8:9], st[:, 8:9], 0.5, None, ALU.mult, None, [st], [st])
        ph.rstd(st[:, 9:10], st[:, 8:9], st[:, 10:11], [st], [st])
        ph.stt("dve", vn[:], gv[:], st[:, 9:10], gain[:], ALU.mult, ALU.mult, [gv, st, gain], [vn])
        ph.tt("dve", gu[:], gu[:], sz[:], ALU.mult, [gu, sz], [gu])

    def stage_b(ti):
        src, dst, r = tiles[ti]
        xt = xts[ti % 3]; gu = gus[ti % 2]; sz = szs[ti % 2]; vn = vns[ti % 2]
        for g in range(8):
            if g % 2 == 0:
                pm = nb()
            o = (g % 2) * 256
            ph.mm(pm[:, o:o + 256], w_sT[:, g, :], vn[:, g * 256:(g + 1) * 256], True, True, [w_sT, vn], [pm])
            if g % 2 == 1:
                for gg in (g - 1, g):
                    oo = (gg % 2) * 256
                    ph.stt("dve", sz[:, gg * 256:(gg + 1) * 256], pm[:, oo:oo + 256], b_s[:, gg:gg + 1],
                           gu[:, gg * 256:(gg + 1) * 256], ALU.add, ALU.mult, [pm, b_s, gu], [sz])
        for k in range(16):
            pt = pTa if k < 8 else pTb
            ph.tr(pt[:, k % 8, :], sz[:, k * 128:(k + 1) * 128], G.identb[:], [sz, G.identb], [pt])
        ph.copy("act", sT[:, 0:8, :], pTa[:], [pTa], [sT])
        ph.copy("act", sT[:, 8:16, :], pTb[:], [pTb], [sT])
        po = [nb(), nb()]
        for half in range(2):
            for k in range(16):
                ph.mm(po[half][:], sT[:, k, :], w_out[:, k, half * 512:(half + 1) * 512], k == 0, k == 15, [sT, w_out], [po[half]])
        for half in range(2):
            sl = slice(half * 512, (half + 1) * 512)
            tl = slice(1024 + half * 512, 1024 + (half + 1) * 512)
            ph.tt("dve", gv[:, tl], po[half][:], gate[r][:, sl], ALU.mult, [po[half], gate[r]], [gv])
            ph.tt("dve", xt[:, sl], gv[:, tl], xt[:, sl], ALU.add, [gv, xt], [xt])
        ph.dma("sp", dst, xt[:], R=[xt])
        if ti + 3 < len(tiles):
            load(ti + 3)
    NTL = len(tiles)
    for ti in range(min(3, NTL)):
        load(ti)
    stage_n(0)
    if NTL > 1:
        stage_n(1)
    stage_a(0)
    for ti in range(NTL):
        if ti + 2 < NTL:
            stage_n(ti + 2)
        if ti + 1 < NTL:
            stage_a(ti + 1)
        stage_b(ti)
    ph.run()
    es_w.close()


C0 = float(np.exp(-0.5))


class PS2:
    def __init__(self, ph):
        self.h = [ph.ps([128, 512], F32, "ps2") for _ in range(2)]

    def __getitem__(self, key):
        rows, cols = key
        hi = cols.start // 512
        assert (cols.stop - 1) // 512 == hi
        return self.h[hi][rows, cols.start - hi * 512:cols.stop - hi * 512]

HS = (slice(0, 512), slice(512, 1024))
SD = BF16


def rwkv_consts():
    idx = np.arange(128)
    out = {}
    cm = np.zeros((2, 128, 3, 128), np.float32)
    mask4 = np.zeros((2, 128, 512), np.float32)
    maskT = np.zeros((2, 128, 256), np.float32)
    for n in range(2):
        before = (idx[:, None] < idx[None, :]) if n == 0 else (idx[:, None] > idx[None, :])
        incl = before | np.eye(128, dtype=bool)
        cm[n, :, 0, :] = -C0 * incl
        cm[n, :, 1, :] = -C0 * before.T
        cm[n, :, 2, :] = -C0
        mask4[n] = np.concatenate([before, incl, before, incl], 1)
        maskT[n] = np.concatenate([before.T, before.T], 1)
    out["rw_cm"] = cm
    out["rw_mask4"] = mask4
    out["rw_maskT"] = maskT
    ir = np.zeros((64, 16, 64), np.float32)
    for h in range(16):
        ir[:, h, :] = np.eye(64)
    out["identrep"] = ir.reshape(64, 1024)
    return out


def declare_rwkv(nc, t, L):
    def din(name, shape):
        setattr(t, name, nc.dram_tensor(name, list(shape), F32, kind="ExternalInput").ap())

    def scr(name, shape, dt=F32):
        setattr(t, name, nc.dram_tensor(name, list(shape), dt, kind="Internal").ap())
    NTOK = LC + L
    din("rw_cm", [2, 128, 3, 128]); din("rw_mask4", [2, 128, 512]); din("rw_maskT", [2, 128, 256]); din("identrep", [64, 1024])
    din("rw_w_in", [4, D, D]); din("rw_w1cat", [D, 128]); din("rw_a1cat", [D, 128])
    din("rw_w2cat", [128, D]); din("rw_a2cat", [128, D]); din("rw_muT", [128, 6, 8])
    din("rw_w0", [2, D]); din("rw_a0", [2, D]); din("rw_k_k", [D]); din("rw_k_a", [D]); din("rw_r_k", [D])
    din("rw_ln_g", [D]); din("rw_ln_b", [D]); din("rw_w_out", [D, D])
    scr("hTc", [8, 128, LC + 2], BF16); scr("hTl", [8, 128, L + 2], BF16)
    scr("sig_d", [2, NTOK, D]); scr("kdir_d", [2, NTOK, D], BF16); scr("b_d", [2, NTOK, D], BF16)
    scr("kk_d", [NTOK, D], BF16); scr("v_d", [NTOK, D], BF16); scr("r_d", [NTOK, D], BF16); scr("sz_d", [NTOK, D], BF16)
    scr("bon_d", [NTOK, 16]); scr("y_d", [2, NTOK, D])


def phase_rwkv_h(prog, t, G, li, L):
    ph = Phase(prog, "rwh")
    xts = [ph.sb([128, D], F32, "xt") for _ in range(2)]
    hTs = [ph.sb([128, 8, 128], BF16, "hT") for _ in range(2)]
    st = ph.sb([128, 16], F32, "st")
    zt = ph.sb([128, 8, 1], BF16, "zt")
    pTa = ph.ps([128, 8, 128], BF16, "pTa")
    ph.memset("dve", zt[:], 0.0, [zt])
    for (dst, n) in ((t.hTc, LC), (t.hTl, L)):
        ph.dma("sp", dst[:, :, 0:1].rearrange("k p t -> p k t"), zt[:], R=[zt], allow_slow_non_contiguous=True)
        ph.dma("sp", dst[:, :, n + 1:n + 2].rearrange("k p t -> p k t"), zt[:], R=[zt], allow_slow_non_contiguous=True)
    tiles = [(t.ctxs, t.hTc, i, 1) for i in range(LC // 128)] + [(t.out, t.hTl, i, 0) for i in range(L // 128)]
    def load(ti):
        src, dst, i, r = tiles[ti]
        ph.dma("sp", xts[ti % 2][:], src[i * 128:(i + 1) * 128, :], W=[xts[ti % 2]])
    load(0)
    for ti, (src, dst, i, r) in enumerate(tiles):
        xt = xts[ti % 2]; hT = hTs[ti % 2]
        if ti + 1 < len(tiles):
            load(ti + 1)
        rms_to_hT(ph, G, xt, hT, pTa, li, r, None, st, 0)
        ph.dma("sp", dst[:, :, 1 + i * 128:1 + (i + 1) * 128].rearrange("k p t -> p k t"), hT[:], R=[hT])
    ph.run()


def phase_rwkv_feat(prog, t, G, L):
    es_w = ExitStack()
    W4 = prog.sb(es_w, [128, 4, 8, D], BF16, "rw_W4")
    w1c = prog.sb(es_w, [128, 8, 128], BF16, "rw_w1c")
    a1c = prog.sb(es_w, [128, 8, 128], BF16, "rw_a1c")
    w2c = prog.sb(es_w, [128, D], BF16, "rw_w2c")
    a2c = prog.sb(es_w, [128, D], BF16, "rw_a2c")
    muT = prog.sb(es_w, [128, 6, 8], F32, "rw_mu")
    w0b = prog.sb(es_w, [128, 2, D], F32, "rw_w0b")
    a0b = prog.sb(es_w, [128, 2, D], F32, "rw_a0b")
    kkb_ = prog.sb(es_w, [128, D], F32, "rw_kkb")
    kab = prog.sb(es_w, [128, D], F32, "rw_kab")
    rkb = prog.sb(es_w, [128, D], F32, "rw_rkb")
    ph = Phase(prog, "rwfw")
    stg = [ph.sb([128, 4, D], F32, "stg") for _ in range(2)]
    n = 0
    for c in range(4):
        for k0 in (0, 4):
            s = stg[n % 2]
            ph.dma("sp", s[:], t.rw_w_in[c, k0 * 128:(k0 + 4) * 128, :].rearrange("(a p) d -> p a d", p=128), W=[s])
            ph.copy(("act", "dve")[n % 2], W4[:, c, k0:k0 + 4, :], s[:], [s], [W4])
            n += 1
    for (src, dstb) in ((t.rw_w1cat, w1c), (t.rw_a1cat, a1c)):
        s = stg[n % 2]
        ph.dma("sp", s[:, 0, :].rearrange("p (a d) -> p a d", d=128), src.rearrange("(a p) d -> p a d", p=128), W=[s])
        ph.copy("dve", dstb[:], s[:, 0, :].rearrange("p (a d) -> p a d", d=128), [s], [dstb])
        n += 1
    for (src, dstb) in ((t.rw_w2cat, w2c), (t.rw_a2cat, a2c)):
        s = stg[n % 2]
        ph.dma("sp", s[:, 0, :], src[:, :], W=[s])
        ph.copy("dve", dstb[:], s[:, 0, :], [s], [dstb])
        n += 1
    ph.dma("sp", muT[:], t.rw_muT[:, :, :], W=[muT])
    for nn in range(2):
        ph.dma("sp", w0b[:, nn, :], t.rw_w0[nn, :].partition_broadcast(128), W=[w0b])
        ph.dma("sp", a0b[:, nn, :], t.rw_a0[nn, :].partition_broadcast(128), W=[a0b])
    ph.dma("sp", kkb_[:], t.rw_k_k.partition_broadcast(128), W=[kkb_])
    ph.dma("sp", kab[:], t.rw_k_a.partition_broadcast(128), W=[kab])
    ph.dma("sp", rkb[:], t.rw_r_k.partition_broadcast(128), W=[rkb])
    ph.run()

    ph = Phase(prog, "rwf")
    hws = [ph.sb([128, 8, 130], BF16, "hw") for _ in range(2)]
    xa = ph.sb([128, 8, 128], F32, "xa")
    xx = ph.sb([128, 8, 128], F32, "xx")
    xs = [ph.sb([128, 8, 128], BF16, "xs") for _ in range(6)]
    th = ph.sb([128, 128], BF16, "th")
    alb = ph.sb([128, 128], BF16, "alb")
    o_sig = [ph.sb([128, D], F32, "osig") for _ in range(2)]
    o_kd = [ph.sb([128, D], BF16, "okd") for _ in range(2)]
    o_b = [ph.sb([128, D], BF16, "ob") for _ in range(2)]
    o_kk = ph.sb([128, D], BF16, "okk")
    o_v = ph.sb([128, D], BF16, "ov")
    o_r = ph.sb([128, D], BF16, "or")
    o_sz = ph.sb([128, D], BF16, "osz")
    o_bon = ph.sb([128, 16], F32, "obon")
    rrk = ph.sb([128, D], F32, "rrk")
    tkk = ph.sb([128, D], F32, "tkk")
    kf = ph.sb([128, D], F32, "kf")
    tmp = ph.sb([128, D], F32, "tmp")
    an = ph.sb([128, D], F32, "an")
    kdf = ph.sb([128, D], F32, "kdf")
    st = ph.sb([128, 64], F32, "st")
    pa = [PS2(ph) for _ in range(3)]
    p1 = ph.ps([128, 256], F32, "p1")
    h3 = lambda ap: ap.rearrange("p (h k) -> p h k", k=64)
    tiles = [(t.hTc, i, i) for i in range(LC // 128)] + [(t.hTl, i, LC // 128 + i) for i in range(L // 128)]
    npa = 0
    import os
    CUT = int(os.environ.get("CUT", "99"))
    for ti, (src, i, g) in enumerate(tiles):
        if CUT == 0:
            break
        hw = hws[ti % 2]
        rows = slice(g * 128, (g + 1) * 128)
        if ti == 0:
            ph.dma("sp", hw[:], src[:, :, i * 128:i * 128 + 130].rearrange("k p t -> p k t"), W=[hw])
        if ti + 1 < len(tiles):
            s2, i2, g2 = tiles[ti + 1]
            ph.dma("sp", hws[(ti + 1) % 2][:], s2[:, :, i2 * 128:i2 * 128 + 130].rearrange("k p t -> p k t"), W=[hws[(ti + 1) % 2]])
        ph.tt("dve", xa[:], hw[:, :, 0:128], hw[:, :, 2:130], ALU.add, [hw], [xa])
        ph.stt("dve", xx[:], xa[:], 0.5, hw[:, :, 1:129], ALU.mult, ALU.subtract, [xa, hw], [xx])
        for c in range(6):
            for k in range(8):
                ph.stt("dve", xs[c][:, k, :], xx[:, k, :], muT[:, c, k:k + 1], hw[:, k, 1:129], ALU.mult, ALU.add, [xx, muT, hw], [xs[c]])
        if CUT == 1:
            continue
        for k in range(8):
            ph.mm(p1[:, 0:128], w1c[:, k, :], xs[4][:, k, :], k == 0, k == 7, [w1c, xs[4]], [p1])
        for k in range(8):
            ph.mm(p1[:, 128:256], a1c[:, k, :], xs[5][:, k, :], k == 0, k == 7, [a1c, xs[5]], [p1])
        ph.act(th[:], p1[:, 0:128], AF.Tanh, [p1], [th])
        ph.copy("act", alb[:], p1[:, 128:256], [p1], [alb])
        if CUT == 2:
            continue
        pcs = []
        for c in range(4):
            p = pa[npa % 3]; npa += 1
            for half in range(2):
                for k in range(8):
                    ph.mm(p[:, half * 512:(half + 1) * 512], xs[c][:, k, :], W4[:, c, k, half * 512:(half + 1) * 512], k == 0, k == 7, [xs[c], W4], p.h)
            for hs in HS:
                if CUT == 30:
                    continue
                if c == 0:
                    ph.copy("act", o_r[:, hs], p[:, hs], p.h, [o_r])
                    if CUT != 31:
                        ph.tt("dve", rrk[:, hs], p[:, hs], rkb[:, hs], ALU.mult, p.h + [rkb], [rrk])
                elif c == 1:
                    ph.copy("act", kf[:, hs], p[:, hs], p.h, [kf])
                    if CUT != 31:
                        ph.tt("dve", tkk[:, hs], p[:, hs], kkb_[:, hs], ALU.mult, p.h + [kkb_], [tkk])
                elif c == 2:
                    ph.copy("act", o_v[:, hs], p[:, hs], p.h, [o_v])
                else:
                    ph.act(o_sz[:, hs], p[:, hs], AF.Silu, p.h, [o_sz])
        if CUT in (3, 30, 31):
            continue
        ph.tt("dve", tmp[:], tkk[:], tkk[:], ALU.mult, [tkk], [tmp])
        ph.red("dve", st[:, 0:16], h3(tmp[:]), ALU.add, [tmp], [st])
        ph.ts("dve", st[:, 0:16], st[:, 0:16], 1e-12, None, ALU.add, None, [st], [st])
        ph.act(st[:, 0:16], st[:, 0:16], AF.Sqrt, [st], [st])
        ph.recip(st[:, 16:32], st[:, 0:16], [st], [st])
        ph.tt("dve", h3(o_kk[:]), h3(tkk[:]), st[:, 16:32].unsqueeze(2).broadcast_to([128, 16, 64]), ALU.mult, [tkk, st], [o_kk])
        if CUT == 4:
            continue
        for n in range(2):
            p = pa[npa % 3]; npa += 1
            for half in range(2):
                ph.mm(p[:, half * 512:(half + 1) * 512], th[64 * n:64 * n + 64, :], w2c[64 * n:64 * n + 64, half * 512:(half + 1) * 512], True, True, [th, w2c], p.h)
            for hs in HS:
                ph.tt("dve", tmp[:, hs], p[:, hs], w0b[:, n, hs], ALU.add, p.h + [w0b], [tmp])
            ph.act(o_sig[n][:], tmp[:], AF.Sigmoid, [tmp], [o_sig[n]])
            p = pa[npa % 3]; npa += 1
            for half in range(2):
                ph.mm(p[:, half * 512:(half + 1) * 512], alb[64 * n:64 * n + 64, :], a2c[64 * n:64 * n + 64, half * 512:(half + 1) * 512], True, True, [alb, a2c], p.h)
            for hs in HS:
                ph.tt("dve", tmp[:, hs], p[:, hs], a0b[:, n, hs], ALU.add, p.h + [a0b], [tmp])
            ph.act(an[:], tmp[:], AF.Sigmoid, [tmp], [an])
            ph.stt("dve", tmp[:], an[:], -1.0, kab[:], ALU.add, ALU.mult, [an, kab], [tmp])
            ph.stt("dve", kdf[:], tmp[:], 1.0, kf[:], ALU.add, ALU.mult, [tmp, kf], [kdf])
            ph.copy("act", o_kd[n][:], kdf[:], [kdf], [o_kd[n]])
            ph.tt("dve", o_b[n][:], o_kk[:], an[:], ALU.mult, [o_kk, an], [o_b[n]])
            ph.tt("dve", tmp[:], rrk[:], kdf[:], ALU.mult, [rrk, kdf], [tmp])
            ph.red("dve", st[:, 32 + 16 * n:48 + 16 * n], h3(tmp[:]), ALU.add, [tmp], [st])
        if CUT == 5:
            continue
        ph.tt("dve", o_bon[:], st[:, 32:48], st[:, 48:64], ALU.add, [st], [o_bon])
        for n in range(2):
            ph.dma("sp", t.sig_d[n, rows, :], o_sig[n][:], R=[o_sig[n]])
            ph.dma("sp", t.kdir_d[n, rows, :], o_kd[n][:], R=[o_kd[n]])
            ph.dma("sp", t.b_d[n, rows, :], o_b[n][:], R=[o_b[n]])
        ph.dma("sp", t.kk_d[rows, :], o_kk[:], R=[o_kk])
        ph.dma("sp", t.v_d[rows, :], o_v[:], R=[o_v])
        ph.dma("sp", t.r_d[rows, :], o_r[:], R=[o_r])
        ph.dma("sp", t.sz_d[rows, :], o_sz[:], R=[o_sz])
        ph.dma("sp", t.bon_d[rows, :], o_bon[:], R=[o_bon])
    ph.run()
    es_w.close()


def phase_rwkv_scan(prog, t, G, L):
    ph = Phase(prog, "rws")
    NT = (LC + L) // 128
    nct = LC // 128
    order = [list(range(NT)), list(range(nct - 1, -1, -1)) + list(range(NT - 1, nct - 1, -1))]
    cmf = ph.sb([128, 2, 3, 128], F32, "cm")
    mask4 = ph.sb([128, 2, 512], F32, "mask4")
    maskT = ph.sb([128, 2, 256], F32, "maskT")
    idrep = ph.sb([64, D], F32, "idrep")
    identS = ph.sb([128, 128], SD, "identS")
    for n in range(2):
        ph.dma("sp", cmf[:, n, :, :], t.rw_cm[n, :, :, :], W=[cmf])
        ph.dma("sp", mask4[:, n, :], t.rw_mask4[n, :, :], W=[mask4])
        ph.dma("sp", maskT[:, n, :], t.rw_maskT[n, :, :], W=[maskT])
    ph.dma("sp", idrep[:], t.identrep[:, :], W=[idrep])
    ph.copy("dve", identS[:], G.identf[:], [G.identf], [identS])
    def mk(shape, dt, nm, k=2):
        return [ph.sb(shape, dt, nm) for _ in range(k)]
    i_sig = [mk([128, D], F32, "isig", 1) * 2 for _ in range(2)]
    i_kk = [mk([128, D], BF16, "ikk", 1) * 2 for _ in range(2)]
    i_b = [mk([128, D], BF16, "ib", 1) * 2 for _ in range(2)]
    i_kd = [mk([128, D], BF16, "ikd", 1) * 2 for _ in range(2)]
    i_v = [mk([128, D], BF16, "iv") for _ in range(2)]
    i_r = [mk([128, D], BF16, "ir", 1) * 2 for _ in range(2)]
    Gt = ph.sb([128, D], F32, "Gt")
    t1 = ph.sb([128, D], F32, "t1")
    TMa = ph.sb([128, D], SD, "TMa"); TMr = [ph.sb([128, D], SD, "TMr") for _ in range(2)]; TMb = ph.sb([128, D], SD, "TMb"); TMk = ph.sb([128, D], SD, "TMk")
    bh = [ph.sb([128, D], SD, "bh") for _ in range(2)]
    kh = [ph.sb([128, D], SD, "kh") for _ in range(2)]
    DG = [ph.sb([64, D], F32, "DG") for _ in range(2)]
    XT = [[ph.sb([128, 4, 128], SD, "XT") for _ in range(8)] for _ in range(2)]
    Zbig = [ph.sb([128, 8, 2, 2, 64], SD, "Z") for _ in range(2)]
    Zv = [[ph.view(Zbig[n], (slice(None), hp)) for hp in range(8)] for n in range(2)]
    GR = [[ph.sb([128, 512], SD, "GR") for _ in range(2)] for _ in range(8)]
    P0 = [ph.sb([128, 2, 128], SD, "P0") for _ in range(8)]
    QP = [[ph.sb([128, 4, 128], SD, "QP") for _ in range(2)] for _ in range(8)]
    RhT = [[ph.sb([64, 2, 128], SD, "RhT") for _ in range(8)] for _ in range(2)]
    YU = [[ph.sb([128, 128], F32, "YU") for _ in range(8)] for _ in range(2)]
    SU = [[ph.sb([64, 2, 64], F32, "SU") for _ in range(8)] for _ in range(2)]
    ACT_ = [[ph.sb([64, 2, 64], SD, "ACT") for _ in range(8)] for _ in range(2)]
    Ss = [[ph.sb([64, 16, 64], SD, "Ss") for _ in range(2)] for _ in range(2)]
    Sv = [[[ph.view(Ss[n][b], (slice(None), slice(2 * hp, 2 * hp + 2))) for hp in range(8)] for b in range(2)] for n in range(2)]
    Yt = [mk([128, D], F32, "Yt") for _ in range(2)]
    Yv = [[[ph.view(Yt[n][b], (slice(None), slice(hp * 128, (hp + 1) * 128))) for hp in range(8)] for b in range(2)] for n in range(2)]
    pL = PS2(ph)
    pool = [ph.ps([128, 512], F32, "pp") for _ in range(5)]
    pTt = ph.ps([128, 4, 128], SD, "pTt")
    cnt = [0]

    def bank():
        cnt[0] += 1
        return pool[cnt[0] % 5]
    for n in range(2):
        ph.memset("dve", Ss[n][0][:], 0.0, Sv[n][0])
    h4 = lambda ap: ap.rearrange("p (a b k) -> p a b k", b=2, k=64)
    def pre(n, s):
        g = order[n][s]
        rows = slice(g * 128, (g + 1) * 128)
        sb_ = s % 2
        sig = i_sig[n][sb_]; kkt = i_kk[n][sb_]; bt = i_b[n][sb_]; kdt = i_kd[n][sb_]; vt = i_v[n][sb_]; rt = i_r[n][sb_]
        ph.dma("sp", sig[:], t.sig_d[n, rows, :], W=[sig])
        yield
        ph.dma("sp", kkt[:], t.kk_d[rows, :], W=[kkt])
        yield
        ph.dma("sp", bt[:], t.b_d[n, rows, :], W=[bt])
        yield
        ph.dma("sp", kdt[:], t.kdir_d[n, rows, :], W=[kdt])
        yield
        ph.dma("sp", vt[:], t.v_d[rows, :], W=[vt])
        yield
        ph.dma("sp", rt[:], t.r_d[rows, :], W=[rt])
        yield
        for half in range(2):
            ph.mm(pL[:, half * 512:(half + 1) * 512], cmf[:, n, 0, :], sig[:, half * 512:(half + 1) * 512], True, True, [cmf, sig], pL.h)
            yield
        for hs in HS:
            ph.act(Gt[:, hs], pL[:, hs], AF.Exp, pL.h, [Gt])
            yield
        ph.tt("dve", TMr[n][:], rt[:], Gt[:], ALU.mult, [rt, Gt], [TMr[n]])
        yield
        for hs in HS:
            ph.act(Gt[:, hs], pL[:, hs], AF.Exp, pL.h, [Gt], scale=-1.0)
            yield
        ph.tt("dve", TMb[:], bt[:], Gt[:], ALU.mult, [bt, Gt], [TMb])
        yield
        ph.tt("dve", TMk[:], kdt[:], Gt[:], ALU.mult, [kdt, Gt], [TMk])
        yield
        for hs in HS:
            ph.stt("dve", t1[:, hs], sig[:, hs], C0, pL[:, hs], ALU.mult, ALU.add, [sig] + pL.h, [t1])
            yield
        ph.act(Gt[:], t1[:], AF.Exp, [t1], [Gt])
        yield
        ph.stt("dve", TMa[:], kkt[:], -1.0, Gt[:], ALU.mult, ALU.mult, [kkt, Gt], [TMa])
        yield
        ph.copy("act", Zbig[n][:, :, :, 0, :], h4(TMa[:]), [TMa], Zv[n])
        yield
        for half in range(2):
            ph.mm(pL[:, half * 512:(half + 1) * 512], cmf[:, n, 1, :], sig[:, half * 512:(half + 1) * 512], True, True, [cmf, sig], pL.h)
            yield
        for hs in HS:
            ph.act(Gt[:, hs], pL[:, hs], AF.Exp, pL.h, [Gt])
            yield
        ph.tt("dve", bh[n][:], bt[:], Gt[:], ALU.mult, [bt, Gt], [bh[n]])
        yield
        ph.tt("dve", kh[n][:], kdt[:], Gt[:], ALU.mult, [kdt, Gt], [kh[n]])
        yield
        for half in range(2):
            ph.mm(pL[:, half * 512:(half + 1) * 512], cmf[:, n, 2, :], sig[:, half * 512:(half + 1) * 512], True, True, [cmf, sig], pL.h)
            yield
        for hs in HS:
            ph.act(Gt[0:64, hs], pL[0:64, hs], AF.Exp, pL.h, [Gt])
            yield
        ph.tt("dve", DG[n][:], Gt[0:64, :], idrep[:], ALU.mult, [Gt, idrep], [DG[n]])
        yield
        for hp in range(8):
            cs = slice(hp * 128, (hp + 1) * 128)
            for j, TM in enumerate((TMa, TMr[n], TMb, TMk)):
                ph.tr(pTt[:, j, :], TM[:, cs], identS[:], [TM, identS], [pTt])
                yield
            ph.copy("act", XT[n][hp][:], pTt[:], [pTt], [XT[n][hp]])
            yield
    import os
    CUT2 = int(os.environ.get("CUT2", "99"))
    groups = [(s, n) for s in range(NT) for n in range(2)]
    pend = [None]

    def advance(k):
        g_ = pend[0]
        if g_ is None:
            return
        for _ in range(k):
            try:
                next(g_)
            except StopIteration:
                pend[0] = None
                return
    pend[0] = pre(groups[0][1], groups[0][0])
    advance(100000)
    for gi, (s, n) in enumerate(groups):
            g = order[n][s]
            rows = slice(g * 128, (g + 1) * 128)
            sb_ = s % 2
            vt = i_v[n][sb_]
            advance(100000)
            if gi + 1 < len(groups):
                pend[0] = pre(groups[gi + 1][1], groups[gi + 1][0])
            if CUT2 == 1:
                continue
            for hp in range(8):
                X = XT[n][hp]
                for hh in range(2):
                    b0 = 64 * hh
                    h = 2 * hp + hh
                    g1 = bank()
                    AR = X[b0:b0 + 64, 0:2, :].rearrange("p a t -> p (a t)")
                    ph.mm(g1[:, 0:256], X[b0:b0 + 64, 2, :], AR, True, True, [X], [g1])
                    ph.mm(g1[:, 256:512], X[b0:b0 + 64, 3, :], AR, True, True, [X], [g1])
                    ph.tt("dve", GR[hp][hh][:], g1[:], mask4[:, n, :], ALU.mult, [g1, mask4], [GR[hp][hh]])
                    g3 = bank()
                    ph.mm(g3[:, 0:128], X[b0:b0 + 64, 0, :], X[b0:b0 + 64, 2, :], True, True, [X], [g3])
                    ph.tt("dve", P0[hp][:, hh, :], g3[:, 0:128], maskT[:, n, 0:128], ALU.mult, [g3, maskT], [P0[hp]])
                    ph.mm(g3[:, 128:192], GR[hp][hh][:, 256:384], vt[:, h * 64:(h + 1) * 64], True, True, [GR[hp][hh], vt], [g3])
                    ph.copy("act", Zv[n][hp][:, hh, 1, :], g3[:, 128:192], [g3], [Zv[n][hp]])
            if CUT2 in (2, 20, 21, 22):
                continue
            for j in range(7):
                for hp in range(8):
                    if j == 0:
                        Qs = [GR[hp][hh][:, 0:128] for hh in range(2)]
                        Ps = [P0[hp][:, hh, :] for hh in range(2)]
                        qb = [GR[hp][0], GR[hp][1], P0[hp]]
                    else:
                        cur = QP[hp][(j - 1) % 2]
                        Qs = [cur[:, hh, :] for hh in range(2)]
                        Ps = [cur[:, 2 + hh, :] for hh in range(2)]
                        qb = [cur]
                    zp = bank()
                    for hh in range(2):
                        ph.mm(zp[:, hh * 128:(hh + 1) * 128], Qs[hh], Zv[n][hp][:, hh, :, :].rearrange("p a k -> p (a k)"), True, True, qb + [Zv[n][hp]], [zp])
                    zv = Zv[n][hp][:, :, :, :].rearrange("p b a k -> p (b a k)")
                    ph.tt("dve", zv, zv, zp[:, 0:256], ALU.add, [zp, Zv[n][hp]], [Zv[n][hp]])
                    if j < 6:
                        nx = QP[hp][j % 2]
                        qp = bank()
                        for hh in range(2):
                            ph.mm(qp[:, hh * 128:(hh + 1) * 128], Ps[hh], Qs[hh], True, True, qb, [qp])
                            ph.mm(qp[:, 256 + hh * 128:256 + (hh + 1) * 128], Qs[hh], Ps[hh], True, True, qb, [qp])
                        ph.copy("act", nx[:].rearrange("p a b -> p (a b)"), qp[:], [qp], [nx])
                    advance(2)
            if CUT2 == 3:
                continue
            for hp in range(8):
                f1 = bank()
                for hh in range(2):
                    h = 2 * hp + hh
                    nbr = GR[hp][hh][:, 128:256]
                    ph.mm(f1[0:64, hh * 128:(hh + 1) * 128], Zv[n][hp][:, hh, 0, :], nbr, True, False, [Zv[n][hp], GR[hp][hh]], [f1])
                    ph.mm(f1[0:64, hh * 128:(hh + 1) * 128], TMr[n][:, h * 64:(h + 1) * 64], identS[:], False, True, [TMr[n], identS], [f1])
                ph.copy("act", RhT[n][hp][:].rearrange("p a b -> p (a b)"), f1[0:64, 0:256], [f1], [RhT[n][hp]])
                f2 = bank()
                for hh in range(2):
                    h = 2 * hp + hh
                    nbr = GR[hp][hh][:, 128:256]
                    nkr = GR[hp][hh][:, 384:512]
                    cu = Zv[n][hp][:, hh, 1, :]
                    ph.mm(f2[:, hh * 64:(hh + 1) * 64], nbr, cu, True, False, [GR[hp][hh], Zv[n][hp]], [f2])
                    ph.mm(f2[:, hh * 64:(hh + 1) * 64], nkr, vt[:, h * 64:(h + 1) * 64], False, True, [GR[hp][hh], vt], [f2])
                    ph.mm(f2[0:64, 128 + hh * 64:128 + (hh + 1) * 64], bh[n][:, h * 64:(h + 1) * 64], cu, True, False, [bh[n], Zv[n][hp]], [f2])
                    ph.mm(f2[0:64, 128 + hh * 64:128 + (hh + 1) * 64], kh[n][:, h * 64:(h + 1) * 64], vt[:, h * 64:(h + 1) * 64], False, True, [kh[n], vt], [f2])
                    ph.mm(f2[0:64, 256 + hh * 64:256 + (hh + 1) * 64], Zv[n][hp][:, hh, 0, :], bh[n][:, h * 64:(h + 1) * 64], True, True, [Zv[n][hp], bh[n]], [f2])
                ph.copy("act", YU[n][hp][:], f2[:, 0:128], [f2], [YU[n][hp]])
                ph.copy("act", SU[n][hp][:].rearrange("p a b -> p (a b)"), f2[0:64, 128:256], [f2], [SU[n][hp]])
                ph.tt("dve", ACT_[n][hp][:].rearrange("p a b -> p (a b)"), f2[0:64, 256:384], DG[n][:, hp * 128:(hp + 1) * 128], ALU.add, [f2, DG[n]], [ACT_[n][hp]])
            if CUT2 == 4:
                continue
            Sc = Sv[n][s % 2]; Sn = Sv[n][(s + 1) % 2]
            Y = Yv[n][s % 2]
            for hp in range(8):
                f4 = bank()
                for hh in range(2):
                    ph.mm(f4[:, hh * 64:(hh + 1) * 64], RhT[n][hp][:, hh, :], Sc[hp][:, hh, :], True, True, [RhT[n][hp], Sc[hp]], [f4])
                    ph.mm(f4[0:64, 128 + hh * 64:128 + (hh + 1) * 64], ACT_[n][hp][:, hh, :], Sc[hp][:, hh, :], True, True, [ACT_[n][hp], Sc[hp]], [f4])
                ph.tt("dve", Y[hp][:], f4[:, 0:128], YU[n][hp][:], ALU.add, [f4, YU[n][hp]], [Y[hp]])
                ph.tt("dve", Sn[hp][:].rearrange("p a b -> p (a b)"), f4[0:64, 128:256], SU[n][hp][:].rearrange("p a b -> p (a b)"), ALU.add, [f4, SU[n][hp]], [Sn[hp]])
            ph.dma("sp", t.y_d[n, rows, :], Yt[n][s % 2][:], R=Y, sbuf=Yt[n][s % 2])
    ph.run()


def phase_rwkv_out(prog, t, G, li, L):
    es_w = ExitStack()
    w_out = prog.sb(es_w, [128, 8, D], BF16, "rwo_w")
    lng = prog.sb(es_w, [128, D], F32, "rwo_g")
    lnb = prog.sb(es_w, [128, D], F32, "rwo_b")
    gate = [prog.sb(es_w, [128, D], F32, "rwo_gate") for _ in range(2)]
    ph = Phase(prog, "rwow")
    stg = [ph.sb([128, 4, D], F32, "stg") for _ in range(2)]
    for n, k0 in enumerate((0, 4)):
        s = stg[n]
        ph.dma("sp", s[:], t.rw_w_out[k0 * 128:(k0 + 4) * 128, :].rearrange("(a p) d -> p a d", p=128), W=[s])
        ph.copy(("act", "dve")[n], w_out[:, k0:k0 + 4, :], s[:], [s], [w_out])
    ph.dma("sp", lng[:], t.rw_ln_g.partition_broadcast(128), W=[lng])
    ph.dma("sp", lnb[:], t.rw_ln_b.partition_broadcast(128), W=[lnb])
    for r in range(2):
        ph.dma("sp", gate[r][:], t.gate_d[li, r, :, :], W=[gate[r]])
    ph.run()
    ph = Phase(prog, "rwo")
    y0 = [ph.sb([128, D], F32, "y0") for _ in range(2)]
    y1 = [ph.sb([128, D], F32, "y1") for _ in range(2)]
    vt = [ph.sb([128, D], BF16, "v") for _ in range(2)]
    szt = [ph.sb([128, D], BF16, "sz") for _ in range(2)]
    bon = [ph.sb([128, 16], F32, "bon") for _ in range(2)]
    xts = [ph.sb([128, D], F32, "xt") for _ in range(2)]
    tmp = ph.sb([128, D], F32, "tmp")
    ob = ph.sb([128, D], BF16, "ob")
    oT = ph.sb([128, 8, 128], BF16, "oT")
    st = ph.sb([128, 64], F32, "st")
    pT = ph.ps([128, 8, 128], BF16, "pT")
    po = [ph.ps([128, 512], F32, "po") for _ in range(2)]
    h3 = lambda ap: ap.rearrange("p (h k) -> p h k", k=64)
    bc = lambda ap: ap.unsqueeze(2).broadcast_to([128, 16, 64])
    tiles = [(t.ctxs, i, i, 1) for i in range(LC // 128)] + [(t.out, i, LC // 128 + i, 0) for i in range(L // 128)]
    def load(ti):
        xs_, i, g, r = tiles[ti]
        rows = slice(g * 128, (g + 1) * 128)
        b_ = ti % 2
        ph.dma("sp", y0[b_][:], t.y_d[0, rows, :], W=[y0[b_]])
        ph.dma("sp", y1[b_][:], t.y_d[1, rows, :], W=[y1[b_]])
        ph.dma("sp", vt[b_][:], t.v_d[rows, :], W=[vt[b_]])
        ph.dma("sp", szt[b_][:], t.sz_d[rows, :], W=[szt[b_]])
        ph.dma("sp", bon[b_][:], t.bon_d[rows, :], W=[bon[b_]])
        ph.dma("sp", xts[b_][:], xs_[i * 128:(i + 1) * 128, :], W=[xts[b_]])
    load(0)
    for ti, (xs_, i, g, r) in enumerate(tiles):
        rows = slice(g * 128, (g + 1) * 128)
        b_ = ti % 2
        ya = y0[b_]; yb = y1[b_]; v = vt[b_]; sz = szt[b_]; bo = bon[b_]; xt = xts[b_]
        if ti + 1 < len(tiles):
            load(ti + 1)
        ph.tt("dve", ya[:], ya[:], yb[:], ALU.add, [ya, yb], [ya])
        ph.red("dve", st[:, 0:16], h3(ya[:]), ALU.add, [ya], [st])
        ph.ts("dve", st[:, 0:16], st[:, 0:16], 1.0 / 64, None, ALU.mult, None, [st], [st])
        ph.tt("dve", h3(ya[:]), h3(ya[:]), bc(st[:, 0:16]), ALU.subtract, [ya, st], [ya])
        ph.tt("dve", tmp[:], ya[:], ya[:], ALU.mult, [ya], [tmp])
        ph.red("dve", st[:, 16:32], h3(tmp[:]), ALU.add, [tmp], [st])
        ph.ts("dve", st[:, 16:32], st[:, 16:32], 1.0 / 64, 64e-5, ALU.mult, ALU.add, [st], [st])
        ph.act(st[:, 16:32], st[:, 16:32], AF.Sqrt, [st], [st])
        ph.recip(st[:, 32:48], st[:, 16:32], [st], [st])
        ph.tt("dve", h3(ya[:]), h3(ya[:]), bc(st[:, 32:48]), ALU.mult, [ya, st], [ya])
        ph.tt("dve", ya[:], ya[:], lng[:], ALU.mult, [ya, lng], [ya])
        ph.tt("dve", ya[:], ya[:], lnb[:], ALU.add, [ya, lnb], [ya])
        ph.tt("dve", h3(tmp[:]), h3(v[:]), bc(bo[:]), ALU.mult, [v, bo], [tmp])
        ph.tt("dve", ya[:], ya[:], tmp[:], ALU.add, [ya, tmp], [ya])
        ph.tt("dve", ob[:], ya[:], sz[:], ALU.mult, [ya, sz], [ob])
        for k in range(8):
            ph.tr(pT[:, k, :], ob[:, k * 128:(k + 1) * 128], G.identb[:], [ob, G.identb], [pT])
        ph.copy("act", oT[:], pT[:], [pT], [oT])
        for half in range(2):
            for k in range(8):
                ph.mm(po[half][:], oT[:, k, :], w_out[:, k, half * 512:(half + 1) * 512], k == 0, k == 7, [oT, w_out], [po[half]])
        for half in range(2):
            sl = slice(half * 512, (half + 1) * 512)
            ph.tt("dve", tmp[:, sl], po[half][:], gate[r][:, sl], ALU.mult, [po[half], gate[r]], [tmp])
            ph.tt("dve", xt[:, sl], tmp[:, sl], xt[:, sl], ALU.add, [tmp, xt], [xt])
        ph.dma("sp", xs_[i * 128:(i + 1) * 128, :], xt[:], R=[xt])
    ph.run()
    es_w.close()

MLA_SCALE_ = 192 ** -0.5


def declare_mla(nc, t, L):
    def din(name, shape):
        setattr(t, name, nc.dram_tensor(name, list(shape), F32, kind="ExternalInput").ap())

    def scr(name, shape, dt=F32):
        setattr(t, name, nc.dram_tensor(name, list(shape), dt, kind="Internal").ap())
    NTOK = LC + L
    din("ml_w_in", [D, 3136]); din("ml_w_uq", [768, 3072]); din("ml_w_uk", [256, 2048]); din("ml_w_uv", [256, 2048])
    din("ml_qn", [768]); din("ml_kvn", [256]); din("ml_gq", [192]); din("ml_gk", [192]); din("ml_w_out", [2048, D])
    din("rope_cs", [L, 64])
    scr("QT_d", [16, 2, 128, L], BF16); scr("KnT_d", [16, 128, NTOK], BF16); scr("KrT_d", [64, NTOK], BF16)
    scr("V_d", [NTOK, 2048], BF16); scr("rk_d", [NTOK, 16]); scr("msz_d", [L, 2048], BF16); scr("o_d", [L, 2048])


def phase_mla_proj(prog, t, G, li, L):
    es_w = ExitStack()
    w_in = prog.sb(es_w, [128, 8, 3136], BF16, "ml_win")
    w_uq = prog.sb(es_w, [128, 6, 3072], BF16, "ml_wuq")
    w_uk = prog.sb(es_w, [128, 2, 2048], BF16, "ml_wuk")
    w_uv = prog.sb(es_w, [128, 2, 2048], BF16, "ml_wuv")
    qnb = prog.sb(es_w, [128, 768], F32, "ml_qnb")
    kvnb = prog.sb(es_w, [128, 256], F32, "ml_kvnb")
    gq = prog.sb(es_w, [128, 192], F32, "ml_gq")
    gk = prog.sb(es_w, [128, 192], F32, "ml_gk")
    ones = prog.sb(es_w, [128, 1], BF16, "ml_ones")
    ph = Phase(prog, "mlw")
    stg = [ph.sb([128, 3136], F32, "stg") for _ in range(2)]
    n = 0
    engs = ("act", "dve")
    for k in range(8):
        s = stg[n % 2]
        ph.dma("sp", s[:], t.ml_w_in[k * 128:(k + 1) * 128, :], W=[s])
        ph.copy(engs[n % 2], w_in[:, k, :], s[:], [s], [w_in]); n += 1
    for k in range(6):
        s = stg[n % 2]
        ph.dma("sp", s[:, 0:3072], t.ml_w_uq[k * 128:(k + 1) * 128, :], W=[s])
        ph.copy(engs[n % 2], w_uq[:, k, :], s[:, 0:3072], [s], [w_uq]); n += 1
    for (src, dst) in ((t.ml_w_uk, w_uk), (t.ml_w_uv, w_uv)):
        for k in range(2):
            s = stg[n % 2]
            ph.dma("sp", s[:, 0:2048], src[k * 128:(k + 1) * 128, :], W=[s])
            ph.copy(engs[n % 2], dst[:, k, :], s[:, 0:2048], [s], [dst]); n += 1
    ph.dma("sp", qnb[:], t.ml_qn.partition_broadcast(128), W=[qnb])
    ph.dma("sp", kvnb[:], t.ml_kvn.partition_broadcast(128), W=[kvnb])
    ph.dma("sp", gq[:], t.ml_gq.partition_broadcast(128), W=[gq])
    ph.dma("sp", gk[:], t.ml_gk.partition_broadcast(128), W=[gk])
    ph.tt("dve", gq[:, 0:128], gq[:, 0:128], gk[:, 0:128], ALU.mult, [gq, gk], [gq])
    ph.memset("dve", ones[:], 1.0, [ones])
    ph.run()

    ph = Phase(prog, "mlp")
    xts = [ph.sb([128, D], F32, "xt") for _ in range(2)]
    hT = ph.sb([128, 8, 128], BF16, "hT")
    cqkv = ph.sb([128, 1088], F32, "cqkv")
    cn = ph.sb([128, 1024], BF16, "cn")
    cT = ph.sb([128, 8, 128], BF16, "cT")
    q = ph.sb([128, 3072], F32, "q")
    tmp = ph.sb([128, 3072], F32, "tmp")
    Qt = ph.sb([128, 16, 256], BF16, "Qt")
    QTt = ph.sb([128, 32, 128], BF16, "QTt")
    KnT = ph.sb([128, 16, 128], BF16, "KnT")
    sq = ph.sb([128, 16, 128], BF16, "sq")
    Vt = ph.sb([128, 2048], BF16, "Vt")
    szt = ph.sb([128, 2048], BF16, "szt")
    cs = [ph.sb([128, 64], F32, "cs") for _ in range(2)]
    krg = ph.sb([128, 64], F32, "krg")
    KRt = ph.sb([128, 64], BF16, "KRt")
    KRT = ph.sb([64, 128], BF16, "KRT")
    rk = ph.sb([128, 16], F32, "rk")
    st = ph.sb([128, 64], F32, "st")
    r4 = ph.sb([128, 4, 16, 32], F32, "r4")
    pT = [ph.ps([128, 8, 128], BF16, "pT") for _ in range(2)]
    pc = [ph.ps([128, 512], F32, "pc") for _ in range(3)]
    pk = [ph.ps([128, 4, 128], F32, "pk") for _ in range(2)]
    pss = ph.ps([128, 16], F32, "pss")
    npc = [0]

    def nxt():
        npc[0] += 1
        return pc[npc[0] % 3]
    tiles = [(t.ctxs, i, i, 1) for i in range(LC // 128)] + [(t.out, i, LC // 128 + i, 0) for i in range(L // 128)]
    for ti, (src, i, g, r) in enumerate(tiles):
        lat = (r == 0)
        rows = slice(g * 128, (g + 1) * 128)
        xt = xts[ti % 2]
        csb = cs[ti % 2]

        def load(tj):
            s2, i2, g2, r2 = tiles[tj]
            ph.dma("sp", xts[tj % 2][:], s2[i2 * 128:(i2 + 1) * 128, :], W=[xts[tj % 2]])
            if r2 == 0:
                ph.dma("sp", cs[tj % 2][:], t.rope_cs[i2 * 128:(i2 + 1) * 128, :], W=[cs[tj % 2]])
        if ti == 0:
            load(0)
        if ti + 1 < len(tiles):
            load(ti + 1)
        rms_to_hT(ph, G, xt, hT, pT[0], li, r, None, st, 0)
        if lat:
            chunks = [(0, 512), (512, 1024), (1024, 1088)]
        else:
            chunks = [(768, 1088)]
        for (c0, c1) in chunks:
            p = nxt()
            for k in range(8):
                ph.mm(p[:, 0:c1 - c0], hT[:, k, :], w_in[:, k, c0:c1], k == 0, k == 7, [hT, w_in], [p])
            ph.copy("act", cqkv[:, c0:c1], p[:, 0:c1 - c0], [p], [cqkv])
        if lat:
            for zc in range(4):
                p = nxt()
                c0 = 1088 + zc * 512
                for k in range(8):
                    ph.mm(p[:], hT[:, k, :], w_in[:, k, c0:c0 + 512], k == 0, k == 7, [hT, w_in], [p])
                ph.act(szt[:, zc * 512:(zc + 1) * 512], p[:], AF.Silu, [p], [szt])
            ph.dma("sp", t.msz_d[i * 128:(i + 1) * 128, :], szt[:], R=[szt])
            ph.act(G.junk[:, 0:768], cqkv[:, 0:768], AF.Square, [cqkv], [G.junk, st], scale=float(768 ** -0.5), accum_out=st[:, 4:5])
            ph.rstd(st[:, 5:6], st[:, 4:5], st[:, 6:7], [st], [st])
            ph.stt("dve", cn[:, 0:768], cqkv[:, 0:768], st[:, 5:6], qnb[:], ALU.mult, ALU.mult, [cqkv, st, qnb], [cn])
        ph.act(G.junk[:, 0:256], cqkv[:, 768:1024], AF.Square, [cqkv], [G.junk, st], scale=1.0 / 16.0, accum_out=st[:, 7:8])
        ph.rstd(st[:, 8:9], st[:, 7:8], st[:, 9:10], [st], [st])
        ph.stt("dve", cn[:, 768:1024], cqkv[:, 768:1024], st[:, 8:9], kvnb[:], ALU.mult, ALU.mult, [cqkv, st, kvnb], [cn])
        ph.act(G.junk[:, 0:64], cqkv[:, 1024:1088], AF.Square, [cqkv], [G.junk, st], accum_out=st[:, 10:11])
        k0 = 0 if lat else 6
        for k in range(k0, 8):
            ph.tr(pT[1][:, k, :], cn[:, k * 128:(k + 1) * 128], G.identb[:], [cn, G.identb], [pT[1]])
        ph.copy("act", cT[:, k0:8, :], pT[1][:, k0:8, :], [pT[1]], [cT])
        if lat:
            for n in range(6):
                p = nxt()
                for k in range(6):
                    ph.mm(p[:], cT[:, k, :], w_uq[:, k, n * 512:(n + 1) * 512], k == 0, k == 5, [cT, w_uq], [p])
                ph.copy("act", q[:, n * 512:(n + 1) * 512], p[:], [p], [q])
            q3 = q[:].rearrange("p (h d) -> p h d", d=192)
            t3 = tmp[:].rearrange("p (h d) -> p h d", d=192)
            ph.tt("dve", tmp[:], q[:], q[:], ALU.mult, [q], [tmp])
            ph.red("dve", st[:, 16:32], t3, ALU.add, [tmp], [st])
            ph.ts("dve", st[:, 16:32], st[:, 16:32], 1.0 / 192, EPS, ALU.mult, ALU.add, [st], [st])
            ph.act(st[:, 16:32], st[:, 16:32], AF.Sqrt, [st], [st])
            ph.recip(st[:, 32:48], st[:, 16:32], [st], [st])
            ph.tt("dve", q3, q3, st[:, 32:48].unsqueeze(2).broadcast_to([128, 16, 192]), ALU.mult, [q, st], [q])
            ph.tt("dve", q3, q3, gq[:].unsqueeze(1).broadcast_to([128, 16, 192]), ALU.mult, [q, gq], [q])
            cb = csb[:, 0:32].unsqueeze(1).broadcast_to([128, 16, 32])
            sb_ = csb[:, 32:64].unsqueeze(1).broadcast_to([128, 16, 32])
            x1 = q3[:, :, 128:160]; x2 = q3[:, :, 160:192]
            ph.tt("dve", r4[:, 0], x1, cb, ALU.mult, [q, csb], [r4])
            ph.tt("dve", r4[:, 1], x2, sb_, ALU.mult, [q, csb], [r4])
            ph.tt("dve", r4[:, 2], x1, sb_, ALU.mult, [q, csb], [r4])
            ph.tt("dve", r4[:, 3], x2, cb, ALU.mult, [q, csb], [r4])
            ph.tt("dve", Qt[:, :, 128:160], r4[:, 0], r4[:, 1], ALU.subtract, [r4], [Qt])
            ph.tt("dve", Qt[:, :, 160:192], r4[:, 2], r4[:, 3], ALU.add, [r4], [Qt])
            ph.copy("act", Qt[:, :, 0:128], q3[:, :, 0:128], [q], [Qt])
            ph.copy("act", Qt[:, :, 192:256], q3[:, :, 128:192], [q], [Qt])
            Qf = Qt[:].rearrange("p h d -> p (h d)")
            for j in range(4):
                pt = pT[j % 2]
                for k in range(8):
                    c = j * 8 + k
                    ph.tr(pt[:, k, :], Qf[:, c * 128:(c + 1) * 128], G.identb[:], [Qt, G.identb], [pt])
                ph.copy(("act", "dve")[j % 2], QTt[:, j * 8:(j + 1) * 8, :], pt[:], [pt], [QTt])
            ph.dma("sp", t.QT_d[:, :, :, i * 128:(i + 1) * 128].rearrange("h e d t -> d (h e) t"), QTt[:], R=[QTt])
        for h4 in range(4):
            p = pk[h4 % 2]
            for hh in range(4):
                h = 4 * h4 + hh
                for kc in range(2):
                    ph.mm(p[:, hh, :], w_uk[:, kc, h * 128:(h + 1) * 128], cT[:, 6 + kc, :], kc == 0, kc == 1, [w_uk, cT], [p])
            ph.copy("act", KnT[:, 4 * h4:4 * h4 + 4, :], p[:], [p], [KnT])
            ph.act(sq[:, 4 * h4:4 * h4 + 4, :], p[:], AF.Square, [p], [sq])
        for h in range(16):
            ph.mm(pss[:, h:h + 1], sq[:, h, :], ones[:], True, True, [sq, ones], [pss])
        ph.ts("dve", rk[:], pss[:], st[:, 10:11], 1.0 / 192, ALU.add, ALU.mult, [pss, st], [rk])
        ph.ts("dve", rk[:], rk[:], EPS, None, ALU.add, None, [rk], [rk])
        ph.act(rk[:], rk[:], AF.Sqrt, [rk], [rk])
        ph.recip(st[:, 48:64], rk[:], [rk], [st])
        ph.ts("dve", rk[:], st[:, 48:64], MLA_SCALE_, None, ALU.mult, None, [st], [rk])
        ph.dma("sp", t.rk_d[rows, :], rk[:], R=[rk])
        ph.dma("sp", t.KnT_d[:, :, rows].rearrange("h d t -> d h t"), KnT[:], R=[KnT])
        for n in range(4):
            p = nxt()
            for kc in range(2):
                ph.mm(p[:], cT[:, 6 + kc, :], w_uv[:, kc, n * 512:(n + 1) * 512], kc == 0, kc == 1, [cT, w_uv], [p])
            ph.copy("act", Vt[:, n * 512:(n + 1) * 512], p[:], [p], [Vt])
        ph.dma("sp", t.V_d[rows, :], Vt[:], R=[Vt])
        ph.tt("dve", krg[:], cqkv[:, 1024:1088], gk[:, 128:192], ALU.mult, [cqkv, gk], [krg])
        if lat:
            ph.tt("dve", r4[:, 0, 0, :], krg[:, 0:32], csb[:, 0:32], ALU.mult, [krg, csb], [r4])
            ph.tt("dve", r4[:, 0, 1, :], krg[:, 32:64], csb[:, 32:64], ALU.mult, [krg, csb], [r4])
            ph.tt("dve", r4[:, 0, 2, :], krg[:, 0:32], csb[:, 32:64], ALU.mult, [krg, csb], [r4])
            ph.tt("dve", r4[:, 0, 3, :], krg[:, 32:64], csb[:, 0:32], ALU.mult, [krg, csb], [r4])
            ph.tt("dve", KRt[:, 0:32], r4[:, 0, 0, :], r4[:, 0, 1, :], ALU.subtract, [r4], [KRt])
            ph.tt("dve", KRt[:, 32:64], r4[:, 0, 2, :], r4[:, 0, 3, :], ALU.add, [r4], [KRt])
        else:
            ph.copy("dve", KRt[:], krg[:], [krg], [KRt])
        ph.tr(pT[0][0:64, 0, :], KRt[:], G.identb[:], [KRt, G.identb], [pT[0]])
        ph.copy("act", KRT[:], pT[0][0:64, 0, :], [pT[0]], [KRT])
        ph.dma("sp", t.KrT_d[:, rows], KRT[:], R=[KRT])
    ph.run()
    es_w.close()


def phase_mla_attn(prog, t, G, L):
    ph = Phase(prog, "mla")
    NTOK = LC + L
    NKT = NTOK // 128
    nct = LC // 128
    NQG = L // 512
    KrT = ph.sb([128, NTOK], BF16, "KrT")
    rk = ph.sb([128, NKT, 16], F32, "rk")
    KnT = [ph.sb([128, NTOK], BF16, "KnT") for _ in range(2)]
    Vx = [ph.sb([128, NKT, 129], BF16, "Vx") for _ in range(2)]
    Qn = [ph.sb([128, 512], BF16, "Qn") for _ in range(2)]
    Qr = [ph.sb([128, 512], BF16, "Qr") for _ in range(2)]
    pTs = [ph.sb([128, 512], BF16, "pTs") for _ in range(3)]
    ot = [ph.sb([128, 4, 128], F32, "ot") for _ in range(2)]
    rden = ph.sb([128, 8], F32, "rden")
    ps = [ph.ps([128, 512], F32, "ps") for _ in range(4)]
    po = [ph.ps([128, 512], F32, "po") for _ in range(4)]
    ph.dma("sp", KrT[0:64, :], t.KrT_d[:, :], W=[KrT])
    ph.dma("sp", KrT[64:128, :], t.KrT_d[:, :], W=[KrT])
    ph.dma("sp", rk[:], t.rk_d.rearrange("(kt p) h -> p kt h", p=128), W=[rk])
    for b in range(2):
        ph.memset("dve", Vx[b][:, :, 128:129], 1.0, [Vx[b]])
    def load_head(h):
        ph.dma("sp", KnT[h % 2][:], t.KnT_d[h, :, :], W=[KnT[h % 2]])
        ph.dma("sp", Vx[h % 2][:, :, 0:128], t.V_d[:, h * 128:(h + 1) * 128].rearrange("(kt p) d -> p kt d", p=128), W=[Vx[h % 2]])

    def load_q(j):
        h, qg = divmod(j, NQG)
        ph.dma("sp", Qn[j % 2][:], t.QT_d[h, 0, :, qg * 512:(qg + 1) * 512], W=[Qn[j % 2]])
        ph.dma("sp", Qr[j % 2][:], t.QT_d[h, 1, :, qg * 512:(qg + 1) * 512], W=[Qr[j % 2]])
    load_head(0)
    load_q(0)
    it = 0
    for h in range(16):
        kn = KnT[h % 2]; vx = Vx[h % 2]
        for qg in range(NQG):
            qn = Qn[it % 2]; qr = Qr[it % 2]; o = ot[it % 2]
            it += 1
            if it < 16 * NQG:
                if it % NQG == 0:
                    load_head(it // NQG)
                load_q(it)

            def qk(kt):
                p = ps[kt % 4]
                ph.mm(p[:], kn[:, kt * 128:(kt + 1) * 128], qn[:], True, False, [kn, qn], [p])
                b0 = 64 if kt < nct else 0
                ph.mm(p[:], KrT[b0:b0 + 64, kt * 128:(kt + 1) * 128], qr[b0:b0 + 64, :], False, True, [KrT, qr], [p])
            qk(0)
            qk(1)
            qk(2)
            for kt in range(NKT):
                if kt + 3 < NKT:
                    qk(kt + 3)
                p = ps[kt % 4]; pt = pTs[kt % 3]
                ph.act(pt[:], p[:], AF.Exp, [p, rk], [pt], scale=rk[:, kt, h:h + 1])
                for qs in range(4):
                    ph.mm(po[qs][:, 0:129], pt[:, qs * 128:(qs + 1) * 128], vx[:, kt, :], kt == 0, kt == NKT - 1, [pt, vx], [po[qs]])
            for qs in range(4):
                ph.recip(rden[:, qs:qs + 1], po[qs][:, 128:129], [po[qs]], [rden])
                ph.ts("dve", o[:, qs, :], po[qs][:, 0:128], rden[:, qs:qs + 1], None, ALU.mult, None, [po[qs], rden], [o])
            ph.dma("sp", t.o_d[qg * 512:(qg + 1) * 512, h * 128:(h + 1) * 128].rearrange("(a p) d -> p a d", p=128), o[:], R=[o])
    ph.run()


def phase_mla_out(prog, t, G, li, L):
    es_w = ExitStack()
    w_out = prog.sb(es_w, [128, 16, D], BF16, "mlo_w")
    gate = prog.sb(es_w, [128, D], F32, "mlo_gate")
    ph = Phase(prog, "mlow")
    stg = [ph.sb([128, 4, D], F32, "stg") for _ in range(2)]
    for n in range(4):
        s = stg[n % 2]
        ph.dma("sp", s[:], t.ml_w_out[n * 512:(n + 1) * 512, :].rearrange("(a p) d -> p a d", p=128), W=[s])
        ph.copy(("act", "dve")[n % 2], w_out[:, 4 * n:4 * n + 4, :], s[:], [s], [w_out])
    ph.dma("sp", gate[:], t.gate_d[li, 0, :, :], W=[gate])
    ph.run()
    ph = Phase(prog, "mlo")
    ots = [ph.sb([128, 2048], F32, "o") for _ in range(2)]
    szs = [ph.sb([128, 2048], BF16, "sz") for _ in range(2)]
    xts = [ph.sb([128, D], F32, "xt") for _ in range(2)]
    ob = ph.sb([128, 2048], BF16, "ob")
    oT = ph.sb([128, 16, 128], BF16, "oT")
    tmp = ph.sb([128, D], F32, "tmp")
    pT = [ph.ps([128, 8, 128], BF16, "pT") for _ in range(2)]
    po = [ph.ps([128, 512], F32, "po") for _ in range(2)]
    def load(i):
        rows = slice(i * 128, (i + 1) * 128)
        ph.dma("sp", ots[i % 2][:], t.o_d[rows, :], W=[ots[i % 2]])
        ph.dma("sp", szs[i % 2][:], t.msz_d[rows, :], W=[szs[i % 2]])
        ph.dma("sp", xts[i % 2][:], t.out[rows, :], W=[xts[i % 2]])
    load(0)
    for i in range(L // 128):
        rows = slice(i * 128, (i + 1) * 128)
        o = ots[i % 2]; sz = szs[i % 2]; xt = xts[i % 2]
        if i + 1 < L // 128:
            load(i + 1)
        ph.tt("dve", ob[:], o[:], sz[:], ALU.mult, [o, sz], [ob])
        for k in range(16):
            ph.tr(pT[k // 8][:, k % 8, :], ob[:, k * 128:(k + 1) * 128], G.identb[:], [ob, G.identb], [pT[k // 8]])
        for j in range(2):
            ph.copy("act", oT[:, 8 * j:8 * j + 8, :], pT[j][:], [pT[j]], [oT])
        for half in range(2):
            for k in range(16):
                ph.mm(po[half][:], oT[:, k, :], w_out[:, k, half * 512:(half + 1) * 512], k == 0, k == 15, [oT, w_out], [po[half]])
        for half in range(2):
            sl = slice(half * 512, (half + 1) * 512)
            ph.tt("dve", tmp[:, sl], po[half][:], gate[:, sl], ALU.mult, [po[half], gate], [tmp])
            ph.tt("dve", xt[:, sl], tmp[:, sl], xt[:, sl], ALU.add, [tmp, xt], [xt])
        ph.dma("sp", t.out[rows, :], xt[:], R=[xt])
    ph.run()
    es_w.close()


def build(L=L_FULL, nlayers=4, dbg=False):
    nc = bass.Bass("TRN2", target_bir_lowering=False)
    t = declare(nc, L)
    declare_rwkv(nc, t, L)
    declare_mla(nc, t, L)
    if dbg:
        t.dbg_ctx = nc.dram_tensor("dbg_ctx", [LC, D], F32, kind="ExternalOutput").ap()
        t.dbg_y = nc.dram_tensor("dbg_y", [2, LC + L, D], F32, kind="ExternalOutput").ap()
    NT = L // 128
    with ExitStack() as es:
        prog = Prog(nc, es)
        G = T()
        G.mod = prog.sb(es, [128, 4, 24, 2], F32, "mod")
        G.identb = prog.sb(es, [128, 128], BF16, "identb")
        G.identf = prog.sb(es, [128, 128], F32, "identf")
        G.junk = prog.sb(es, [128, D], BF16, "junk")
        G.xn = prog.sb(es, [128, D], BF16, "xn")
        ph = Phase(prog, "init")
        ph.dma("sp", G.identf[:], t.ident[:, :], W=[G.identf])
        ph.copy("dve", G.identb[:], G.identf[:], [G.identf], [G.identb])
        ph.run()
        phase_ada(prog, t, G)
        ctx_tiles = [(t.ctx[i * 128:(i + 1) * 128, :], t.ctxs[i * 128:(i + 1) * 128, :], 1) for i in range(LC // 128)]
        x_tiles0 = [(t.x[i * 128:(i + 1) * 128, :], t.out[i * 128:(i + 1) * 128, :], 0) for i in range(NT)]
        phase_sgu(prog, t, G, 0, 0, ctx_tiles + x_tiles0)
        import os
        STOP = int(os.environ.get("STOP", "9"))
        if nlayers >= 2:
            phase_rwkv_h(prog, t, G, 1, L)
            if STOP >= 2:
                phase_rwkv_feat(prog, t, G, L)
            if STOP >= 3:
                phase_rwkv_scan(prog, t, G, L)
            if STOP >= 4:
                phase_rwkv_out(prog, t, G, 1, L)
        if nlayers >= 3:
            phase_mla_proj(prog, t, G, 2, L)
            if STOP >= 6:
                phase_mla_attn(prog, t, G, L)
            if STOP >= 7:
                phase_mla_out(prog, t, G, 2, L)
        if nlayers >= 4:
            phase_sgu(prog, t, G, 3, 1, [(t.out[i * 128:(i + 1) * 128, :], t.out[i * 128:(i + 1) * 128, :], 0) for i in range(NT)])
        if dbg:
            ph = Phase(prog, "dbg")
            d1 = Buf(None, "d1"); d2 = Buf(None, "d2")
            ph.dma("sp", t.dbg_ctx[:, :], t.ctxs[:, :], R=[d1])
            if nlayers >= 2:
                for n in range(2):
                    ph.dma("sp", t.dbg_y[n, :, :], t.y_d[n, :, :], R=[d2])
            ph.run()
        build.n_instr = prog.n_instr
    return nc


def host_inputs(inp, b, L=L_FULL):
    f = lambda a: np.ascontiguousarray(a, dtype=np.float32)
    cond = np.stack([inp["c"][b], inp["c_ctx"]], 0)
    m = {}
    m["x"] = f(inp["x"][b][:L])
    m["ctx"] = f(inp["ctx"][b])
    m["condT"] = f(cond.reshape(2, 8, 128).transpose(2, 1, 0))
    m["ada_w"] = f(inp["ada_w"])
    m["ada_b2"] = f(np.broadcast_to(inp["ada_b"][:, None, :], (4, 2, 3 * D)))
    m["ident"] = np.eye(128, dtype=np.float32)
    sel = np.zeros((2, 2, 128), np.float32); sel[0, 0] = 1; sel[1, 1] = 1
    m["sel"] = sel
    m["sgu_w_in"] = f(inp["sgu_w_in"]); m["sgu_w_out"] = f(inp["sgu_w_out"])
    m["sgu_w_sT"] = f(inp["sgu_w_s"].transpose(0, 3, 1, 2))
    m["sgu_b_sT"] = f(inp["sgu_b_s"].transpose(0, 2, 1))
    m["sgu_gain"] = f(inp["sgu_gain"])
    m.update(rwkv_consts())
    m["rw_w_in"] = f(inp["rwkv_w_in"][0])
    m["rw_w1cat"] = f(np.concatenate([inp["rwkv_w_lora1"][0, 0], inp["rwkv_w_lora1"][0, 1]], 1))
    m["rw_a1cat"] = f(np.concatenate([inp["rwkv_a_lora1"][0, 0], inp["rwkv_a_lora1"][0, 1]], 1))
    m["rw_w2cat"] = f(inp["rwkv_w_lora2"][0].reshape(128, D))
    m["rw_a2cat"] = f(inp["rwkv_a_lora2"][0].reshape(128, D))
    m["rw_muT"] = f(inp["rwkv_mu"][0].reshape(6, 8, 128).transpose(2, 0, 1))
    m["rw_w0"] = f(inp["rwkv_w0"][0]); m["rw_a0"] = f(inp["rwkv_a0"][0])
    m["rw_k_k"] = f(inp["rwkv_k_k"][0]); m["rw_k_a"] = f(inp["rwkv_k_a"][0]); m["rw_r_k"] = f(inp["rwkv_r_k"][0].reshape(D))
    m["rw_ln_g"] = f(inp["rwkv_ln_gain"][0]); m["rw_ln_b"] = f(inp["rwkv_ln_bias"][0]); m["rw_w_out"] = f(inp["rwkv_w_out"][0])
    m["ml_w_in"] = f(inp["mla_w_in"][0]); m["ml_w_uq"] = f(inp["mla_w_uq"][0])
    ukv = inp["mla_w_ukv"][0].reshape(256, 16, 2, 128)
    m["ml_w_uk"] = f(ukv[:, :, 0, :].reshape(256, 2048)); m["ml_w_uv"] = f(ukv[:, :, 1, :].reshape(256, 2048))
    m["ml_qn"] = f(inp["mla_q_norm"][0]); m["ml_kvn"] = f(inp["mla_kv_norm"][0])
    m["ml_gq"] = f(inp["mla_qk_gain_q"][0]); m["ml_gk"] = f(inp["mla_qk_gain_k"][0]); m["ml_w_out"] = f(inp["mla_w_out"][0])
    pos = np.arange(L)
    inv = (10000.0 ** (-np.arange(16, dtype=np.float32) / 16)).astype(np.float32)
    ang = np.concatenate([(pos // 64).astype(np.float32)[:, None] * inv, (pos % 64).astype(np.float32)[:, None] * inv], -1).astype(np.float32)
    m["rope_cs"] = f(np.concatenate([np.cos(ang), np.sin(ang)], -1))
    return m


def kernel(**inputs):
    inp = {k: np.asarray(v) for k, v in inputs.items()}
    nc = build()
    in_maps = [host_inputs(inp, b) for b in range(8)]
    res = run_bass_kernel_spmd(nc, in_maps, core_ids=list(range(8)))
    return np.stack([r["out"] for r in res.results], 0).astype(np.float32)
```
